# Optimizing a Trainium2 kernel written in Bass

```python
import jax
import jax.numpy as jnp
from jax import lax
import numpy as np

D_MODEL = 1024
BATCH = 8
SEQ = 2048
DEPTH = 4

PLE_DIM = 256
CHUNK = 64
HG_HEADS = 4
HG_DK = 128
HG_DV = 128
HG_QK_W = HG_HEADS * HG_DK
HG_W = HG_HEADS * HG_DV
RW_HEADS = 8
RW_N = 64
RW_W = RW_HEADS * RW_N
RW_DECAY_LORA = 64
RW_ICLR_LORA = 64
RW_VRES_LORA = 32
RW_COLS_FIRST = 3 * RW_W + RW_DECAY_LORA + RW_ICLR_LORA
RW_COLS_REST = RW_COLS_FIRST + RW_VRES_LORA
EVEN_W = HG_W + RW_W
EVEN_HG_COLS = 2 * HG_QK_W + HG_W + EVEN_W
ML_W = 2 * D_MODEL
ML_HEADS = 4
ML_DH = ML_W // ML_HEADS
ML_BLOCK = 4
ML_CONV = 4
N_EVEN = (DEPTH + 1) // 2
N_ODD = DEPTH // 2
ALPHA = (2.0 * DEPTH) ** 0.25
BETA = (8.0 * DEPTH) ** -0.25
LN_EPS = 1e-5
RW_GN_EPS = 64e-5

kernel_name = 'hgrn2_rwkv7_mlstm_deepnorm_trunk'


def _heads(t, h):
    return t.reshape(t.shape[:-1] + (h, t.shape[-1] // h))


def _layernorm(h, g, b, dtype):
    hf = h.astype(jnp.float32)
    mu = hf.mean(-1, keepdims=True)
    var = jnp.mean(jnp.square(hf - mu), -1, keepdims=True)
    return ((hf - mu) * lax.rsqrt(var + LN_EPS) * g + b).astype(dtype)


def _head_rmsnorm(h, w):
    h = h * lax.rsqrt(jnp.mean(h * h, -1, keepdims=True) + LN_EPS)
    return h.reshape(h.shape[:-2] + (-1,)) * w


def _head_layernorm(h, w, eps):
    mu = h.mean(-1, keepdims=True)
    var = jnp.mean(jnp.square(h - mu), -1, keepdims=True)
    return ((h - mu) * lax.rsqrt(var + eps)).reshape(h.shape[:-2] + (-1,)) * w


def _token_shift(z, mu):
    z_prev = jnp.pad(z, ((0, 0), (1, 0), (0, 0)))[:, :-1]
    return z + (z_prev - z) * mu


def _headwise(x, w):
    nb, blk, _ = w.shape
    y = jnp.einsum('bsnd,nde->bsne', x.reshape(x.shape[:-1] + (nb, blk)), w)
    return y.reshape(x.shape)


def _hgrn2_chunked(q, k, v, logf):
    B_, S_, H_, DK = q.shape
    DV = v.shape[-1]
    nc = S_ // CHUNK
    def to_chunks(t):
        return t.reshape(B_, nc, CHUNK, H_, t.shape[-1]).transpose(1, 0, 3, 2, 4)
    causal = jnp.tril(jnp.ones((CHUNK, CHUNK), bool))
    def step(state, inp):
        qb, kb, vb, gb = inp
        bcum = jnp.cumsum(gb, axis=2)
        diff = bcum[:, :, :, None, :] - bcum[:, :, None, :, :]
        decay = jnp.exp(jnp.where(causal[:, :, None], diff, -jnp.inf))
        scores = jnp.einsum('bhtk,bhsk,bhtsk->bhts', qb, kb, decay)
        o = jnp.einsum('bhts,bhsv->bhtv', scores, vb) + jnp.einsum('bhtk,bhkv->bhtv', qb * jnp.exp(bcum), state)
        b_last = bcum[:, :, -1:, :]
        state = state * jnp.exp(b_last[:, :, 0, :, None]) + jnp.einsum('bhsk,bhsv->bhkv', kb * jnp.exp(b_last - bcum), vb)
        return state, o
    state0 = jnp.zeros((B_, H_, DK, DV), jnp.float32)
    _, o = lax.scan(step, state0, (to_chunks(q), to_chunks(k), to_chunks(v), to_chunks(logf)))
    return o.transpose(1, 0, 3, 2, 4).reshape(B_, S_, H_, DV)


def _rwkv7_scan(r, w, k, v, a, b):
    B_, S_, H_, N = r.shape
    def step(S, inp):
        r_t, w_t, k_t, v_t, a_t, b_t = inp
        sa = jnp.einsum('bhij,bhj->bhi', S, a_t)
        S = S * w_t[:, :, None, :] + sa[..., None] * b_t[:, :, None, :] + v_t[..., None] * k_t[:, :, None, :]
        return S, jnp.einsum('bhij,bhj->bhi', S, r_t)
    seq_major = lambda t: t.transpose(1, 0, 2, 3)
    S0 = jnp.zeros((B_, H_, N, N), jnp.float32)
    _, y = lax.scan(step, S0, (seq_major(r), seq_major(w), seq_major(k), seq_major(v), seq_major(a), seq_major(b)))
    return y.transpose(1, 0, 2, 3)


def _mlstm_chunked(q, k, v, ig, logf):
    B_, S_, H_, D = q.shape
    nc = S_ // CHUNK
    c4 = lambda t: t.reshape(B_, nc, CHUNK, H_, D).transpose(1, 0, 3, 2, 4)
    c3 = lambda t: t.reshape(B_, nc, CHUNK, H_).transpose(1, 0, 3, 2)
    causal = jnp.tril(jnp.ones((CHUNK, CHUNK), bool))
    def step(carry, inp):
        C, n, m = carry
        qb, kb, vb, ib, fb = inp
        bcum = jnp.cumsum(fb, axis=-1)
        dmat = jnp.where(causal, bcum[..., :, None] - bcum[..., None, :] + ib[..., None, :], -jnp.inf)
        inter = bcum + m[..., None]
        m_t = jnp.maximum(inter, dmat.max(-1))
        wts = jnp.exp(dmat - m_t[..., None]) * jnp.einsum('bhtd,bhsd->bhts', qb, kb)
        s_inter = jnp.exp(inter - m_t)
        num = jnp.einsum('bhts,bhsd->bhtd', wts, vb) + s_inter[..., None] * jnp.einsum('bhvk,bhtk->bhtv', C, qb)
        den = wts.sum(-1) + s_inter * jnp.einsum('bhk,bhtk->bht', n, qb)
        h = num / jnp.maximum(jnp.abs(den), jnp.exp(-m_t))[..., None]
        b_last = bcum[..., -1]
        g_s = b_last[..., None] - bcum + ib
        m_new = jnp.maximum(b_last + m, g_s.max(-1))
        ws = jnp.exp(g_s - m_new[..., None])
        dec = jnp.exp(b_last + m - m_new)
        C = dec[..., None, None] * C + jnp.einsum('bhs,bhsv,bhsk->bhvk', ws, vb, kb)
        n = dec[..., None] * n + jnp.einsum('bhs,bhsk->bhk', ws, kb)
        return (C, n, m_new), h
    carry0 = (jnp.zeros((B_, H_, D, D), jnp.float32), jnp.zeros((B_, H_, D), jnp.float32), jnp.zeros((B_, H_), jnp.float32))
    _, h = lax.scan(step, carry0, (c4(q), c4(k), c4(v), c3(ig), c3(logf)))
    return h.transpose(1, 0, 3, 2, 4).reshape(B_, S_, H_, D)


def _even_layer(x, w_in, mu, lb, gnorm, w0, w_up, a0, a_up, kk_scale, ka_scale, r_k, gn_w, gn_b, w_out, v_first, v0, v_up):
    B_, S_, _ = x.shape
    z = (x @ w_in).astype(jnp.float32)
    o = 0
    hq = z[..., o:o + HG_QK_W]; o += HG_QK_W
    hf = z[..., o:o + HG_QK_W]; o += HG_QK_W
    hi = z[..., o:o + HG_W]; o += HG_W
    gate = z[..., o:o + EVEN_W]; o += EVEN_W
    zr = _token_shift(z[..., o:], mu.astype(jnp.float32))
    f = lb + (1.0 - lb) * jax.nn.sigmoid(hf)
    hg = _hgrn2_chunked(_heads(hq, HG_HEADS), _heads(1.0 - f, HG_HEADS), _heads(hi, HG_HEADS), _heads(jnp.log(f), HG_HEADS))
    hg = _head_rmsnorm(hg, gnorm)
    r = zr[..., :RW_W]
    k = zr[..., RW_W:2 * RW_W]
    v = zr[..., 2 * RW_W:3 * RW_W]
    o = 3 * RW_W
    wd = zr[..., o:o + RW_DECAY_LORA]; o += RW_DECAY_LORA
    ad = zr[..., o:o + RW_ICLR_LORA]; o += RW_ICLR_LORA
    w_log = -jax.nn.softplus(-(w0 + jnp.tanh(wd) @ w_up)) - 0.5
    decay = jnp.exp(-jnp.exp(w_log))
    a = jax.nn.sigmoid(a0 + ad @ a_up)
    if v_first is not None:
        vd = zr[..., o:o + RW_VRES_LORA]
        v = v + (v_first - v) * jax.nn.sigmoid(v0 + vd @ v_up)
    kk = _heads(k * kk_scale, RW_HEADS)
    kk = kk / jnp.maximum(jnp.sqrt(jnp.sum(kk * kk, -1, keepdims=True)), 1e-12)
    k = k * (1.0 + (a - 1.0) * ka_scale)
    rh, kh, vh, ah = _heads(r, RW_HEADS), _heads(k, RW_HEADS), _heads(v, RW_HEADS), _heads(a, RW_HEADS)
    y = _rwkv7_scan(rh, _heads(decay, RW_HEADS), kh, vh, -kk, kk * ah)
    y = _head_layernorm(y, gn_w, RW_GN_EPS) + gn_b
    bonus = jnp.sum(rh * kh * _heads(r_k, RW_HEADS), -1, keepdims=True) * vh
    y = y + bonus.reshape(B_, S_, RW_W)
    mixed = jnp.concatenate([hg, y], axis=-1) * jax.nn.silu(gate)
    return mixed @ w_out, v


def _odd_layer(x, w_in, conv_w, conv_b, wq, wk, wv, w_ig, b_ig, w_fg, b_fg, norm_w, skip, w_out):
    z = (x @ w_in).astype(jnp.float32)
    xm, zg = z[..., :ML_W], z[..., ML_W:]
    xc = lax.conv_general_dilated(xm, conv_w.astype(jnp.float32)[:, None, :], (1,), [(ML_CONV - 1, 0)],
                                  dimension_numbers=('NWC', 'WIO', 'NWC'), feature_group_count=ML_W)
    xc = jax.nn.silu(xc + conv_b)
    q = _headwise(xc, wq)
    k = _headwise(xc, wk)
    v = _headwise(xm, wv)
    qkv = jnp.concatenate([q, k, v], axis=-1)
    ig = (qkv @ w_ig + b_ig).astype(jnp.float32)
    logf = jax.nn.log_sigmoid(qkv @ w_fg + b_fg).astype(jnp.float32)
    h = _mlstm_chunked(_heads(q, ML_HEADS), _heads(k, ML_HEADS) * (ML_DH ** -0.5), _heads(v, ML_HEADS), ig, logf)
    h = _head_layernorm(h, norm_w, LN_EPS)
    h = (h + skip * xc) * jax.nn.silu(zg)
    return h @ w_out


def setup_inputs(seed: int = 0) -> dict:
    key = jax.random.key(seed)
    ks = jax.random.split(key, 40)
    def nrm(i, shape, scale):
        return jax.random.normal(ks[i], shape, jnp.float32) * scale
    n_idx = np.arange(RW_W) / (RW_W - 1)
    w0_rows = []
    for j in range(N_EVEN):
        ratio = (2 * j) / max(DEPTH - 1, 1)
        w0_rows.append(-7.0 + 5.0 * n_idx ** (0.85 + ratio ** 0.5) + 0.5)
    w0_base = jnp.asarray(np.stack(w0_rows), jnp.float32)
    n_rest = N_EVEN - 1
    return {
        'x': nrm(0, (BATCH, SEQ, D_MODEL), 1.0),
        'p': nrm(1, (DEPTH, BATCH, SEQ, PLE_DIM), 1.0),
        'hg_lower_bounds': nrm(2, (DEPTH, HG_QK_W), 1.0),
        'ln_g': 1.0 + nrm(3, (DEPTH, D_MODEL), 0.02),
        'ln_b': nrm(4, (DEPTH, D_MODEL), 0.02),
        'ple_proj': nrm(5, (DEPTH, PLE_DIM, D_MODEL), PLE_DIM ** -0.5),
        'ple_gate': nrm(6, (DEPTH, D_MODEL, D_MODEL), D_MODEL ** -0.5),
        'ev_first_w_in': nrm(7, (D_MODEL, EVEN_HG_COLS + RW_COLS_FIRST), D_MODEL ** -0.5),
        'ev_first_mu': jax.random.uniform(ks[8], (RW_COLS_FIRST,), jnp.float32),
        'ev_rest_w_in': nrm(9, (n_rest, D_MODEL, EVEN_HG_COLS + RW_COLS_REST), D_MODEL ** -0.5),
        'ev_rest_mu': jax.random.uniform(ks[10], (n_rest, RW_COLS_REST), jnp.float32),
        'rw_v0': 1.0 + nrm(11, (n_rest, RW_W), 0.02),
        'rw_v_up': nrm(12, (n_rest, RW_VRES_LORA, RW_W), RW_VRES_LORA ** -0.5),
        'hg_gnorm': 1.0 + nrm(13, (N_EVEN, HG_W), 0.02),
        'rw_w0': w0_base + nrm(14, (N_EVEN, RW_W), 0.01),
        'rw_w_up': nrm(15, (N_EVEN, RW_DECAY_LORA, RW_W), 0.5 * RW_DECAY_LORA ** -0.5),
        'rw_a0': nrm(16, (N_EVEN, RW_W), 0.1),
        'rw_a_up': nrm(17, (N_EVEN, RW_ICLR_LORA, RW_W), RW_ICLR_LORA ** -0.5),
        'rw_kk': 0.85 + nrm(18, (N_EVEN, RW_W), 0.02),
        'rw_ka': 1.0 + nrm(19, (N_EVEN, RW_W), 0.02),
        'rw_rk': nrm(20, (N_EVEN, RW_W), 0.1),
        'rw_gn_w': 1.0 + nrm(21, (N_EVEN, RW_W), 0.02),
        'rw_gn_b': nrm(22, (N_EVEN, RW_W), 0.02),
        'ev_w_out': nrm(23, (N_EVEN, EVEN_W, D_MODEL), BETA * EVEN_W ** -0.5),
        'ml_w_in': nrm(24, (N_ODD, D_MODEL, 2 * ML_W), D_MODEL ** -0.5),
        'ml_conv_w': nrm(25, (N_ODD, ML_CONV, ML_W), ML_CONV ** -0.5),
        'ml_conv_b': nrm(26, (N_ODD, ML_W), 0.02),
        'ml_wq': nrm(27, (N_ODD, ML_W // ML_BLOCK, ML_BLOCK, ML_BLOCK), ML_BLOCK ** -0.5),
        'ml_wk': nrm(28, (N_ODD, ML_W // ML_BLOCK, ML_BLOCK, ML_BLOCK), ML_BLOCK ** -0.5),
        'ml_wv': nrm(29, (N_ODD, ML_W // ML_BLOCK, ML_BLOCK, ML_BLOCK), ML_BLOCK ** -0.5),
        'ml_w_ig': nrm(30, (N_ODD, 3 * ML_W, ML_HEADS), 0.5 * (3 * ML_W) ** -0.5),
        'ml_b_ig': nrm(31, (N_ODD, ML_HEADS), 0.1),
        'ml_w_fg': nrm(32, (N_ODD, 3 * ML_W, ML_HEADS), 0.5 * (3 * ML_W) ** -0.5),
        'ml_b_fg': jnp.linspace(3.0, 6.0, ML_HEADS, dtype=jnp.float32)[None, :] + nrm(33, (N_ODD, ML_HEADS), 0.1),
        'ml_norm': 1.0 + nrm(34, (N_ODD, ML_W), 0.02),
        'ml_skip': 1.0 + nrm(35, (N_ODD, ML_W), 0.02),
        'ml_w_out': nrm(36, (N_ODD, ML_W, D_MODEL), BETA * ML_W ** -0.5),
    }


def reference(x, p, hg_lower_bounds, ln_g, ln_b, ple_proj, ple_gate, ev_first_w_in, ev_first_mu, ev_rest_w_in, ev_rest_mu, rw_v0, rw_v_up, hg_gnorm, rw_w0, rw_w_up, rw_a0, rw_a_up, rw_kk, rw_ka, rw_rk, rw_gn_w, rw_gn_b, ev_w_out, ml_w_in, ml_conv_w, ml_conv_b, ml_wq, ml_wk, ml_wv, ml_w_ig, ml_b_ig, ml_w_fg, ml_b_fg, ml_norm, ml_skip, ml_w_out):
    lb = jnp.cumsum(jax.nn.softmax(hg_lower_bounds.astype(jnp.float32), axis=0), axis=0)
    lb = lb - lb[0]
    v_first = None
    for layer in range(DEPTH):
        j = layer // 2
        if layer % 2 == 0:
            if j == 0:
                w_in, mu, v0, v_up = ev_first_w_in, ev_first_mu, None, None
            else:
                w_in, mu, v0, v_up = ev_rest_w_in[j - 1], ev_rest_mu[j - 1], rw_v0[j - 1], rw_v_up[j - 1]
            out, v_layer = _even_layer(x, w_in, mu, lb[layer], hg_gnorm[j], rw_w0[j], rw_w_up[j], rw_a0[j], rw_a_up[j],
                                       rw_kk[j], rw_ka[j], rw_rk[j], rw_gn_w[j], rw_gn_b[j], ev_w_out[j], v_first, v0, v_up)
            if j == 0:
                v_first = v_layer
        else:
            out = _odd_layer(x, ml_w_in[j], ml_conv_w[j], ml_conv_b[j], ml_wq[j], ml_wk[j], ml_wv[j], ml_w_ig[j],
                             ml_b_ig[j], ml_w_fg[j], ml_b_fg[j], ml_norm[j], ml_skip[j], ml_w_out[j])
        y = _layernorm(ALPHA * x + out, ln_g[layer], ln_b[layer], x.dtype)
        x = y + jax.nn.sigmoid(y @ ple_gate[layer]) * (p[layer] @ ple_proj[layer])
    return x
```

```python
from contextlib import ExitStack
import numpy as np
import concourse.bass as bass
import concourse.mybir as mybir
from concourse.bass_utils import run_bass_kernel_spmd

F32 = mybir.dt.float32
BF16 = mybir.dt.bfloat16
AF = mybir.ActivationFunctionType
ALU = mybir.AluOpType
AX = mybir.AxisListType

T = 2048
D = 1024
NT = T // 128
DEPTH = 4
ALPHA = (2.0 * DEPTH) ** 0.25
LN_EPS = 1e-5

SEM_LIMIT = 30000
N_DMA_SLOTS = 16


class Emit:
    def __init__(self, nc):
        self.nc = nc
        self.eng = {"pe": nc.tensor, "dve": nc.vector, "act": nc.scalar, "pool": nc.gpsimd, "sp": nc.sync}
        self.cnt = {e: 0 for e in self.eng}
        self.epoch = {e: 0 for e in self.eng}
        self.sems = {}
        self.seen = {e: {} for e in self.eng}
        self.pending = {e: False for e in self.eng}
        self.dma_nq = {}
        self.dma_cnt = {}
        self.bufs = {}
        self.psum_names = {}
        self.n_wait = 0
        self.n_ins = 0

    def _sem(self, key):
        s = self.sems.get(key)
        if s is None:
            s = self.nc.alloc_semaphore("s_" + "_".join(str(k) for k in key))
            self.sems[key] = s
        return s

    def _next_event(self, e):
        if self.cnt[e] >= SEM_LIMIT and not self.pending[e]:
            self.epoch[e] += 1
            self.cnt[e] = 0
        return (("eng", e, self.epoch[e]), self.cnt[e] + 1)

    def _wait(self, e, ev):
        key, val = ev
        if key[0] == "eng" and key[1] == "pe" and e == "pe":
            return
        if self.seen[e].get(key, 0) >= val:
            return
        self.seen[e][key] = val
        self.eng[e].wait_ge(self._sem(key), val)
        self.n_wait += 1

    def _deps(self, e, reads, writes):
        evs = []
        for k in reads:
            st = self.bufs.get(k)
            if st is not None and st[0] is not None:
                evs.append(st[0])
        for k in writes:
            st = self.bufs.get(k)
            if st is not None:
                if st[0] is not None:
                    evs.append(st[0])
                evs.extend(st[1].values())
        for ev in evs:
            self._wait(e, ev)

    def _record(self, ev, reads, writes, who):
        for k in reads:
            st = self.bufs.setdefault(k, [None, {}])
            st[1][who] = ev
        for k in writes:
            self.bufs[k] = [ev, {}]

    def psum(self, *names, merge=None):
        for n in names:
            self.psum_names[n] = merge

    def _canon(self, reads, writes):
        r2, w2 = [], list(writes)
        for k in reads:
            base = k[0] if isinstance(k, tuple) else k
            if base in self.psum_names:
                w2.append(k)
            else:
                r2.append(k)
        out = []
        for k in w2:
            base = k[0] if isinstance(k, tuple) else k
            if base in self.psum_names and self.psum_names[base] is not None:
                k = self.psum_names[base]
            if k not in out:
                out.append(k)
        return r2, out

    def op(self, e, fn, reads=(), writes=(), inc=True):
        reads, writes = self._canon(reads, writes)
        self._deps(e, reads, writes)
        ev = self._next_event(e)
        ins = fn(self.eng[e])
        self.n_ins += 1
        if inc:
            ins.then_inc(self._sem(ev[0]), 1)
            self.cnt[e] += 1
            self.pending[e] = False
        else:
            self.pending[e] = True
        self._record(ev, reads, writes, ("eng", e))
        return ins

    def dma(self, out, in_, reads=(), writes=(), q="sp", **kw):
        n = self.dma_nq.get(q, 0)
        slot = (q, n % N_DMA_SLOTS)
        key = ("dma",) + slot
        cnt = self.dma_cnt.get(slot, 0)
        if cnt > 0:
            self._wait(q, (key, 16 * cnt))
        self._deps(q, reads, writes)
        ins = self.eng[q].dma_start(out=out, in_=in_, **kw)
        self.dma_nq[q] = n + 1
        self.dma_cnt[slot] = cnt + 1
        ev = (key, 16 * (cnt + 1))
        ins.then_inc(self._sem(key), 16)
        self.n_ins += 1
        self._record(ev, reads, writes, ("dma", slot, cnt + 1))
        return ins

    def barrier(self):
        evs = []
        for e2 in self.eng:
            assert not self.pending[e2], e2
            if self.cnt[e2] > 0:
                evs.append((("eng", e2, self.epoch[e2]), self.cnt[e2]))
        for slot, c in self.dma_cnt.items():
            evs.append((("dma",) + slot, 16 * c))
        for e in self.eng:
            for ev in evs:
                if not (ev[0][0] == "eng" and ev[0][1] == e):
                    self._wait(e, ev)

    def finish(self, e="sp"):
        for slot, c in self.dma_cnt.items():
            self._wait(e, (("dma",) + slot, 16 * c))


def run_pipeline(gens, nstages=2):
    gens = list(gens)
    active = []
    for newer in gens + [None] * (nstages - 1):
        if newer is not None:
            active.append(newer)
        done = [False] * len(active)
        finished = set()
        while not all(done):
            for i, g in enumerate(active):
                if done[i]:
                    continue
                try:
                    if next(g) == "STAGE":
                        done[i] = True
                except StopIteration:
                    done[i] = True
                    finished.add(i)
        active = [g for i, g in enumerate(active) if i not in finished]


class Ctx:
    def sb(self, name, shape, dtype):
        return self.stack.enter_context(self.nc.sbuf_tensor(name, shape, dtype))


def _bcast_rows(ap_1d, n, parts=128):
    return ap_1d.rearrange("(o n) -> o n", o=1).broadcast(0, parts) if hasattr(ap_1d, "broadcast") else None


def prologue(C, x_dram):
    E, nc = C.E, C.nc
    xt = C.sb("pro_x", [128, 2, D], F32)
    xb = C.sb("pro_xb", [128, 2, D], BF16)
    E.psum("pro_ps")
    with nc.psum_tensor("pro_ps", [128, 2, D], BF16) as ps:
        for tt in range(NT):
            s = tt % 2
            E.dma(xt[:, s, :], x_dram[tt * 128:(tt + 1) * 128, :], writes=[("pro_x", s)])
            E.op("act", lambda e: e.copy(out=xb[:, s, :], in_=xt[:, s, :]), reads=[("pro_x", s)], writes=[("pro_xb", s)])
            for k in range(8):
                E.op("pe", lambda e: e.transpose(ps[:, s, k * 128:(k + 1) * 128], xb[:, s, k * 128:(k + 1) * 128], C.ident[:]),
                     reads=[("pro_xb", s), "ident"], writes=[("pro_ps", s)], inc=(k == 7))
            E.op("dve", lambda e: e.tensor_copy(out=C.xT[:, :, tt * 128:(tt + 1) * 128],
                                                in_=ps[:, s, :].rearrange("p (k t) -> p k t", k=8)),
                 reads=[("pro_ps", s)], writes=[("xT", tt)])


def tail(C, L, KF, w_out_d, mixT_d, xres_in, xres_out, last):
    E, nc, W = C.E, C.nc, C.W
    KC = KF // 128
    wout = C.sb(f"tl{L}_wout", [128, KC, D], BF16)
    wg = C.sb(f"tl{L}_wg", [128, 8, D], BF16)
    wp = C.sb(f"tl{L}_wp", [128, 2, D], BF16)
    pT = C.sb(f"tl{L}_pT", [128, 2, T], BF16)
    lng = C.sb(f"tl{L}_lng", [128, D], F32)
    lnb = C.sb(f"tl{L}_lnb", [128, D], F32)
    for k in range(KC):
        E.dma(wout[:, k, :], w_out_d[k * 128:(k + 1) * 128, :], writes=[(f"tl{L}_wout", k)], q="pool")
    for k in range(8):
        E.dma(wg[:, k, :], W["ple_gate"][L, k * 128:(k + 1) * 128, :], writes=[(f"tl{L}_wg", k)], q="pool")
    for k in range(2):
        E.dma(wp[:, k, :], W["ple_proj"][L, k * 128:(k + 1) * 128, :], writes=[(f"tl{L}_wp", k)], q="pool")
        E.dma(pT[:, k, :], C.pT_d[L, k * 128:(k + 1) * 128, :], writes=[(f"tl{L}_pT", k)], q="pool")
    E.dma(lng[:], W["ln_g"][L, :].partition_broadcast(128), writes=[f"tl{L}_lng"])
    E.dma(lnb[:], W["ln_b"][L, :].partition_broadcast(128), writes=[f"tl{L}_lnb"])
    wkeys_out = [(f"tl{L}_wout", k) for k in range(KC)]
    wkeys_g = [(f"tl{L}_wg", k) for k in range(8)]
    wkeys_p = [(f"tl{L}_wp", k) for k in range(2)] + [(f"tl{L}_pT", k) for k in range(2)]

    NB = 4
    mT = C.sb(f"tl{L}_mT", [128, NB, KC, 128], BF16)
    xr = C.sb(f"tl{L}_xr", [128, NB, D], F32)
    h = C.sb(f"tl{L}_h", [128, NB, D], F32)
    yb = C.sb(f"tl{L}_yb", [128, NB, D], BF16)
    yT = C.sb(f"tl{L}_yT", [128, NB, 8, 128], BF16)
    sg = C.sb(f"tl{L}_sg", [128, NB, D], F32)
    xn = C.sb(f"tl{L}_xn", [128, NB, D], F32)
    xnb = C.sb(f"tl{L}_xnb", [128, NB, D], BF16)
    st = C.sb(f"tl{L}_st", [128, NB, 2, 6], F32)
    mv = C.sb(f"tl{L}_mv", [128, NB, 4], F32)
    E.psum("tl_pso", "tl_pt1", "tl_psg", "tl_psp", "tl_pt2")
    with (nc.psum_tensor(f"tl{L}_pso", [128, 2, 512], F32) as pso,
          nc.psum_tensor(f"tl{L}_psg", [128, 2, 512], F32) as psg,
          nc.psum_tensor(f"tl{L}_psp", [128, 2, 512], F32) as psp,
          nc.psum_tensor(f"tl{L}_pt1", [128, D], BF16) as pt1,
          nc.psum_tensor(f"tl{L}_pt2", [128, D], BF16) as pt2):
        def body(tt):
            s = tt % NB
            K = lambda n: (f"tl{L}_{n}", s)
            E.dma(mT[:, s], mixT_d[tt], reads=[("mixT", L, tt)], writes=[K("mT")])
            E.dma(xr[:, s, :], xres_in[tt * 128:(tt + 1) * 128, :], reads=[("xres", L, tt)], writes=[K("xr")])
            for n in range(2):
                yield
                for k in range(KC):
                    E.op("pe", lambda e: e.matmul(pso[:, n, :], lhsT=mT[:, s, k, :], rhs=wout[:, k, n * 512:(n + 1) * 512],
                                                  start=(k == 0), stop=(k == KC - 1)),
                         reads=[K("mT")] + wkeys_out, writes=["tl_pso"], inc=(n == 1 and k == KC - 1))
            yield
            E.op("dve", lambda e: e.scalar_tensor_tensor(out=h[:, s, :], in0=xr[:, s, :], scalar=ALPHA,
                                                         in1=pso[:].rearrange("p a b -> p (a b)"), op0=ALU.mult, op1=ALU.add),
                 reads=[K("xr"), "tl_pso"], writes=[K("h")])
            yield "STAGE"
            for a in range(2):
                E.op("dve", lambda e: e.bn_stats(out=st[:, s, a, :], in_=h[:, s, a * 512:(a + 1) * 512]),
                     reads=[K("h")], writes=[(f"tl{L}_st{a}", s)])
            E.op("dve", lambda e: e.bn_aggr(out=mv[:, s, 0:2], in_=st[:, s].rearrange("p a b -> p (a b)")),
                 reads=[(f"tl{L}_st0", s), (f"tl{L}_st1", s)], writes=[K("mv")])
            E.op("act", lambda e: e.activation(out=mv[:, s, 2:3], in_=mv[:, s, 1:2], func=AF.Sqrt, bias=C.eps_ln[:, 0:1], scale=1.0),
                 reads=[K("mv"), "consts"], writes=[K("mv2")])
            E.op("dve", lambda e: e.reciprocal(out=mv[:, s, 3:4], in_=mv[:, s, 2:3]), reads=[K("mv2")], writes=[K("mv3")])
            yield
            E.op("dve", lambda e: e.tensor_scalar(out=h[:, s, :], in0=h[:, s, :], scalar1=mv[:, s, 0:1], scalar2=mv[:, s, 3:4],
                                                  op0=ALU.subtract, op1=ALU.mult),
                 reads=[K("h"), K("mv"), K("mv3")], writes=[K("h")])
            E.op("pool", lambda e: e.tensor_tensor(out=h[:, s, :], in0=h[:, s, :], in1=lng[:], op=ALU.mult),
                 reads=[K("h"), f"tl{L}_lng"], writes=[K("h")])
            E.op("pool", lambda e: e.tensor_tensor(out=h[:, s, :], in0=h[:, s, :], in1=lnb[:], op=ALU.add),
                 reads=[K("h"), f"tl{L}_lnb"], writes=[K("h")])
            E.op("act", lambda e: e.copy(out=yb[:, s, :], in_=h[:, s, :]), reads=[K("h")], writes=[K("yb")])
            yield
            for k in range(8):
                E.op("pe", lambda e: e.transpose(pt1[:, k * 128:(k + 1) * 128], yb[:, s, k * 128:(k + 1) * 128], C.ident[:]),
                     reads=[K("yb"), "ident"], writes=["tl_pt1"], inc=(k == 7))
            E.op("act", lambda e: e.copy(out=yT[:, s].rearrange("p k t -> p (k t)"), in_=pt1[:]), reads=["tl_pt1"], writes=[K("yT")])
            yield "STAGE"
            for n in range(2):
                yield
                for k in range(8):
                    E.op("pe", lambda e: e.matmul(psg[:, n, :], lhsT=yT[:, s, k, :], rhs=wg[:, k, n * 512:(n + 1) * 512],
                                                  start=(k == 0), stop=(k == 7)),
                         reads=[K("yT")] + wkeys_g, writes=["tl_psg"], inc=(n == 1 and k == 7))
            for n in range(2):
                for k in range(2):
                    E.op("pe", lambda e: e.matmul(psp[:, n, :], lhsT=pT[:, k, tt * 128:(tt + 1) * 128], rhs=wp[:, k, n * 512:(n + 1) * 512],
                                                  start=(k == 0), stop=(k == 1)),
                         reads=wkeys_p, writes=["tl_psp"], inc=(n == 1 and k == 1))
            yield "STAGE"
            E.op("act", lambda e: e.activation(out=sg[:, s, :], in_=psg[:].rearrange("p a b -> p (a b)"), func=AF.Sigmoid),
                 reads=["tl_psg"], writes=[K("sg")])
            E.op("dve", lambda e: e.tensor_tensor(out=sg[:, s, :], in0=sg[:, s, :], in1=psp[:].rearrange("p a b -> p (a b)"), op=ALU.mult),
                 reads=[K("sg"), "tl_psp"], writes=[K("sg")])
            E.op("pool", lambda e: e.tensor_tensor(out=xn[:, s, :], in0=sg[:, s, :], in1=h[:, s, :], op=ALU.add),
                 reads=[K("sg"), K("h")], writes=[K("xn")])
            yield
            if last:
                E.dma(xres_out[tt * 128:(tt + 1) * 128, :], xn[:, s, :], reads=[K("xn")], writes=[("xres", L + 1, tt)])
            else:
                E.dma(xres_out[tt * 128:(tt + 1) * 128, :], xn[:, s, :], reads=[K("xn")], writes=[("xres", L + 1, tt)])
                E.op("act", lambda e: e.copy(out=xnb[:, s, :], in_=xn[:, s, :]), reads=[K("xn")], writes=[K("xnb")])
                for k in range(8):
                    E.op("pe", lambda e: e.transpose(pt2[:, k * 128:(k + 1) * 128], xnb[:, s, k * 128:(k + 1) * 128], C.ident[:]),
                         reads=[K("xnb"), "ident"], writes=["tl_pt2"], inc=(k == 7))
                E.op("dve", lambda e: e.tensor_copy(out=C.xT[:, :, tt * 128:(tt + 1) * 128],
                                                    in_=pt2[:].rearrange("p (k t) -> p k t", k=8)),
                     reads=["tl_pt2"], writes=[("xT", tt)])

        run_pipeline([body(tt) for tt in range(NT)], nstages=4)


def setup_common(nc, E):
    C = Ctx()
    C.nc, C.E = nc, E
    C.free_later = []
    C.xT = nc.alloc_sbuf_tensor("xT", [128, 8, T], BF16)[:]
    C.ident = nc.alloc_sbuf_tensor("ident", [128, 128], BF16)
    identf = nc.alloc_sbuf_tensor("identf", [128, 128], F32)
    C.identf = identf
    C.eps_ln = nc.alloc_sbuf_tensor("eps_ln", [128, 4], F32)
    E.op("pool", lambda e: e.memset(identf[:], 0.0), writes=["identf"])
    E.op("pool", lambda e: e.affine_select(out=identf[:], in_=identf[:], pattern=[[-1, 128]], compare_op=ALU.not_equal,
                                           fill=1.0, base=0, channel_multiplier=1), reads=["identf"], writes=["identf"])
    E.op("pool", lambda e: e.tensor_copy(out=C.ident[:], in_=identf[:]), reads=["identf"], writes=["ident"])
    E.op("pool", lambda e: e.memset(C.eps_ln[:], LN_EPS), writes=["consts"])
    return C


WEIGHT_SPECS = {
    "ln_g": [4, 1024], "ln_b": [4, 1024], "ple_proj": [4, 256, 1024], "ple_gate": [4, 1024, 1024],
    "ev_w_out": [2, 1024, 1024], "ml_w_out": [2, 2048, 1024],
}


def build_tail_test(KF):
    nc = bass.Bass("TRN2", target_bir_lowering=False)
    E = Emit(nc)
    x_d = nc.dram_tensor("x", [T, D], F32, kind="ExternalInput").ap()
    pT_d = nc.dram_tensor("pT", [4, 256, T], F32, kind="ExternalInput").ap()
    mix_d = nc.dram_tensor("mixT", [NT, 128, KF // 128, 128], F32, kind="ExternalInput").ap()
    W = {k: nc.dram_tensor(k, s, F32, kind="ExternalInput").ap() for k, s in WEIGHT_SPECS.items()}
    out_d = nc.dram_tensor("out", [T, D], F32, kind="ExternalOutput").ap()
    xTo_d = nc.dram_tensor("xT_out", [128, 8, T], F32, kind="ExternalOutput").ap()
    mixb_d = nc.dram_tensor("mixb", [NT, 128, KF // 128, 128], BF16).ap()
    C = setup_common(nc, E)
    C.W, C.pT_d = W, pT_d
    tmp = nc.alloc_sbuf_tensor("tmpmix", [128, KF // 128, 128], BF16)
    for tt in range(NT):
        E.dma(tmp[:], mix_d[tt], writes=["tmpmix"], q="pool")
        E.dma(mixb_d[tt], tmp[:], reads=["tmpmix"], writes=[("mixT", 1, tt)])
    L = 1
    with ExitStack() as st:
        C.stack = st
        tail(C, L, KF, W["ml_w_out" if KF == 2048 else "ev_w_out"][0], mixb_d, x_d, out_d, last=False)
        E.barrier()
    tmpo = nc.alloc_sbuf_tensor("tmpo", [128, T], F32)
    for k in range(8):
        E.op("dve", lambda e: e.tensor_copy(out=tmpo[:], in_=C.xT[:, k, :]), reads=[("xT", tt) for tt in range(NT)], writes=["tmpo"])
        E.dma(xTo_d[:, k, :], tmpo[:], reads=["tmpo"], writes=[("xTo", k)])
    E.finish()
    print("instructions", E.n_ins, "waits", E.n_wait, "sems", len(E.sems))
    return nc


ML_KSCALE = 512.0 ** -0.5
import os
MODE = int(os.environ.get('P1MODE', '2'))


def odd_pass1(C, L, XM, XC, ZG):
    E, nc, W = C.E, C.nc, C.W
    j = L // 2
    P = f"o{L}a_"
    w_in = W["ml_w_in"][j]
    cw = C.sb(P + "cw", [128, 16, 4], F32)
    cb = C.sb(P + "cb", [128, 16], F32)
    E.dma(cw[:], C.HX["ml_cw"][j], writes=[P + "cw"])
    E.dma(cb[:], C.HX["ml_cb"][j], writes=[P + "cb"])
    wch = C.sb(P + "wch", [128, 2, 8, 128], BF16)
    xmf = C.sb(P + "xmf", [128, 2, T + 4], F32)
    acc = C.sb(P + "acc", [128, 2, T], F32)
    xmb = C.sb(P + "xmb", [128, 2, T], BF16)
    xcb = C.sb(P + "xcb", [128, 2, T], BF16)
    zgs = C.sb(P + "zgs", [128, 2, T], BF16)
    for s in range(2):
        E.op("pool", lambda e: e.memset(xmf[:, s, 0:4], 0.0), writes=[(P + "xmf", s)])
    xTkeys = [("xT", tt) for tt in range(NT)]
    E.psum(P + "ps")
    with nc.psum_tensor(P + "ps", [128, 4, 512], F32) as ps:
        nb = 0
        for m in range(32):
            s = m % 2
            is_xm = m < 16
            E.dma(wch[:, s], w_in[:, m * 128:(m + 1) * 128].rearrange("(k p) c -> p k c", p=128), writes=[(P + "wch", s)], q="pool")
            for blk in range(4):
                b = nb % 4
                nb += 1
                for k in range(8):
                    E.op("pe", lambda e: e.matmul(ps[:, b, :], lhsT=wch[:, s, k, :], rhs=C.xT[:, k, blk * 512:(blk + 1) * 512],
                                                  start=(k == 0), stop=(k == 7)),
                         reads=[(P + "wch", s)] + xTkeys[blk * 4:(blk + 1) * 4], writes=[(P + "ps", b)], inc=(k == 7))
                if is_xm:
                    E.op("act", lambda e: e.copy(out=xmf[:, s, 4 + blk * 512:4 + (blk + 1) * 512], in_=ps[:, b, :]),
                         reads=[(P + "ps", b)], writes=[(P + "xmf", s)])
                    E.op("dve", lambda e: e.tensor_copy(out=xmb[:, s, blk * 512:(blk + 1) * 512], in_=ps[:, b, :]),
                         reads=[(P + "ps", b)], writes=[(P + "xmb", s)])
                else:
                    E.op("act", lambda e: e.activation(out=zgs[:, s, blk * 512:(blk + 1) * 512], in_=ps[:, b, :], func=AF.Silu),
                         reads=[(P + "ps", b)], writes=[(P + "zgs", s)])
            if MODE < 1:
                continue
            if is_xm:
                E.op("dve", lambda e: e.tensor_scalar(out=acc[:, s, :], in0=xmf[:, s, 1:1 + T], scalar1=cw[:, m, 0:1], scalar2=cb[:, m:m + 1],
                                                      op0=ALU.mult, op1=ALU.add),
                     reads=[(P + "xmf", s), P + "cw", P + "cb"], writes=[(P + "acc", s)])
                for i in range(1, 4):
                    E.op("dve", lambda e: e.scalar_tensor_tensor(out=acc[:, s, :], in0=xmf[:, s, 1 + i:1 + i + T], scalar=cw[:, m, i:i + 1],
                                                                 in1=acc[:, s, :], op0=ALU.mult, op1=ALU.add),
                         reads=[(P + "xmf", s), (P + "acc", s), P + "cw"], writes=[(P + "acc", s)])
                E.op("act", lambda e: e.activation(out=xcb[:, s, :], in_=acc[:, s, :], func=AF.Silu),
                     reads=[(P + "acc", s)], writes=[(P + "xcb", s)])
                if MODE >= 2:
                    E.dma(XM[m], xmb[:, s, :], reads=[(P + "xmb", s)], writes=[("XM", L, m)])
                    E.dma(XC[m], xcb[:, s, :], reads=[(P + "xcb", s)], writes=[("XC", L, m)])
            elif MODE >= 2:
                E.dma(ZG[m - 16], zgs[:, s, :], reads=[(P + "zgs", s)], writes=[("ZG", L, m - 16)])


def odd_pass2(C, L, XM, XC, ZG, mixT_d):
    E, nc, W = C.E, C.nc, C.W
    j = L // 2
    P = f"o{L}b_"
    K = lambda n: P + n
    bd = C.sb(P + "bd", [128, 3, 16, 128], BF16)
    for wi in range(3):
        E.dma(bd[:, wi], C.bd_d[j, wi].rearrange("m c d -> c m d"), writes=[(P + "bd", wi)], q="pool")
    bdkeys = [(P + "bd", wi) for wi in range(3)]
    G1f = C.sb(P + "G1f", [128, 16, 8], F32)
    G2f = C.sb(P + "G2f", [128, 16, 8], F32)
    G1 = C.sb(P + "G1", [128, 16, 8], BF16)
    G2 = C.sb(P + "G2", [128, 16, 8], BF16)
    wg = C.sb(P + "wgate", [128, 3, 16, 8], F32)
    E.dma(wg[:], C.HX["ml_wgate"][j], writes=[(P + "wgate", wi, a) for wi in range(3) for a in range(2)])
    wgk = [(P + "wgate", wi, a) for wi in range(3) for a in range(2)]
    bdT = C.sb(P + "bdT", [128, 2, 3, 128], F32)
    E.psum(P + "psG", merge=P + "psG")
    with nc.psum_tensor(P + "psG", [128, 2, 2, 8], F32) as psG:
        for m in range(16):
            s = m % 2
            for wi in range(3):
                E.dma(bdT[:, s, wi, :], C.bdT_d[j, wi, m], writes=[(P + "bdT", s, wi)])
            E.op("pe", lambda e: e.matmul(psG[:, s, 0, :], lhsT=bdT[:, s, 0, :], rhs=wg[:, 0, m, :], start=True, stop=False),
                 reads=[(P + "bdT", s, 0)] + wgk, writes=[(P + "psG", s)], inc=False)
            E.op("pe", lambda e: e.matmul(psG[:, s, 0, :], lhsT=bdT[:, s, 1, :], rhs=wg[:, 1, m, :], start=False, stop=True),
                 reads=[(P + "bdT", s, 1)] + wgk, writes=[(P + "psG", s)], inc=False)
            E.op("pe", lambda e: e.matmul(psG[:, s, 1, :], lhsT=bdT[:, s, 2, :], rhs=wg[:, 2, m, :], start=True, stop=True),
                 reads=[(P + "bdT", s, 2)] + wgk, writes=[(P + "psG", s)])
            E.op("dve", lambda e: e.tensor_copy(out=G1[:, m, :], in_=psG[:, s, 0, :]), reads=[(P + "psG", s)], writes=[K("G1")])
            E.op("dve", lambda e: e.tensor_copy(out=G2[:, m, :], in_=psG[:, s, 1, :]), reads=[(P + "psG", s)], writes=[K("G2")])
    nrm_bc = C.sb(P + "nrm", [128, 2048], F32)
    skp = C.sb(P + "skp", [128, 16], F32)
    E.dma(nrm_bc[:], W["ml_norm"][j].partition_broadcast(128), writes=[K("nrm")])
    E.dma(skp[:], C.HX["ml_skp"][j], writes=[K("skp")])
    big = C.sb(P + "big", [4, 1], F32)
    bfg = C.sb(P + "bfg", [4, 1], F32)
    nbfg = C.sb(P + "nbfg", [4, 1], F32)
    E.dma(big[:], W["ml_b_ig"][j].rearrange("(h o) -> h o", o=1), writes=[K("big")])
    E.dma(bfg[:], W["ml_b_fg"][j].rearrange("(h o) -> h o", o=1), writes=[K("bfg")])
    E.op("dve", lambda e: e.tensor_scalar(out=nbfg[:], in0=bfg[:], scalar1=-1.0, scalar2=None, op0=ALU.mult), reads=[K("bfg")], writes=[K("nbfg")])
    mask01 = C.sb(P + "mask", [128, 4, 128], F32)
    E.op("pool", lambda e: e.memset(mask01[:], 1.0), writes=[K("mask")])
    E.op("pool", lambda e: e.affine_select(out=mask01[:], in_=mask01[:], pattern=[[0, 4], [1, 128]], compare_op=ALU.is_ge,
                                           fill=0.0, base=0, channel_multiplier=-1), reads=[K("mask")], writes=[K("mask")])
    dmask = C.sb(P + "dmask", [4, 4], F32)
    E.op("pool", lambda e: e.memset(dmask[:], 0.0), writes=[K("dmask")])
    E.op("pool", lambda e: e.affine_select(out=dmask[:], in_=dmask[:], pattern=[[-1, 4]], compare_op=ALU.not_equal,
                                           fill=1.0, base=0, channel_multiplier=1), reads=[K("dmask")], writes=[K("dmask")])
    Ysel = C.sb(P + "Ysel", [4, 4, 128], F32)
    E.op("pool", lambda e: e.memset(Ysel[:], 0.0), writes=[K("Ysel")])
    E.op("pool", lambda e: e.affine_select(out=Ysel[:], in_=Ysel[:], pattern=[[-1, 4], [0, 128]], compare_op=ALU.not_equal,
                                           fill=1.0, base=0, channel_multiplier=1), reads=[K("Ysel")], writes=[K("Ysel")])
    ones4 = C.sb(P + "ones4", [4, 128], F32)
    E.op("pool", lambda e: e.memset(ones4[:], 1.0), writes=[K("ones4")])
    onesb = C.sb(P + "onesb", [128, 1], BF16)
    E.op("pool", lambda e: e.memset(onesb[:], 1.0), writes=[K("onesb")])
    Cf = C.sb(P + "Cf", [128, 4, 4, 512], F32)
    Cb = C.sb(P + "Cb", [128, 4, 4, 512], BF16)
    nf = C.sb(P + "nf", [128, 16], F32)
    nbf = C.sb(P + "nbf", [128, 16], BF16)
    for h in range(4):
        E.op("pool", lambda e: e.memset(Cf[:, h], 0.0), writes=[(P + "Cf", h)])
        E.op("pool", lambda e: e.memset(Cb[:, h], 0.0), writes=[(P + "Cb", h)])
    E.op("pool", lambda e: e.memset(nf[:], 0.0), writes=[K("nf")])
    E.op("pool", lambda e: e.memset(nbf[:], 0.0), writes=[K("nbf")])
    mcar = C.sb(P + "mcar", [4, 2], F32)
    E.op("pool", lambda e: e.memset(mcar[:], 0.0), writes=[(P + "mcar", 0), (P + "mcar", 1)])
    NB = 2
    xm_t = C.sb(P + "xm_t", [128, NB, 16, 128], BF16)
    xc_t = C.sb(P + "xc_t", [128, NB, 16, 128], BF16)
    zg_t = C.sb(P + "zg_t", [128, NB, 16, 128], BF16)
    qT = C.sb(P + "qT", [128, 16, 128], BF16)
    kT = C.sb(P + "kT", [128, 16, 128], BF16)
    qsTD = C.sb(P + "qsT", [128, 2, 16, 128], BF16)
    vtokD = C.sb(P + "vtok", [128, 2, 2048], BF16)
    kwsD = C.sb(P + "kws", [128, 2, 2048], BF16)
    R = C.sb(P + "R", [4, 16, 128], F32)
    Xb = C.sb(P + "Xb", [4, 2, 4, 128], F32)
    ddec = C.sb(P + "ddec", [4, 4], F32)
    colD = C.sb(P + "col", [128, 2, 16], F32)
    sm = C.sb(P + "sm", [128, 4, 8], F32)
    wexp = C.sb(P + "wexp", [128, 512], F32)
    wtmp = C.sb(P + "wtmp", [128, 512], F32)
    wtsD = C.sb(P + "wts", [128, 2, 4, 128], BF16)
    hsc = C.sb(P + "hsc", [128, 2, 512], F32)
    hn = C.sb(P + "hn", [128, 2048], BF16)
    st6 = C.sb(P + "st6", [128, 2, 6], F32)
    t1 = C.sb(P + "t1", [128, 16, 128], F32)
    oT = C.sb(P + "oT", [128, 1, 16, 128], BF16)
    ROW = {n: i for i, n in enumerate(["ig", "e", "lf", "logf", "bcum", "m", "negmu", "g", "a1", "c1", "sint", "ws", "emm", "ones"])}
    rr = lambda n: R[:, ROW[n], :]
    E.op("pool", lambda e: e.memset(rr("ones"), 1.0), writes=[K("r_ones")])
    E.psum(P + "pg", P + "pe", P + "pz", P + "pqk", P + "pt")
    E.psum(P + "psm_g", P + "psm_c", P + "psm_d", P + "psm_n", merge=P + "psm")
    with (nc.psum_tensor(P + "pg", [128, 3, 512], F32) as pg,
          nc.psum_tensor(P + "psm", [128, 512], F32) as psm,
          nc.psum_tensor(P + "pe", [128, 512], F32) as pe_,
          nc.psum_tensor(P + "pz", [128, 512], F32) as pz,
          nc.psum_tensor(P + "pqk", [128, 512], F32) as pqk,
          nc.psum_tensor(P + "pt", [128, 1024], BF16) as pt):
        gi = [0]

        def gbank():
            b = gi[0] % 3
            gi[0] += 1
            return b

        def body(tt):
            s = tt % NB
            qsT, vtok, kws, col, wts = qsTD[:, s], vtokD[:, s], kwsD[:, s], colD[:, s], wtsD[:, s]
            tsl = slice(tt * 128, (tt + 1) * 128)
            E.dma(xm_t[:, s], XM[:, :, tsl].rearrange("m p t -> p m t"), reads=[("XM", L, m) for m in range(16)], writes=[(P + "xm_t", s)])
            E.dma(xc_t[:, s], XC[:, :, tsl].rearrange("m p t -> p m t"), reads=[("XC", L, m) for m in range(16)], writes=[(P + "xc_t", s)])
            E.dma(zg_t[:, s], ZG[:, :, tsl].rearrange("m p t -> p m t"), reads=[("ZG", L, m) for m in range(16)], writes=[(P + "zg_t", s)])
            kxm, kxc, kzg = (P + "xm_t", s), (P + "xc_t", s), (P + "zg_t", s)
            for gsel in range(2):
                for m in range(16):
                    E.op("pe", lambda e: e.matmul(psm[0:4, gsel * 128:(gsel + 1) * 128], lhsT=G1[:, m, gsel * 4:(gsel + 1) * 4], rhs=xc_t[:, s, m, :],
                                                  start=(m == 0), stop=False), reads=[K("G1"), kxc], writes=[K("psm_g")], inc=False)
                for m in range(16):
                    E.op("pe", lambda e: e.matmul(psm[0:4, gsel * 128:(gsel + 1) * 128], lhsT=G2[:, m, gsel * 4:(gsel + 1) * 4], rhs=xm_t[:, s, m, :],
                                                  start=False, stop=(m == 15)), reads=[K("G2"), kxm], writes=[K("psm_g")], inc=(m == 15 and gsel == 1))
            yield
            mprev = mcar[:, (tt % 2):(tt % 2) + 1]
            mnext = mcar[:, ((tt + 1) % 2):((tt + 1) % 2) + 1]
            kmp, kmn = (P + "mcar", tt % 2), (P + "mcar", (tt + 1) % 2)
            E.op("act", lambda e: e.activation(out=rr("ig"), in_=psm[0:4, 0:128], func=AF.Identity, bias=big[:, 0:1], scale=1.0),
                 reads=[K("psm_g"), K("big")], writes=[K("r_ig")])
            E.op("act", lambda e: e.activation(out=rr("e"), in_=psm[0:4, 128:256], func=AF.Exp, bias=nbfg[:, 0:1], scale=-1.0),
                 reads=[K("psm_g"), K("nbfg")], writes=[K("r_e")])
            E.op("act", lambda e: e.activation(out=rr("lf"), in_=rr("e"), func=AF.Ln, bias=1.0, scale=1.0), reads=[K("r_e")], writes=[K("r_lf")])
            E.op("dve", lambda e: e.tensor_scalar(out=rr("logf"), in0=rr("lf"), scalar1=-1.0, scalar2=None, op0=ALU.mult),
                 reads=[K("r_lf")], writes=[K("r_logf")])
            E.op("dve", lambda e: e.tensor_tensor_scan(out=rr("bcum"), data0=rr("ones"), data1=rr("logf"), initial=0.0, op0=ALU.mult, op1=ALU.add),
                 reads=[K("r_ones"), K("r_logf")], writes=[K("r_bcum")])
            E.op("dve", lambda e: e.tensor_tensor_scan(out=rr("m"), data0=rr("logf"), data1=rr("ig"), initial=mprev, op0=ALU.add, op1=ALU.max),
                 reads=[K("r_logf"), K("r_ig"), kmp], writes=[K("r_m")])
            E.op("dve", lambda e: e.tensor_copy(out=mnext, in_=R[:, ROW["m"], 127:128]), reads=[K("r_m")], writes=[kmn])
            E.op("dve", lambda e: e.tensor_tensor(out=rr("negmu"), in0=rr("bcum"), in1=rr("m"), op=ALU.subtract),
                 reads=[K("r_bcum"), K("r_m")], writes=[K("r_negmu")])
            E.op("dve", lambda e: e.tensor_tensor(out=rr("g"), in0=rr("ig"), in1=rr("bcum"), op=ALU.subtract),
                 reads=[K("r_ig"), K("r_bcum")], writes=[K("r_g")])
            E.op("act", lambda e: e.activation(out=rr("sint"), in_=rr("negmu"), func=AF.Exp, bias=mprev, scale=1.0),
                 reads=[K("r_negmu"), kmp], writes=[K("r_sint")])
            E.op("act", lambda e: e.activation(out=rr("ws"), in_=rr("g"), func=AF.Exp, bias=R[:, ROW["negmu"], 127:128], scale=1.0),
                 reads=[K("r_g"), K("r_negmu")], writes=[K("r_ws")])
            E.op("dve", lambda e: e.tensor_scalar(out=rr("ws"), in0=rr("ws"), scalar1=ML_KSCALE, scalar2=None, op0=ALU.mult),
                 reads=[K("r_ws")], writes=[K("r_ws")])
            E.op("act", lambda e: e.activation(out=rr("emm"), in_=rr("m"), func=AF.Exp, scale=-1.0), reads=[K("r_m")], writes=[K("r_emm")])
            E.op("act", lambda e: e.activation(out=R[:, ROW["a1"], 0:1], in_=R[:, ROW["negmu"], 127:128], func=AF.Exp, bias=mprev, scale=1.0),
                 reads=[K("r_negmu"), kmp], writes=[K("r_dec")])
            yield
            E.op("dve", lambda e: e.tensor_tensor(out=Xb[:, 0], in0=rr("negmu").unsqueeze(1).broadcast_to([4, 4, 128]),
                                                  in1=dmask[:].unsqueeze(2).broadcast_to([4, 4, 128]), op=ALU.mult),
                 reads=[K("r_negmu"), K("dmask")], writes=[K("Xb0")])
            E.op("dve", lambda e: e.tensor_tensor(out=Xb[:, 1], in0=rr("sint").unsqueeze(1).broadcast_to([4, 4, 128]),
                                                  in1=dmask[:].unsqueeze(2).broadcast_to([4, 4, 128]), op=ALU.mult),
                 reads=[K("r_sint"), K("dmask")], writes=[K("Xb1")])
            E.op("dve", lambda e: e.tensor_scalar(out=ddec[:], in0=dmask[:], scalar1=R[:, ROW["a1"], 0:1], scalar2=None, op0=ALU.mult),
                 reads=[K("r_dec"), K("dmask")], writes=[K("ddec")])
            yield
            for qi, nm in enumerate(["sint", "ws", "emm"]):
                E.op("pe", lambda e: e.matmul(psm[:, 256 + qi * 4:256 + (qi + 1) * 4], lhsT=rr(nm), rhs=C.identf[0:4, 0:4], start=True, stop=True),
                     reads=[K("r_" + nm), "identf"], writes=[K("psm_c")], inc=False)
            E.op("pe", lambda e: e.matmul(psm[:, 268:272], lhsT=ones4[:], rhs=ddec[:], start=True, stop=True),
                 reads=[K("ones4"), K("ddec")], writes=[K("psm_c")])
            E.op("dve", lambda e: e.tensor_copy(out=col[:], in_=psm[:, 256:272]), reads=[K("psm_c")], writes=[(P + "col", s)])
            yield
            for which, dst in ((0, qT), (1, kT)):
                for h in range(4):
                    yield
                    b = gbank()
                    for mm in range(4):
                        m = 4 * h + mm
                        E.op("pe", lambda e: e.matmul(pg[:, b, mm * 128:(mm + 1) * 128], lhsT=bd[:, which, m, :], rhs=xc_t[:, s, m, :], start=True, stop=True),
                             reads=[kxc] + bdkeys, writes=[(P + "pg", b)], inc=(mm == 3))
                    if which == 0:
                        E.op("act", lambda e: e.copy(out=qT[:, 4 * h:4 * h + 4, :].rearrange("p a b -> p (a b)"), in_=pg[:, b, :]),
                             reads=[(P + "pg", b)], writes=[(P + "qT", h)])
                    else:
                        E.op("act", lambda e: e.activation(out=kT[:, 4 * h:4 * h + 4, :].rearrange("p a b -> p (a b)"), in_=pg[:, b, :],
                                                           func=AF.Copy, scale=ML_KSCALE),
                             reads=[(P + "pg", b)], writes=[(P + "kT", h)])
            yield
            for h in range(4):
                yield
                b = gbank()
                for mm in range(4):
                    m = 4 * h + mm
                    E.op("pe", lambda e: e.matmul(pg[:, b, mm * 128:(mm + 1) * 128], lhsT=xc_t[:, s, m, :], rhs=bd[:, 1, m, :], start=True, stop=True),
                         reads=[kxc] + bdkeys, writes=[(P + "pg", b)], inc=(mm == 3))
                E.op("act", lambda e: e.activation(out=kws[:, h * 512:(h + 1) * 512], in_=pg[:, b, :], func=AF.Copy, scale=col[:, 4 + h:5 + h]),
                     reads=[(P + "pg", b), (P + "col", s)], writes=[(P + "kws", h, s)])
                b = gbank()
                for mm in range(4):
                    m = 4 * h + mm
                    E.op("pe", lambda e: e.matmul(pg[:, b, mm * 128:(mm + 1) * 128], lhsT=xm_t[:, s, m, :], rhs=bd[:, 2, m, :], start=True, stop=True),
                         reads=[kxm] + bdkeys, writes=[(P + "pg", b)], inc=(mm == 3))
                E.op("dve", lambda e: e.tensor_copy(out=vtok[:, h * 512:(h + 1) * 512], in_=pg[:, b, :]),
                     reads=[(P + "pg", b)], writes=[(P + "vtok", h, s)])
            yield
            E.op("pe", lambda e: e.matmul(pe_[:], lhsT=ones4[:], rhs=Xb[:, 0].rearrange("q h t -> q (h t)"), start=True, stop=False),
                 reads=[K("ones4"), K("Xb0")], writes=[K("pe")], inc=False)
            E.op("pe", lambda e: e.matmul(pe_[:], lhsT=rr("g"), rhs=Ysel[:].rearrange("q h t -> q (h t)"), start=False, stop=True),
                 reads=[K("r_g"), K("Ysel")], writes=[K("pe")])
            E.op("act", lambda e: e.activation(out=wexp[:], in_=pe_[:], func=AF.Exp), reads=[K("pe")], writes=[K("wexp")])
            E.op("pe", lambda e: e.matmul(pz[:], lhsT=ones4[:], rhs=Xb[:, 1].rearrange("q h t -> q (h t)"), start=True, stop=True),
                 reads=[K("ones4"), K("Xb1")], writes=[K("pz")])
            yield
            for h in range(4):
                for mm in range(4):
                    m = 4 * h + mm
                    E.op("pe", lambda e: e.matmul(pqk[:, h * 128:(h + 1) * 128], lhsT=kT[:, m, :], rhs=qT[:, m, :], start=(mm == 0), stop=(mm == 3)),
                         reads=[(P + "kT", h), (P + "qT", h)], writes=[K("pqk")], inc=(h == 3 and mm == 3))
            E.op("dve", lambda e: e.tensor_tensor(out=wtmp[:], in0=wexp[:], in1=pqk[:], op=ALU.mult), reads=[K("wexp"), K("pqk")], writes=[K("wtmp")])
            E.op("pool", lambda e: e.tensor_tensor(out=wts[:].rearrange("p h t -> p (h t)"), in0=wtmp[:], in1=mask01[:].rearrange("p h t -> p (h t)"), op=ALU.mult),
                 reads=[K("wtmp"), K("mask")], writes=[(P + "wts", s)])
            yield
            for h in range(4):
                E.op("dve", lambda e: e.tensor_tensor(out=qsT[:, 4 * h:4 * h + 4, :], in0=qT[:, 4 * h:4 * h + 4, :],
                                                      in1=pz[:, h * 128:(h + 1) * 128].unsqueeze(1).broadcast_to([128, 4, 128]), op=ALU.mult),
                     reads=[(P + "qT", h), K("pz")], writes=[(P + "qsT", h, s)])
            yield "STAGE"
            for h in range(4):
                yield
                b = gbank()
                E.op("pe", lambda e: e.matmul(pg[:, b, :], lhsT=wts[:, h, :], rhs=vtok[:, h * 512:(h + 1) * 512], start=True, stop=False),
                     reads=[(P + "wts", s), (P + "vtok", h, s)], writes=[(P + "pg", b)], inc=False)
                for mm in range(4):
                    E.op("pe", lambda e: e.matmul(pg[:, b, :], lhsT=qsT[:, 4 * h + mm, :], rhs=Cb[:, h, mm, :], start=False, stop=(mm == 3)),
                         reads=[(P + "qsT", h, s), (P + "Cb", h)], writes=[(P + "pg", b)], inc=(mm == 3))
                E.op("pe", lambda e: e.matmul(psm[:, 272 + h:273 + h], lhsT=wts[:, h, :], rhs=onesb[:], start=True, stop=False),
                     reads=[(P + "wts", s), K("onesb")], writes=[(P + "psm_d", h)], inc=False)
                for mm in range(4):
                    E.op("pe", lambda e: e.matmul(psm[:, 272 + h:273 + h], lhsT=qsT[:, 4 * h + mm, :], rhs=nbf[:, 4 * h + mm:4 * h + mm + 1], start=False, stop=(mm == 3)),
                         reads=[(P + "qsT", h, s), K("nbf")], writes=[(P + "psm_d", h)], inc=(mm == 3))
                sh = h % 2
                E.op("act", lambda e: e.activation(out=sm[:, h, 6:7], in_=psm[:, 272 + h:273 + h], func=AF.Abs),
                     reads=[(P + "psm_d", h)], writes=[(P + "sm6", h)])
                E.op("dve", lambda e: e.tensor_tensor(out=sm[:, h, 0:1], in0=sm[:, h, 6:7], in1=col[:, 8 + h:9 + h], op=ALU.max),
                     reads=[(P + "sm6", h), (P + "col", s)], writes=[(P + "sm0", h)])
                E.op("dve", lambda e: e.reciprocal(out=sm[:, h, 1:2], in_=sm[:, h, 0:1]), reads=[(P + "sm0", h)], writes=[(P + "sm1", h)])
                E.op("act", lambda e: e.activation(out=hsc[:, sh, :], in_=pg[:, b, :], func=AF.Copy, scale=sm[:, h, 1:2]),
                     reads=[(P + "pg", b), (P + "sm1", h)], writes=[(P + "hsc", sh)])
                E.op("dve", lambda e: e.bn_stats(out=st6[:, sh, :], in_=hsc[:, sh, :]), reads=[(P + "hsc", sh)], writes=[(P + "st6", sh)])
                E.op("dve", lambda e: e.bn_aggr(out=sm[:, h, 2:4], in_=st6[:, sh, :]), reads=[(P + "st6", sh)], writes=[(P + "sm2", h)])
                E.op("act", lambda e: e.activation(out=sm[:, h, 4:5], in_=sm[:, h, 3:4], func=AF.Sqrt, bias=C.eps_ln[:, 0:1], scale=1.0),
                     reads=[(P + "sm2", h), "consts"], writes=[(P + "sm4", h)])
                E.op("dve", lambda e: e.reciprocal(out=sm[:, h, 5:6], in_=sm[:, h, 4:5]), reads=[(P + "sm4", h)], writes=[(P + "sm5", h)])
                E.op("dve", lambda e: e.tensor_scalar(out=hsc[:, sh, :], in0=hsc[:, sh, :], scalar1=sm[:, h, 2:3], scalar2=sm[:, h, 5:6],
                                                      op0=ALU.subtract, op1=ALU.mult),
                     reads=[(P + "hsc", sh), (P + "sm2", h), (P + "sm5", h)], writes=[(P + "hsc", sh)])
                E.op("pool", lambda e: e.tensor_tensor(out=hn[:, h * 512:(h + 1) * 512], in0=hsc[:, sh, :], in1=nrm_bc[:, h * 512:(h + 1) * 512], op=ALU.mult),
                     reads=[(P + "hsc", sh), K("nrm")], writes=[(P + "hn", h)])
            yield
            for h in range(4):
                yield
                for mm in range(4):
                    b = gbank()
                    E.op("pe", lambda e: e.matmul(pg[:, b, :], lhsT=kws[:, h * 512 + mm * 128:h * 512 + (mm + 1) * 128], rhs=vtok[:, h * 512:(h + 1) * 512],
                                                  start=True, stop=True),
                         reads=[(P + "kws", h, s), (P + "vtok", h, s)], writes=[(P + "pg", b)])
                    E.op("dve", lambda e: e.scalar_tensor_tensor(out=Cf[:, h, mm, :], in0=Cf[:, h, mm, :], scalar=col[:, 12 + h:13 + h],
                                                                 in1=pg[:, b, :], op0=ALU.mult, op1=ALU.add),
                         reads=[(P + "Cf", h), (P + "col", s), (P + "pg", b)], writes=[(P + "Cf", h)])
                E.op("act", lambda e: e.copy(out=Cb[:, h].rearrange("p a b -> p (a b)"), in_=Cf[:, h].rearrange("p a b -> p (a b)")),
                     reads=[(P + "Cf", h)], writes=[(P + "Cb", h)])
            for h in range(4):
                for mm in range(4):
                    E.op("pe", lambda e: e.matmul(psm[:, 276 + 4 * h + mm:277 + 4 * h + mm], lhsT=kws[:, h * 512 + mm * 128:h * 512 + (mm + 1) * 128], rhs=onesb[:],
                                                  start=True, stop=True),
                         reads=[(P + "kws", h, s), K("onesb")], writes=[K("psm_n")], inc=(h == 3 and mm == 3))
            for h in range(4):
                E.op("dve", lambda e: e.scalar_tensor_tensor(out=nf[:, 4 * h:4 * h + 4], in0=nf[:, 4 * h:4 * h + 4], scalar=col[:, 12 + h:13 + h],
                                                             in1=psm[:, 276 + 4 * h:280 + 4 * h], op0=ALU.mult, op1=ALU.add),
                     reads=[K("nf"), (P + "col", s), K("psm_n")], writes=[K("nf")])
            E.op("dve", lambda e: e.tensor_copy(out=nbf[:], in_=nf[:]), reads=[K("nf")], writes=[K("nbf")])
            yield
            for half in range(2):
                for mm in range(8):
                    m = half * 8 + mm
                    E.op("pe", lambda e: e.transpose(pt[:, mm * 128:(mm + 1) * 128], hn[:, m * 128:(m + 1) * 128], C.ident[:]),
                         reads=[(P + "hn", m // 4), "ident"], writes=[K("pt")], inc=(mm == 7))
                for mm in range(8):
                    m = half * 8 + mm
                    E.op("dve", lambda e: e.scalar_tensor_tensor(out=t1[:, m, :], in0=xc_t[:, s, m, :], scalar=skp[:, m:m + 1],
                                                                 in1=pt[:, mm * 128:(mm + 1) * 128], op0=ALU.mult, op1=ALU.add),
                         reads=[kxc, K("skp"), K("pt")], writes=[(P + "t1", half)])
            E.op("pool", lambda e: e.tensor_tensor(out=oT[:, 0].rearrange("p m t -> p (m t)"), in0=t1[:].rearrange("p m t -> p (m t)"),
                                                   in1=zg_t[:, s].rearrange("p m t -> p (m t)"), op=ALU.mult),
                 reads=[(P + "t1", 0), (P + "t1", 1), kzg], writes=[(P + "oT", 0)])
            E.dma(mixT_d[tt], oT[:, 0], reads=[(P + "oT", 0)], writes=[("mixT", L, tt)])

        run_pipeline([body(tt) for tt in range(C.ntiles)], nstages=2)


ALL_WEIGHTS = {
    "hg_lower_bounds": [4, 512], "ln_g": [4, 1024], "ln_b": [4, 1024], "ple_proj": [4, 256, 1024], "ple_gate": [4, 1024, 1024],
    "ev_first_w_in": [1024, 4224], "ev_first_mu": [1664], "ev_rest_w_in": [1, 1024, 4256], "ev_rest_mu": [1, 1696],
    "rw_v0": [1, 512], "rw_v_up": [1, 32, 512], "hg_gnorm": [2, 512], "rw_w0": [2, 512], "rw_w_up": [2, 64, 512],
    "rw_a0": [2, 512], "rw_a_up": [2, 64, 512], "rw_kk": [2, 512], "rw_ka": [2, 512], "rw_rk": [2, 512],
    "rw_gn_w": [2, 512], "rw_gn_b": [2, 512], "ev_w_out": [2, 1024, 1024], "ml_w_in": [2, 1024, 4096],
    "ml_conv_w": [2, 4, 2048], "ml_conv_b": [2, 2048], "ml_w_ig": [2, 6144, 4], "ml_b_ig": [2, 4], "ml_w_fg": [2, 6144, 4],
    "ml_b_fg": [2, 4], "ml_norm": [2, 2048], "ml_skip": [2, 2048], "ml_w_out": [2, 2048, 1024],
}
HOST_EXTRA = {"ml_bd": [2, 3, 16, 128, 128], "ml_bdT": [2, 3, 16, 128, 128], "ml_cw": [2, 128, 16, 4], "ml_cb": [2, 128, 16],
              "ml_skp": [2, 128, 16], "ml_wgate": [2, 128, 3, 16, 8], "ev_mu_pm": [2, 128, 14], "hg_lb_pm": [128, 4, 4]}


def build_program(layers=(0, 1, 2, 3), debug_mix=False, stages=None, ntiles=NT):
    nc = bass.Bass("TRN2", target_bir_lowering=False)
    E = Emit(nc)
    x_d = nc.dram_tensor("x", [T, D], F32, kind="ExternalInput").ap()
    pT_d = nc.dram_tensor("pT", [4, 256, T], F32, kind="ExternalInput").ap()
    W = {k: nc.dram_tensor(k, s, F32, kind="ExternalInput").ap() for k, s in ALL_WEIGHTS.items()}
    HX = {k: nc.dram_tensor(k, s, F32, kind="ExternalInput").ap() for k, s in HOST_EXTRA.items()}
    out_d = nc.dram_tensor("out", [T, D], F32, kind="ExternalOutput").ap()
    xres = [nc.dram_tensor(f"xres{i}", [T, D], F32).ap() for i in range(2)]
    XM = nc.dram_tensor("XM", [16, 128, T], BF16).ap()
    XC = nc.dram_tensor("XC", [16, 128, T], BF16).ap()
    ZG = nc.dram_tensor("ZG", [16, 128, T], BF16).ap()
    mixo = nc.dram_tensor("mixo", [NT, 128, 16, 128], BF16).ap()
    mixe = nc.dram_tensor("mixe", [NT, 128, 8, 128], BF16).ap()
    EVS = {k: nc.dram_tensor("evs_" + k, sh, dt).ap() for k, (sh, dt) in EV_SCR.items()}
    C = setup_common(nc, E)
    C.W, C.pT_d, C.bd_d, C.bdT_d = W, pT_d, HX["ml_bd"], HX["ml_bdT"]
    C.HX = HX
    C.ntiles = ntiles
    def phase(fn, *a):
        with ExitStack() as st:
            C.stack = st
            fn(*a)
            E.barrier()

    phase(prologue, C, x_d)
    cur = x_d
    for li, L in enumerate(layers):
        last = li == len(layers) - 1
        nxt = out_d if last else xres[li % 2]
        if L % 2 == 1:
            if stages is None or "p1" in stages:
                phase(odd_pass1, C, L, XM, XC, ZG)
            if stages is None or "p2" in stages:
                phase(odd_pass2, C, L, XM, XC, ZG, mixo)
            if stages is None or "tail" in stages:
                phase(tail, C, L, 2048, W["ml_w_out"][L // 2], mixo, cur, nxt, last)
        else:
            even_layer_phases(C, L, EVS, mixe, phase, stages)
            if stages is None or "tail" in stages:
                phase(tail, C, L, 1024, W["ev_w_out"][L // 2], mixe, cur, nxt, last)
        cur = nxt
    if debug_mix:
        KC = 16 if layers[-1] % 2 == 1 else 8
        src = mixo if KC == 16 else mixe
        dbg = nc.dram_tensor("dbg_mix", [NT, 128, KC, 128], F32, kind="ExternalOutput").ap()
        tb = nc.alloc_sbuf_tensor("dbg_b", [128, KC, 128], BF16)
        tf = nc.alloc_sbuf_tensor("dbg_f", [128, KC, 128], F32)
        for tt in range(NT):
            E.dma(tb[:], src[tt], reads=[("mixT", layers[-1], tt)], writes=["dbg_b"])
            E.op("dve", lambda e: e.tensor_copy(out=tf[:], in_=tb[:]), reads=["dbg_b"], writes=["dbg_f"])
            E.dma(dbg[tt], tf[:], reads=["dbg_f"], writes=[("dbg", tt)])
    E.finish()
    print("instructions", E.n_ins, "waits", E.n_wait, "sems", len(E.sems), flush=True)
    return nc


def host_prep(inputs):
    f = lambda a: np.ascontiguousarray(np.asarray(a, dtype=np.float32))
    shared = {k: f(inputs[k]) for k in ALL_WEIGHTS}
    bd = np.zeros((2, 3, 16, 128, 128), np.float32)
    for wi, nm in enumerate(["ml_wq", "ml_wk", "ml_wv"]):
        w = np.asarray(inputs[nm], np.float32).reshape(2, 16, 32, 4, 4)
        for b in range(32):
            bd[:, wi, :, 4 * b:4 * b + 4, 4 * b:4 * b + 4] = w[:, :, b]
    shared["ml_bd"] = bd
    shared["ml_bdT"] = np.ascontiguousarray(bd.transpose(0, 1, 2, 4, 3))
    fm = lambda v: np.ascontiguousarray(np.asarray(v, np.float32).reshape(2, 16, 128).transpose(0, 2, 1))
    shared["ml_cw"] = np.ascontiguousarray(np.asarray(inputs["ml_conv_w"], np.float32).reshape(2, 4, 16, 128).transpose(0, 3, 2, 1))
    shared["ml_cb"] = fm(inputs["ml_conv_b"])
    shared["ml_skp"] = fm(inputs["ml_skip"])
    wgt = np.zeros((2, 128, 3, 16, 8), np.float32)
    for a, nm in enumerate(["ml_w_ig", "ml_w_fg"]):
        w = np.asarray(inputs[nm], np.float32).reshape(2, 3, 16, 128, 4)
        wgt[:, :, :, :, 4 * a:4 * a + 4] = w.transpose(0, 3, 1, 2, 4)
    shared["ml_wgate"] = wgt
    mu = np.zeros((2, 14 * 128), np.float32)
    mu[0, :1664] = np.asarray(inputs["ev_first_mu"], np.float32)
    mu[1, :1696] = np.asarray(inputs["ev_rest_mu"], np.float32)[0]
    shared["ev_mu_pm"] = np.ascontiguousarray(mu.reshape(2, 14, 128).transpose(0, 2, 1))
    shared["hg_lb_pm"] = np.ascontiguousarray(np.asarray(inputs["hg_lower_bounds"], np.float32).reshape(4, 4, 128).transpose(2, 1, 0))
    x = np.asarray(inputs["x"], np.float32)
    p = np.asarray(inputs["p"], np.float32)
    per_core = []
    for b in range(x.shape[0]):
        per_core.append({"x": f(x[b]), "pT": f(np.transpose(p[:, b], (0, 2, 1)))})
    return shared, per_core


RW_KAPPA = -float(np.exp(-0.5))
RW_GN_EPS = 64e-5
EV_SCR = {"HQ": ([4, 128, T], F32), "HF": ([4, 128, T], F32), "HI": ([4, 128, T], BF16), "GT": ([8, 128, T], BF16),
          "RR": ([4, 128, T], BF16), "KK": ([4, 128, T], BF16), "VV": ([4, 128, T], BF16), "LR": ([1, 128, T], F32),
          "LV": ([1, 128, T], F32), "VF": ([T, 512], BF16)}


def even_pass1(C, L, S):
    E, nc, W = C.E, C.nc, C.W
    j = L // 2
    P = f"e{L}a_"
    w_in = W["ev_first_w_in"] if j == 0 else W["ev_rest_w_in"][j - 1]
    ncol = 4224 if j == 0 else 4256
    mu = C.sb(P + "mu", [128, 14], F32)
    E.dma(mu[:], C.HX["ev_mu_pm"][j], writes=[P + "mu"])
    wch = C.sb(P + "wch", [128, 2, 8, 128], BF16)
    zf = C.sb(P + "zf", [128, 2, T + 4], F32)
    o32 = C.sb(P + "o32", [128, 2, T], F32)
    o16 = C.sb(P + "o16", [128, 2, T], BF16)
    for s in range(2):
        E.op("pool", lambda e: e.memset(zf[:, s, 0:4], 0.0), writes=[(P + "zf", s)])
    xTkeys = [("xT", tt) for tt in range(NT)]
    kinds = ["HQ"] * 4 + ["HF"] * 4 + ["HI"] * 4 + ["GT"] * 8 + ["RR"] * 4 + ["KK"] * 4 + ["VV"] * 4 + ["LR", "LV"]
    base = {"HQ": 0, "HF": 4, "HI": 8, "GT": 12, "RR": 20, "KK": 24, "VV": 28, "LR": 32, "LV": 33}
    nchunks = (ncol + 127) // 128
    E.psum(P + "ps")
    with nc.psum_tensor(P + "ps", [128, 4, 512], F32) as ps:
        nb = 0
        for m in range(nchunks):
            s = m % 2
            cw = min(128, ncol - m * 128)
            kind = kinds[m]
            mi = m - base[kind]
            shift = m >= 20
            E.dma(wch[:, s, :, 0:cw], w_in[:, m * 128:m * 128 + cw].rearrange("(k p) c -> p k c", p=128), writes=[(P + "wch", s)], q="pool")
            for blk in range(4):
                b = nb % 4
                nb += 1
                for k in range(8):
                    E.op("pe", lambda e: e.matmul(ps[0:cw, b, :], lhsT=wch[:, s, k, 0:cw], rhs=C.xT[:, k, blk * 512:(blk + 1) * 512],
                                                  start=(k == 0), stop=(k == 7)),
                         reads=[(P + "wch", s)] + xTkeys[blk * 4:(blk + 1) * 4], writes=[(P + "ps", b)], inc=(k == 7))
                bs = slice(blk * 512, (blk + 1) * 512)
                if shift:
                    E.op("act", lambda e: e.copy(out=zf[0:cw, s, 4 + blk * 512:4 + (blk + 1) * 512], in_=ps[0:cw, b, :]),
                         reads=[(P + "ps", b)], writes=[(P + "zf", s)])
                elif kind in ("HQ", "HF"):
                    E.op("act", lambda e: e.copy(out=o32[0:cw, s, bs], in_=ps[0:cw, b, :]), reads=[(P + "ps", b)], writes=[(P + "o32", s)])
                elif kind == "HI":
                    E.op("act", lambda e: e.copy(out=o16[0:cw, s, bs], in_=ps[0:cw, b, :]), reads=[(P + "ps", b)], writes=[(P + "o16", s)])
                else:
                    E.op("act", lambda e: e.activation(out=o16[0:cw, s, bs], in_=ps[0:cw, b, :], func=AF.Silu),
                         reads=[(P + "ps", b)], writes=[(P + "o16", s)])
            if shift:
                mr = m - 20
                E.op("dve", lambda e: e.tensor_tensor(out=o32[0:cw, s, :], in0=zf[0:cw, s, 3:3 + T], in1=zf[0:cw, s, 4:4 + T], op=ALU.subtract),
                     reads=[(P + "zf", s)], writes=[(P + "o32", s)])
                dst = o16 if kind in ("RR", "KK", "VV") else o32
                E.op("dve", lambda e: e.scalar_tensor_tensor(out=dst[0:cw, s, :], in0=o32[0:cw, s, :], scalar=mu[0:cw, mr:mr + 1],
                                                             in1=zf[0:cw, s, 4:4 + T], op0=ALU.mult, op1=ALU.add),
                     reads=[(P + "o32", s), (P + "zf", s), P + "mu"], writes=[(P + ("o16" if dst is o16 else "o32"), s)])
                src, skey = (o16, "o16") if dst is o16 else (o32, "o32")
            else:
                src, skey = (o32, "o32") if kind in ("HQ", "HF") else (o16, "o16")
            E.dma(S[kind][mi, 0:cw, :], src[0:cw, s, :], reads=[(P + skey, s)], writes=[(kind, L, mi)])


def even_pass2(C, L, S, mixT_d):
    E, nc, W = C.E, C.nc, C.W
    j = L // 2
    P = f"e{L}b_"
    K = lambda n: P + n
    first = (j == 0)
    sb = C.sb
    def bc(name, src):
        t_ = sb(P + name, [128, 512], F32)
        E.dma(t_[:], src.partition_broadcast(128), writes=[K(name)])
        return t_
    gnorm = bc("gnorm", W["hg_gnorm"][j])
    kksc = bc("kksc", W["rw_kk"][j])
    kasc = bc("kasc", W["rw_ka"][j])
    rksc = bc("rksc", W["rw_rk"][j])
    gnw = bc("gnw", W["rw_gn_w"][j])
    gnb = bc("gnb", W["rw_gn_b"][j])
    waup = sb(P + "waup", [128, 512], F32)
    rows = sb(P + "rows", [1, 3, 512], F32)
    E.dma(waup[0:64, :], W["rw_w_up"][j], writes=[K("wup")])
    E.dma(waup[64:128, :], W["rw_a_up"][j], writes=[K("aup")])
    E.dma(rows[:, 0, :], W["rw_w0"][j:j + 1, :], writes=[K("rows")])
    E.dma(rows[:, 1, :], W["rw_a0"][j:j + 1, :], writes=[K("rows")])
    if not first:
        vup = sb(P + "vup", [32, 512], F32)
        E.dma(vup[:], W["rw_v_up"][j - 1], writes=[K("vup")])
        E.dma(rows[:, 2, :], W["rw_v0"][j - 1:j, :], writes=[K("rows")])
    ones1 = sb(P + "ones1", [1, 128], F32)
    E.op("pool", lambda e: e.memset(ones1[:], 1.0), writes=[K("ones1")])
    epsg = sb(P + "epsg", [128, 2], F32)
    E.op("pool", lambda e: e.memset(epsg[:, 0:1], RW_GN_EPS), writes=[K("epsg")])
    E.op("pool", lambda e: e.memset(epsg[:, 1:2], LN_EPS), writes=[K("epsg")])

    def sel_tile(name, shape, dtype, fillv, sels):
        t_ = sb(P + name, shape, dtype)
        E.op("pool", lambda e: e.memset(t_[:], fillv), writes=[K(name)])
        for (pat, cmp_, fv, base, cm) in sels:
            E.op("pool", lambda e: e.affine_select(out=t_[:], in_=t_[:], pattern=pat, compare_op=cmp_, fill=fv, base=base, channel_multiplier=cm),
                 reads=[K(name)], writes=[K(name)])
        return t_
    m_iu = sel_tile("m_iu", [128, 4, 128], BF16, 1.0, [([[0, 4], [1, 128]], ALU.is_ge, 0.0, 0, -1)])
    m_su = sel_tile("m_su", [128, 4, 128], BF16, 1.0, [([[0, 4], [1, 128]], ALU.is_ge, 0.0, -1, -1)])
    m_sl = sel_tile("m_sl", [128, 4, 128], BF16, 1.0, [([[0, 4], [-1, 128]], ALU.is_ge, 0.0, -1, 1)])
    id4 = sel_tile("id4", [128, 4, 128], BF16, 0.0, [([[0, 4], [-1, 128]], ALU.not_equal, 1.0, 0, 1)])
    Utri = sel_tile("Utri", [128, 128], F32, RW_KAPPA, [([[1, 128]], ALU.is_ge, 0.0, 0, -1)])
    Umid = sel_tile("Umid", [128, 128], F32, RW_KAPPA, [([[0, 128]], ALU.is_ge, 0.0, 63, -1)])
    Uall = sel_tile("Uall", [128, 128], F32, RW_KAPPA, [])
    Ucol = sb(P + "Ucol", [128, 2], F32)
    E.op("pool", lambda e: e.tensor_copy(out=Ucol[:, 0:1], in_=Uall[:, 0:1]), reads=[K("Uall")], writes=[K("Ucol")])
    E.op("pool", lambda e: e.tensor_copy(out=Ucol[:, 1:2], in_=Umid[:, 0:1]), reads=[K("Umid")], writes=[K("Ucol")])
    rst = sel_tile("rst", [128, 4, 128], F32, 1.0, [([[0, 4], [1, 128]], ALU.not_equal, 0.0, 0, 0), ([[0, 4], [1, 128]], ALU.not_equal, 0.0, -64, 0)])
    hlb = sb(P + "hlb", [128, 4, 4], F32)
    lbt = sb(P + "lbt", [128, 4, 4], F32)
    E.dma(hlb[:], C.HX["hg_lb_pm"], writes=[K("hlb")])
    E.op("act", lambda e: e.activation(out=hlb[:], in_=hlb[:], func=AF.Exp), reads=[K("hlb")], writes=[K("hlb")])
    E.op("dve", lambda e: e.reduce_sum(out=lbt[:, :, 2], in_=hlb[:], axis=AX.X), reads=[K("hlb")], writes=[K("lbt2")])
    E.op("dve", lambda e: e.reciprocal(out=lbt[:, :, 3], in_=lbt[:, :, 2]), reads=[K("lbt2")], writes=[K("lbt3")])
    if L == 0:
        E.op("dve", lambda e: e.memset(lbt[:, :, 0], 0.0), writes=[K("lbt0")])
    else:
        E.op("dve", lambda e: e.reduce_sum(out=lbt[:, :, 2], in_=hlb[:, :, 1:L + 1], axis=AX.X), reads=[K("hlb"), K("lbt3")], writes=[K("lbt2")])
        E.op("dve", lambda e: e.tensor_tensor(out=lbt[:, :, 0], in0=lbt[:, :, 2], in1=lbt[:, :, 3], op=ALU.mult),
             reads=[K("lbt2"), K("lbt3")], writes=[K("lbt0")])
    E.op("dve", lambda e: e.tensor_scalar(out=lbt[:, :, 1], in0=lbt[:, :, 0], scalar1=-1.0, scalar2=1.0, op0=ALU.mult, op1=ALU.add),
         reads=[K("lbt0")], writes=[K("lbt1")])
    Sh = sb(P + "Sh", [128, 4, 128], F32)
    Shb = sb(P + "Shb", [128, 4, 128], BF16)
    Sr = sb(P + "Sr", [64, 8, 64], F32)
    Srb = sb(P + "Srb", [64, 8, 64], BF16)
    E.op("pool", lambda e: e.memset(Sh[:], 0.0), writes=[K("Sh")])
    E.op("pool", lambda e: e.memset(Shb[:], 0.0), writes=[K("Shb")])
    E.op("pool", lambda e: e.memset(Sr[:], 0.0), writes=[K("Sr")])
    qead = sb(P + "qea", [128, 2, 4, 128], BF16)
    kzd = sb(P + "kz", [128, 2, 4, 2, 128], BF16)
    E.op("pool", lambda e: e.memset(kzd[:], 0.0), writes=[(P + "kz", 0), (P + "kz", 1)])
    eref = sb(P + "eref", [128, 8], F32)
    e01d = sb(P + "e01", [128, 2, 4], F32)
    hq_t = sb(P + "hq_t", [128, 4, 128], F32)
    hf_t = sb(P + "hf_t", [128, 4, 128], F32)
    f16 = {n: sb(P + n, [128, (8 if n == "gt_t" else 4), 128], BF16) for n in ["hi_t", "gt_t", "r_t", "k_t", "v_t"]}
    lr_t = sb(P + "lr_t", [128, 128], F32)
    lv_t = sb(P + "lv_t", [32, 128], F32)
    tokkv = sb(P + "tokkv", [128, 1024], BF16)
    F = {n: sb(P + "F_" + n, [128, 512], F32) for n in ["a", "b", "c", "d", "e", "f", "g", "h", "j"]}
    H = {n: sb(P + "H_" + n, [128, 512], F32) for n in ["b", "c", "d", "f"]}
    B1 = {n: sb(P + "B_" + n, [128, 512], BF16) for n in ["At", "Rt", "Bt", "Kt", "kd"]}
    FT1 = {n: sb(P + "T_" + n, [64, 8, 128], BF16) for n in ["BT", "KT"]}
    s81 = sb(P + "s81", [128, 8, 8], F32)
    th = sb(P + "th", [64, 128], F32)
    vf_t = sb(P + "vf_t", [128, 512], BF16)
    tokd = sb(P + "tokd", [128, 2, 2, 1024], BF16)
    B2 = {n: sb(P + "B_" + n, [128, 2, 512], BF16) for n in ["Vt", "Bh", "Kh", "qt"]}
    FT2 = {n: sb(P + "T_" + n, [64, 2, 8, 128], BF16) for n in ["AT", "RT"]}
    M2 = {n: sb(P + "M_" + n, [128, 2, 8, 128], BF16) for n in ["abT", "ak", "br", "kr"]}
    TQ0 = sb(P + "TQ0", [128, 2, 8, 2, 128], BF16)
    for pp in range(2):
        for g in range(2):
            E.op("pool", lambda e: e.tensor_copy(out=TQ0[:, pp, 4 * g:4 * g + 4, 1, :], in_=id4[:]), reads=[K("id4")], writes=[(P + "TQ0i", pp)])
    kdtokd = sb(P + "kdtok", [128, 2, 4, 128], BF16)
    elcd = sb(P + "elc", [64, 2, 8, 2], F32)
    ebld = sb(P + "ebl", [128, 2, 8], F32)
    bond = sb(P + "bon", [128, 2, 512], F32)
    TQ = [sb(P + f"TQ{i + 1}", [128, 8, 2, 128], BF16) for i in range(2)]
    TT = [sb(P + f"TT{i + 1}", [128, 8, 128], BF16) for i in range(2)]
    scm = sb(P + "scm", [128, 4, 128], BF16)
    S2 = {n: sb(P + "S2_" + n, [128, 512], F32) for n in ["a", "b", "c", "d"]}
    s82 = sb(P + "s82", [128, 8, 8], F32)
    xu = sb(P + "xu", [128, 2, 512], BF16)
    mixtok = sb(P + "mixtok", [128, 1024], BF16)
    mT = sb(P + "mT", [128, 8, 128], BF16)
    E.psum(P + "pa", P + "ptb", P + "pgm", P + "pseq", P + "phg")
    with (nc.psum_tensor(P + "pa", [128, 2, 512], F32) as pa,
          nc.psum_tensor(P + "ptb", [128, 1024], BF16) as ptb,
          nc.psum_tensor(P + "pgm", [128, 3, 512], F32) as pgm,
          nc.psum_tensor(P + "pseq", [128, 512], F32) as pseq,
          nc.psum_tensor(P + "phg", [128, 512], F32) as phg):
        ai = [0]

        def abank():
            b = ai[0] % 2
            ai[0] += 1
            return b

        def ew(eng, fn_name, out, reads, writes, **kw):
            E.op(eng, lambda e: getattr(e, fn_name)(out=out, **kw), reads=reads, writes=writes)

        v3 = lambda ap, a=8: ap.rearrange("p (a b) -> p a b", a=a)
        f4 = lambda ap: ap.rearrange("p (h t) -> p h t", h=4)
        hc = lambda ap, h: ap[:, h * 64:(h + 1) * 64]

        def body(tt):
            par = tt % 2
            KP = lambda n: (P + n, par)
            tsl = slice(tt * 128, (tt + 1) * 128)
            qea, kz, kdtok, elc, ebl, e01 = qead[:, par], kzd[:, par], kdtokd[:, par], elcd[:, par], ebld[:, par], e01d[:, par]
            bon = bond[:, par, :]
            Vt, Bh, Kh, qt = (B2[n][:, par, :] for n in ["Vt", "Bh", "Kh", "qt"])
            AT, RT = FT2["AT"][:, par], FT2["RT"][:, par]
            FTs = {"AT": AT, "RT": RT, "BT": FT1["BT"][:], "KT": FT1["KT"][:]}
            FTk = {"AT": KP("TAT"), "RT": KP("TRT"), "BT": K("TBT"), "KT": K("TKT")}
            ld = lambda dst, nm, key: E.dma(dst, S[nm][:, :, tsl].rearrange("m p t -> p m t"),
                                            reads=[(nm, L, m) for m in range(S[nm].shape[0])], writes=[K(key)])
            ld(hq_t[:], "HQ", "hq_t"); ld(hf_t[:], "HF", "hf_t")
            ld(f16["hi_t"][:], "HI", "hi_t"); ld(f16["gt_t"][:], "GT", "gt_t")
            ld(f16["r_t"][:], "RR", "r_t"); ld(f16["k_t"][:], "KK", "k_t"); ld(f16["v_t"][:], "VV", "v_t")
            E.dma(lr_t[:], S["LR"][0, :, tsl], reads=[("LR", L, 0)], writes=[K("lr_t")])
            if not first:
                E.dma(lv_t[:], S["LV"][0, 0:32, tsl], reads=[("LV", L, 0)], writes=[K("lv_t")])
                E.dma(vf_t[:], S["VF"][tsl, :], reads=[("VF", tt)], writes=[K("vf_t")])
            yield
            for (names, dst, dkey) in [(("hi_t", "r_t"), tokd[:, par, 0, :], KP("tok0")), (("gt_t",), tokd[:, par, 1, :], KP("tok1")),
                                       (("k_t", "v_t"), tokkv[:], K("tokkv"))]:
                ci = 0
                for nm in names:
                    for m in range(f16[nm].shape[1]):
                        E.op("pe", lambda e: e.transpose(ptb[:, ci * 128:(ci + 1) * 128], f16[nm][:, m, :], C.ident[:]),
                             reads=[K(nm), "ident"], writes=[K("ptb")], inc=(ci == 7))
                        ci += 1
                E.op("act", lambda e: e.copy(out=dst, in_=ptb[:]), reads=[K("ptb")], writes=[dkey])
                yield
            hi_k = r_k = KP("tok0"); gt_k = KP("tok1"); kv_k = K("tokkv")
            vh_tok = tokd[:, par, 0, 0:512]; r_tok = tokd[:, par, 0, 512:1024]; g_tok = tokd[:, par, 1, :]
            k_tok = tokkv[:, 0:512]; v_tok = tokkv[:, 512:1024]
            def hgrn1():
                sg, lf, bcm, tq = hf_t, H["b"], H["c"], H["d"]
                ew("act", "activation", hf_t[:], [K("hf_t")], [K("hf_t")], in_=hf_t[:], func=AF.Sigmoid)
                for h in range(4):
                    ew("dve", "tensor_scalar", sg[:][:, h, :], [K("hf_t"), K("lbt0"), K("lbt1")], [K("hf_t")], in0=sg[:][:, h, :],
                       scalar1=lbt[:, h, 1:2], scalar2=lbt[:, h, 0:1], op0=ALU.mult, op1=ALU.add)
                ew("act", "activation", lf[:], [K("hf_t")], [K("Hb")], in_=sg[:].rearrange("p h t -> p (h t)"), func=AF.Ln)
                yield
                ew("dve", "tensor_scalar", sg[:], [K("hf_t")], [K("hf_t")], in0=sg[:], scalar1=-1.0, scalar2=1.0, op0=ALU.mult, op1=ALU.add)
                E.op("dve", lambda e: e.tensor_tensor_scan(out=bcm[:], data0=rst[:].rearrange("p h t -> p (h t)"), data1=lf[:], initial=0.0,
                                                           op0=ALU.mult, op1=ALU.add), reads=[K("rst"), K("Hb")], writes=[K("Hc")])
                b8 = bcm[:].rearrange("p (c t) -> p c t", c=8)
                ew("dve", "tensor_tensor", v3(lf[:]), [K("Hc")], [K("Hb")], in0=b8, in1=b8[:, :, 31:32].broadcast_to([128, 8, 64]), op=ALU.subtract)
                ew("pool", "tensor_tensor", v3(tq[:]), [K("Hc")], [K("Hd")], in0=b8[:, :, 63:64].broadcast_to([128, 8, 64]), in1=b8, op=ALU.subtract)
                ew("act", "activation", ebl, [K("Hc")], [KP("ebl")], in_=b8[:, :, 63], func=AF.Exp)
                ew("act", "activation", eref[:], [K("Hc")], [K("eref")], in_=b8[:, :, 31], func=AF.Exp)
                yield
                eq, ek, eb = H["b"], H["f"], H["c"]
                ew("act", "activation", ek[:], [K("Hb")], [K("Hf")], in_=lf[:], func=AF.Exp, scale=-1.0)
                ew("act", "activation", eq[:], [K("Hb")], [K("Hb")], in_=lf[:], func=AF.Exp)
                ew("act", "activation", eb[:], [K("Hc")], [K("Hc")], in_=bcm[:], func=AF.Exp)
                ew("act", "activation", tq[:], [K("Hd")], [K("Hd")], in_=tq[:], func=AF.Exp)
                yield
                hqf = hq_t[:].rearrange("p h t -> p (h t)")
                ew("dve", "tensor_tensor", qt, [K("hq_t"), K("Hb")], [KP("Bqt")], in0=hqf, in1=eq[:], op=ALU.mult)
                ew("pool", "tensor_tensor", B1["kd"][:], [K("hf_t"), K("Hd")], [K("Bkd")], in0=sg[:].rearrange("p h t -> p (h t)"), in1=tq[:], op=ALU.mult)
                ebl2 = ebl.rearrange("p (h c) -> p h c", c=2)
                erf2 = eref[:].rearrange("p (h c) -> p h c", c=2)
                kd4 = f4(B1["kd"][:])
                ew("dve", "tensor_tensor", qea, [K("hq_t"), K("Hc")], [KP("qea")], in0=hq_t[:], in1=f4(eb[:]), op=ALU.mult)
                ew("pool", "tensor_tensor", qea[:, :, 64:128], [KP("qea"), KP("ebl")], [KP("qea")], in0=qea[:, :, 64:128],
                   in1=ebl2[:, :, 0:1].broadcast_to([128, 4, 64]), op=ALU.mult)
                for c in range(2):
                    cs = slice(c * 64, (c + 1) * 64)
                    ew("pool", "tensor_tensor", kz[:, :, c, cs], [K("hf_t"), K("Hf")], [KP("kz")], in0=sg[:][:, :, cs], in1=f4(ek[:])[:, :, cs], op=ALU.mult)
                ew("dve", "tensor_tensor", kz[:, :, 1, 0:64], [K("Bkd"), K("eref")], [KP("kz")], in0=kd4[:, :, 0:64],
                   in1=erf2[:, :, 1:2].broadcast_to([128, 4, 64]), op=ALU.mult)
                ew("pool", "tensor_tensor", kd4[:, :, 0:64], [K("Bkd"), KP("ebl")], [K("Bkd")], in0=kd4[:, :, 0:64],
                   in1=ebl2[:, :, 1:2].broadcast_to([128, 4, 64]), op=ALU.mult)
                ew("dve", "tensor_tensor", e01, [KP("ebl")], [KP("e01")], in0=ebl2[:, :, 0], in1=ebl2[:, :, 1], op=ALU.mult)
                yield
                for h in range(4):
                    E.op("pe", lambda e: e.transpose(ptb[:, h * 128:(h + 1) * 128], B1["kd"][:, h * 128:(h + 1) * 128], C.ident[:]),
                         reads=[K("Bkd"), "ident"], writes=[K("ptb")], inc=(h == 3))
                E.op("act", lambda e: e.copy(out=kdtok.rearrange("p h k -> p (h k)"), in_=ptb[:, 0:512]), reads=[K("ptb")], writes=[KP("kdtok")])
                yield

            def rwkv1():
                ew("act", "activation", th[:], [K("lr_t")], [K("th0")], in_=lr_t[0:64, :], func=AF.Tanh)
                sgw, av = F["a"], F["b"]
                b = abank()
                E.op("pe", lambda e: e.matmul(pa[:, b, :], lhsT=th[:], rhs=waup[0:64, :], start=True, stop=False), reads=[K("th0"), K("wup")], writes=[(P + "pa", b)], inc=False)
                E.op("pe", lambda e: e.matmul(pa[:, b, :], lhsT=ones1[:], rhs=rows[:, 0, :], start=False, stop=True), reads=[K("ones1"), K("rows")], writes=[(P + "pa", b)])
                ew("act", "activation", sgw[:], [(P + "pa", b)], [K("Fa")], in_=pa[:, b, :], func=AF.Sigmoid)
                b = abank()
                E.op("pe", lambda e: e.matmul(pa[:, b, :], lhsT=lr_t[64:128, :], rhs=waup[64:128, :], start=True, stop=False), reads=[K("lr_t"), K("aup")], writes=[(P + "pa", b)], inc=False)
                E.op("pe", lambda e: e.matmul(pa[:, b, :], lhsT=ones1[:], rhs=rows[:, 1, :], start=False, stop=True), reads=[K("ones1"), K("rows")], writes=[(P + "pa", b)])
                ew("act", "activation", av[:], [(P + "pa", b)], [K("Fb")], in_=pa[:, b, :], func=AF.Sigmoid)
                yield
                vv = F["c"]
                if first:
                    ew("pool", "tensor_copy", vv[:], [kv_k], [K("Fc")], in_=v_tok)
                    E.dma(S["VF"][tsl, :], v_tok, reads=[kv_k], writes=[("VF", tt)])
                else:
                    b = abank()
                    E.op("pe", lambda e: e.matmul(pa[:, b, :], lhsT=lv_t[:], rhs=vup[:], start=True, stop=False), reads=[K("lv_t"), K("vup")], writes=[(P + "pa", b)], inc=False)
                    E.op("pe", lambda e: e.matmul(pa[:, b, :], lhsT=ones1[:], rhs=rows[:, 2, :], start=False, stop=True), reads=[K("ones1"), K("rows")], writes=[(P + "pa", b)])
                    ew("act", "activation", F["d"][:], [(P + "pa", b)], [K("Fd")], in_=pa[:, b, :], func=AF.Sigmoid)
                    ew("dve", "tensor_tensor", vv[:], [K("vf_t"), kv_k], [K("Fc")], in0=vf_t[:], in1=v_tok, op=ALU.subtract)
                    ew("dve", "tensor_tensor", vv[:], [K("Fc"), K("Fd")], [K("Fc")], in0=vv[:], in1=F["d"][:], op=ALU.mult)
                    ew("dve", "tensor_tensor", vv[:], [K("Fc"), kv_k], [K("Fc")], in0=vv[:], in1=v_tok, op=ALU.add)
                ew("act", "copy", Vt, [K("Fc")], [KP("BVt")], in_=vv[:])
                yield
                d1, d4, cms = F["d"], F["e"], F["f"]
                b = abank()
                E.op("pe", lambda e: e.matmul(pa[:, b, :], lhsT=Umid[:], rhs=sgw[:], start=True, stop=True), reads=[K("Umid"), K("Fa")], writes=[(P + "pa", b)])
                ew("act", "copy", cms[:], [(P + "pa", b)], [K("Ff")], in_=pa[:, b, :])
                b = abank()
                E.op("pe", lambda e: e.matmul(pa[:, b, :], lhsT=Utri[:], rhs=sgw[:], start=True, stop=True), reads=[K("Utri"), K("Fa")], writes=[(P + "pa", b)])
                ew("dve", "tensor_tensor", d1[:], [(P + "pa", b), K("Ff")], [K("Fd")], in0=pa[:, b, :], in1=cms[:], op=ALU.subtract)
                yield
                b = abank()
                E.op("pe", lambda e: e.matmul(pa[:, b, :], lhsT=Uall[:], rhs=sgw[:], start=True, stop=True), reads=[K("Uall"), K("Fa")], writes=[(P + "pa", b)])
                ew("dve", "tensor_tensor", d4[:], [(P + "pa", b), K("Ff")], [K("Fe")], in0=pa[:, b, :], in1=cms[:], op=ALU.subtract)
                ew("pool", "tensor_tensor", d4[:], [K("Fe"), K("Fd")], [K("Fe")], in0=d4[:], in1=d1[:], op=ALU.subtract)
                d3 = F["f"]
                ew("dve", "scalar_tensor_tensor", d3[:], [K("Fa"), K("Fd")], [K("Ff")], in0=sgw[:], scalar=-RW_KAPPA, in1=d1[:], op0=ALU.mult, op1=ALU.add)
                b = abank()
                for h in range(8):
                    E.op("pe", lambda e: e.matmul(pa[0:64, b, 2 * h:2 * h + 2], lhsT=sgw[:, h * 64:(h + 1) * 64], rhs=Ucol[:], start=True, stop=True),
                         reads=[K("Fa"), K("Ucol")], writes=[(P + "pa", b)], inc=(h == 7))
                ew("act", "activation", elc.rearrange("p h c -> p (h c)"), [(P + "pa", b)], [KP("elc")], in_=pa[0:64, b, 0:16], func=AF.Exp)
                yield
                e1, e2, e3, e4 = F["g"], F["h"], F["f"], F["e"]
                ew("act", "activation", e1[:], [K("Fd")], [K("Fg")], in_=d1[:], func=AF.Exp)
                ew("act", "activation", e2[:], [K("Fd")], [K("Fh")], in_=d1[:], func=AF.Exp, scale=-1.0)
                ew("act", "activation", e3[:], [K("Ff")], [K("Ff")], in_=d3[:], func=AF.Exp)
                ew("act", "activation", e4[:], [K("Fe")], [K("Fe")], in_=d4[:], func=AF.Exp)
                yield
                kkn, kp, tmp = F["d"], F["a"], F["j"]
                ew("dve", "tensor_tensor", kkn[:], [kv_k, K("kksc")], [K("Fd")], in0=k_tok, in1=kksc[:], op=ALU.mult)
                ew("pool", "tensor_tensor", tmp[:], [K("Fd")], [K("Fj")], in0=kkn[:], in1=kkn[:], op=ALU.mult)
                ew("dve", "reduce_sum", s81[:, :, 3], [K("Fj")], [K("s8d")], in_=v3(tmp[:]), axis=AX.X)
                ew("dve", "tensor_scalar", s81[:, :, 4], [K("s8d")], [K("s8e")], in0=s81[:, :, 3], scalar1=1e-24, scalar2=None, op0=ALU.max)
                ew("act", "activation", s81[:, :, 4], [K("s8e")], [K("s8e")], in_=s81[:, :, 4], func=AF.Ln)
                ew("act", "activation", s81[:, :, 5], [K("s8e")], [K("s8f")], in_=s81[:, :, 4], func=AF.Exp, scale=-0.5)
                yield
                ew("dve", "tensor_tensor", v3(kkn[:]), [K("Fd"), K("s8f")], [K("Fd")], in0=v3(kkn[:]), in1=s81[:, :, 5:6].broadcast_to([128, 8, 64]), op=ALU.mult)
                ew("dve", "scalar_tensor_tensor", kp[:], [K("Fb"), K("kasc")], [K("Fa")], in0=av[:], scalar=-1.0, in1=kasc[:], op0=ALU.add, op1=ALU.mult)
                ew("dve", "scalar_tensor_tensor", kp[:], [K("Fa"), kv_k], [K("Fa")], in0=kp[:], scalar=1.0, in1=k_tok, op0=ALU.add, op1=ALU.mult)
                kka = F["b"]
                ew("dve", "tensor_tensor", kka[:], [K("Fd"), K("Fb")], [K("Fb")], in0=kkn[:], in1=av[:], op=ALU.mult)
                yield
                ew("dve", "tensor_tensor", tmp[:], [r_k, K("Fa")], [K("Fj")], in0=r_tok, in1=kp[:], op=ALU.mult)
                ew("pool", "tensor_tensor", tmp[:], [K("Fj"), K("rksc")], [K("Fj")], in0=tmp[:], in1=rksc[:], op=ALU.mult)
                ew("dve", "reduce_sum", s81[:, :, 6], [K("Fj")], [K("s8g")], in_=v3(tmp[:]), axis=AX.X)
                ew("dve", "tensor_tensor", v3(bon), [K("Fc"), K("s8g")], [KP("bon")], in0=v3(vv[:]), in1=s81[:, :, 6:7].broadcast_to([128, 8, 64]), op=ALU.mult)
                yield
                ew("dve", "scalar_tensor_tensor", B1["At"][:], [K("Fd"), K("Ff")], [K("BAt")], in0=kkn[:], scalar=-1.0, in1=e3[:], op0=ALU.mult, op1=ALU.mult)
                ew("pool", "tensor_tensor", B1["Rt"][:], [r_k, K("Fg")], [K("BRt")], in0=r_tok, in1=e1[:], op=ALU.mult)
                ew("dve", "tensor_tensor", B1["Bt"][:], [K("Fb"), K("Fh")], [K("BBt")], in0=kka[:], in1=e2[:], op=ALU.mult)
                ew("pool", "tensor_tensor", B1["Kt"][:], [K("Fa"), K("Fh")], [K("BKt")], in0=kp[:], in1=e2[:], op=ALU.mult)
                yield
                ew("dve", "tensor_tensor", Bh, [K("Fb"), K("Fe")], [KP("BBh")], in0=kka[:], in1=e4[:], op=ALU.mult)
                ew("pool", "tensor_tensor", Kh, [K("Fa"), K("Fe")], [KP("BKh")], in0=kp[:], in1=e4[:], op=ALU.mult)
                for nm, src in (("AT", "At"), ("BT", "Bt"), ("KT", "Kt"), ("RT", "Rt")):
                    for h in range(8):
                        E.op("pe", lambda e: e.transpose(ptb[0:64, h * 128:(h + 1) * 128], B1[src][:, h * 64:(h + 1) * 64], C.ident[:]),
                             reads=[K("B" + src), "ident"], writes=[K("ptb")], inc=(h == 7))
                    E.op("act", lambda e: e.copy(out=FTs[nm].rearrange("p h t -> p (h t)"), in_=ptb[0:64, :]), reads=[K("ptb")], writes=[FTk[nm]])
                    yield
                def gram(dst, lh, rh, mask):
                    for g in range(2):
                        for hh in range(4):
                            h = 4 * g + hh
                            E.op("pe", lambda e: e.matmul(pgm[:, g, hh * 128:(hh + 1) * 128], lhsT=FTs[lh][:, h, :], rhs=FTs[rh][:, h, :], start=True, stop=True),
                                 reads=[FTk[lh], FTk[rh]], writes=[(P + "pgm", g)], inc=(hh == 3))
                        dsta = TQ0[:, par, 4 * g:4 * g + 4, 0, :] if dst == "ab" else M2[dst][:, par, 4 * g:4 * g + 4, :]
                        ew("dve", "tensor_tensor", dsta, [(P + "pgm", g), K(mask[1])], [(P + "M" + dst, g, par)],
                           in0=pgm[:, g, :].rearrange("p (h t) -> p h t", h=4), in1=mask[0][:], op=ALU.mult)
                gram("ab", "BT", "AT", (m_su, "m_su"))
                yield
                gram("abT", "AT", "BT", (m_sl, "m_sl"))
                yield
                gram("ak", "KT", "AT", (m_su, "m_su"))
                yield
                gram("br", "BT", "RT", (m_iu, "m_iu"))
                yield
                gram("kr", "KT", "RT", (m_iu, "m_iu"))
                yield

            g1, g2 = rwkv1(), hgrn1()
            d1_, d2_ = False, False
            while not (d1_ and d2_):
                if not d1_:
                    try:
                        next(g1)
                    except StopIteration:
                        d1_ = True
                if not d2_:
                    try:
                        next(g2)
                    except StopIteration:
                        d2_ = True
                yield
            yield "STAGE"

            def hgrn2():
                b = abank()
                for h in range(4):
                    for c in range(2):
                        E.op("pe", lambda e: e.matmul(pa[:, b, h * 128 + c * 64:h * 128 + (c + 1) * 64], lhsT=kz[:, h, c, :], rhs=qt[:, h * 128 + c * 64:h * 128 + (c + 1) * 64],
                                                      start=True, stop=True), reads=[KP("kz"), KP("Bqt")], writes=[(P + "pa", b)], inc=(h == 3 and c == 1))
                ew("dve", "tensor_tensor", scm[:].rearrange("p h t -> p (h t)"), [(P + "pa", b), K("m_iu")], [K("scm")],
                   in0=pa[:, b, :], in1=m_iu[:].rearrange("p h t -> p (h t)"), op=ALU.mult)
                yield
                for h in range(4):
                    E.op("pe", lambda e: e.matmul(phg[:, h * 128:(h + 1) * 128], lhsT=scm[:, h, :], rhs=vh_tok[:, h * 128:(h + 1) * 128], start=True, stop=False),
                         reads=[K("scm"), hi_k], writes=[K("phg")], inc=False)
                    E.op("pe", lambda e: e.matmul(phg[:, h * 128:(h + 1) * 128], lhsT=qea[:, h, :], rhs=Shb[:, h, :], start=False, stop=True),
                         reads=[KP("qea"), K("Shb")], writes=[K("phg")], inc=(h == 3))
                yield
                b2 = abank()
                for h in range(4):
                    E.op("pe", lambda e: e.matmul(pa[:, b2, h * 128:(h + 1) * 128], lhsT=kdtok[:, h, :], rhs=vh_tok[:, h * 128:(h + 1) * 128], start=True, stop=True),
                         reads=[KP("kdtok"), hi_k], writes=[(P + "pa", b2)], inc=(h == 3))
                ew("dve", "tensor_tensor", Sh[:], [K("Sh"), KP("e01")], [K("Sh")], in0=Sh[:], in1=e01.unsqueeze(2).broadcast_to([128, 4, 128]), op=ALU.mult)
                ew("dve", "tensor_tensor", Sh[:].rearrange("p h v -> p (h v)"), [K("Sh"), (P + "pa", b2)], [K("Sh")], in0=Sh[:].rearrange("p h v -> p (h v)"),
                   in1=pa[:, b2, :], op=ALU.add)
                ew("act", "copy", Shb[:], [K("Sh")], [K("Shb")], in_=Sh[:])
                yield
                osq, hgn = S2["a"], S2["b"]
                ew("act", "activation", osq[:], [K("phg")], [K("S2a")], in_=phg[:], func=AF.Square)
                ew("dve", "reduce_sum", s82[:, 0:4, 0], [K("S2a")], [K("s2a")], in_=f4(osq[:]), axis=AX.X)
                ew("act", "activation", s82[:, 0:4, 1], [K("s2a"), K("epsg")], [K("s2b")], in_=s82[:, 0:4, 0], func=AF.Ln, bias=epsg[:, 1:2], scale=1.0 / 128.0)
                ew("act", "activation", s82[:, 0:4, 2], [K("s2b")], [K("s2c")], in_=s82[:, 0:4, 1], func=AF.Exp, scale=-0.5)
                yield
                ew("dve", "tensor_tensor", f4(hgn[:]), [K("phg"), K("s2c")], [K("S2b")], in0=f4(phg[:]), in1=s82[:, 0:4, 2:3].broadcast_to([128, 4, 128]), op=ALU.mult)
                ew("pool", "tensor_tensor", hgn[:], [K("S2b"), K("gnorm")], [K("S2b")], in0=hgn[:], in1=gnorm[:], op=ALU.mult)
                ew("pool", "tensor_tensor", mixtok[:, 0:512], [K("S2b"), gt_k], [K("mix0")], in0=hgn[:], in1=g_tok[:, 0:512], op=ALU.mult)
                yield

            def rwkv2():
                Mv = {n: M2[n][:, par] for n in M2}
                Mk = lambda n, g: (P + "M" + n, g, par)
                tq_src, tq_key = TQ0[:, par], lambda g: [Mk("ab", g), (P + "TQ0i", par)]
                tt_src, tt_key = Mv["abT"], lambda g: [Mk("abT", g)]
                for kstep in range(1, 8):
                    dq, dt = TQ[kstep % 2], TT[kstep % 2]
                    dqk = lambda g: (P + f"TQ{kstep % 2 + 1}", g)
                    dtk = lambda g: (P + f"TT{kstep % 2 + 1}", g)
                    wide = kstep <= 5
                    for g in range(2):
                        gs = slice(4 * g, 4 * g + 4)
                        for hh in range(4):
                            h = 4 * g + hh
                            if wide:
                                E.op("pe", lambda e: e.matmul(pgm[:, hh // 2, (hh % 2) * 256:(hh % 2 + 1) * 256], lhsT=tt_src[:, h, :],
                                                              rhs=tq_src[:, h].rearrange("p x t -> p (x t)"), start=True, stop=True),
                                     reads=tt_key(g) + tq_key(g), writes=[(P + "pgm", 0), (P + "pgm", 1)], inc=(hh == 3))
                            else:
                                E.op("pe", lambda e: e.matmul(pgm[:, 0, hh * 128:(hh + 1) * 128], lhsT=tt_src[:, h, :], rhs=tq_src[:, h, 1, :], start=True, stop=True),
                                     reads=tt_key(g) + tq_key(g), writes=[(P + "pgm", 0)], inc=(hh == 3))
                        if wide:
                            pv = pgm[:, 0:2, :].rearrange("p b (h x t) -> p (b h) x t", h=2, x=2)
                            ew("act", "copy", dq[:, gs, 0, :], [(P + "pgm", 0), (P + "pgm", 1)], [dqk(g)], in_=pv[:, :, 0, :])
                            ew("dve", "tensor_tensor", dq[:, gs, 1, :], [(P + "pgm", 0), (P + "pgm", 1)] + tq_key(g), [dqk(g)],
                               in0=pv[:, :, 1, :], in1=tq_src[:, gs, 1, :], op=ALU.add)
                        else:
                            ew("dve", "tensor_tensor", dq[:, gs, 1, :], [(P + "pgm", 0)] + tq_key(g), [dqk(g)],
                               in0=pgm[:, 0, :].rearrange("p (h t) -> p h t", h=4), in1=tq_src[:, gs, 1, :], op=ALU.add)
                        yield
                        if kstep <= 6:
                            for hh in range(4):
                                h = 4 * g + hh
                                E.op("pe", lambda e: e.matmul(pgm[:, 2, hh * 128:(hh + 1) * 128], lhsT=tq_src[:, h, 0, :], rhs=tt_src[:, h, :], start=True, stop=True),
                                     reads=tt_key(g) + tq_key(g), writes=[(P + "pgm", 2)], inc=(hh == 3))
                            ew("act", "copy", dt[:, gs, :], [(P + "pgm", 2)], [dtk(g)], in_=pgm[:, 2, :].rearrange("p (h t) -> p h t", h=4))
                            yield
                    tq_src, tt_src = dq, dt
                    tq_key = (lambda kq: (lambda g: [(P + f"TQ{kq % 2 + 1}", g)]))(kstep)
                    tt_key = (lambda kq: (lambda g: [(P + f"TT{kq % 2 + 1}", g)]))(kstep)
                Pfin = tq_src
                Pk = tq_key(0) + tq_key(1)
                ew("dve", "tensor_tensor", Srb[:], [K("Sr"), KP("elc")], [K("Srb")], in0=Sr[:], in1=elc[:, :, 1:2].broadcast_to([64, 8, 64]), op=ALU.mult)
                for h in range(8):
                    E.op("pe", lambda e: e.matmul(hc(pseq, h), lhsT=AT[:, h, :], rhs=Srb[:, h, :], start=True, stop=False),
                         reads=[KP("TAT"), K("Srb")], writes=[K("pseq")], inc=False)
                    E.op("pe", lambda e: e.matmul(hc(pseq, h), lhsT=Mv["ak"][:, h, :], rhs=hc(Vt, h), start=False, stop=True),
                         reads=[Mk("ak", h // 4), KP("BVt")], writes=[K("pseq")], inc=(h == 7))
                ew("act", "copy", xu[:, 0, :], [K("pseq")], [K("xu0")], in_=pseq[:])
                yield
                for h in range(8):
                    E.op("pe", lambda e: e.matmul(hc(pseq, h), lhsT=Pfin[:, h, 1, :], rhs=hc(xu[:, 0, :], h), start=True, stop=True),
                         reads=Pk + [K("xu0")], writes=[K("pseq")], inc=(h == 7))
                ew("act", "copy", xu[:, 1, :], [K("pseq")], [K("xu1")], in_=pseq[:])
                yield
                for h in range(8):
                    E.op("pe", lambda e: e.matmul(hc(pseq, h), lhsT=RT[:, h, :], rhs=Srb[:, h, :], start=True, stop=False),
                         reads=[KP("TRT"), K("Srb")], writes=[K("pseq")], inc=False)
                    E.op("pe", lambda e: e.matmul(hc(pseq, h), lhsT=Mv["br"][:, h, :], rhs=hc(xu[:, 1, :], h), start=False, stop=False),
                         reads=[Mk("br", h // 4), K("xu1")], writes=[K("pseq")], inc=False)
                    E.op("pe", lambda e: e.matmul(hc(pseq, h), lhsT=Mv["kr"][:, h, :], rhs=hc(Vt, h), start=False, stop=True),
                         reads=[Mk("kr", h // 4), KP("BVt")], writes=[K("pseq")], inc=(h == 7))
                yv = S2["c"]
                ew("act", "copy", yv[:], [K("pseq")], [K("S2c")], in_=pseq[:])
                yield
                for h in range(8):
                    E.op("pe", lambda e: e.matmul(pseq[0:64, h * 64:(h + 1) * 64], lhsT=hc(Bh, h), rhs=hc(xu[:, 1, :], h), start=True, stop=False),
                         reads=[KP("BBh"), K("xu1")], writes=[K("pseq")], inc=False)
                    E.op("pe", lambda e: e.matmul(pseq[0:64, h * 64:(h + 1) * 64], lhsT=hc(Kh, h), rhs=hc(Vt, h), start=False, stop=True),
                         reads=[KP("BKh"), KP("BVt")], writes=[K("pseq")], inc=(h == 7))
                ew("dve", "tensor_tensor", Sr[:], [K("Sr"), KP("elc")], [K("Sr")], in0=Sr[:], in1=elc[:, :, 0:1].broadcast_to([64, 8, 64]), op=ALU.mult)
                ew("dve", "tensor_tensor", Sr[:].rearrange("p h i -> p (h i)"), [K("Sr"), K("pseq")], [K("Sr")], in0=Sr[:].rearrange("p h i -> p (h i)"),
                   in1=pseq[0:64, :], op=ALU.add)
                yield
                yc = S2["d"]
                ew("dve", "reduce_sum", s82[:, :, 3], [K("S2c")], [K("s2d")], in_=v3(yv[:]), axis=AX.X)
                ew("dve", "scalar_tensor_tensor", v3(yc[:]), [K("s2d"), K("S2c")], [K("S2d")], in0=s82[:, :, 3:4].broadcast_to([128, 8, 64]), scalar=-1.0 / 64.0,
                   in1=v3(yv[:]), op0=ALU.mult, op1=ALU.add)
                ew("pool", "tensor_tensor", yv[:], [K("S2d")], [K("S2c")], in0=yc[:], in1=yc[:], op=ALU.mult)
                ew("dve", "reduce_sum", s82[:, :, 4], [K("S2c")], [K("s2e")], in_=v3(yv[:]), axis=AX.X)
                yield
                ew("act", "activation", s82[:, :, 5], [K("s2e"), K("epsg")], [K("s2f")], in_=s82[:, :, 4], func=AF.Ln, bias=epsg[:, 0:1], scale=1.0 / 64.0)
                ew("act", "activation", s82[:, :, 6], [K("s2f")], [K("s2g")], in_=s82[:, :, 5], func=AF.Exp, scale=-0.5)
                ew("dve", "tensor_tensor", v3(yc[:]), [K("S2d"), K("s2g")], [K("S2d")], in0=v3(yc[:]), in1=s82[:, :, 6:7].broadcast_to([128, 8, 64]), op=ALU.mult)
                ew("pool", "tensor_tensor", yc[:], [K("S2d"), K("gnw")], [K("S2d")], in0=yc[:], in1=gnw[:], op=ALU.mult)
                yield
                ew("pool", "tensor_tensor", yc[:], [K("S2d"), K("gnb")], [K("S2d")], in0=yc[:], in1=gnb[:], op=ALU.add)
                ew("dve", "tensor_tensor", yc[:], [K("S2d"), KP("bon")], [K("S2d")], in0=yc[:], in1=bon, op=ALU.add)
                ew("dve", "tensor_tensor", mixtok[:, 512:1024], [K("S2d"), gt_k], [K("mix1")], in0=yc[:], in1=g_tok[:, 512:1024], op=ALU.mult)
                yield

            ga, gb = hgrn2(), rwkv2()
            da = db = False
            while not (da and db):
                if not db:
                    try:
                        next(gb)
                    except StopIteration:
                        db = True
                if not da:
                    try:
                        next(ga)
                    except StopIteration:
                        da = True
                yield
            for m in range(8):
                E.op("pe", lambda e: e.transpose(ptb[:, m * 128:(m + 1) * 128], mixtok[:, m * 128:(m + 1) * 128], C.ident[:]),
                     reads=[K("mix0"), K("mix1"), "ident"], writes=[K("ptb")], inc=(m == 7))
            E.op("act", lambda e: e.copy(out=mT[:].rearrange("p m t -> p (m t)"), in_=ptb[:]), reads=[K("ptb")], writes=[K("mT")])
            E.dma(mixT_d[tt], mT[:], reads=[K("mT")], writes=[("mixT", L, tt)])

        run_pipeline([body(tt) for tt in range(C.ntiles)])


def even_layer_phases(C, L, S, mixe, phase, stages):
    if stages is None or "p1" in stages:
        phase(even_pass1, C, L, S)
    if stages is None or "p2" in stages:
        phase(even_pass2, C, L, S, mixe)


_NC_CACHE = {}


def kernel(**inputs):
    shared, per_core = host_prep(inputs)
    if "nc" not in _NC_CACHE:
        _NC_CACHE["nc"] = build_program()
    nc = _NC_CACHE["nc"]
    n = len(per_core)
    in_maps = [{**shared, **pc} for pc in per_core]
    res = run_bass_kernel_spmd(nc, in_maps, core_ids=list(range(n)))
    return np.stack([np.asarray(r["out"], dtype=np.float32) for r in res.results], axis=0)
```

```python
from contextlib import ExitStack
import numpy as np
import concourse.bass as bass
import concourse.mybir as mybir
from concourse.bass_utils import run_bass_kernel_spmd

F32 = mybir.dt.float32
BF16 = mybir.dt.bfloat16
AF = mybir.ActivationFunctionType
ALU = mybir.AluOpType
AX = mybir.AxisListType

T = 2048
D = 1024
NT = T // 128
DEPTH = 4
ALPHA = (2.0 * DEPTH) ** 0.25
LN_EPS = 1e-5

SEM_LIMIT = 30000
N_DMA_SLOTS = 16


class Emit:
    def __init__(self, nc):
        self.nc = nc
        self.eng = {"pe": nc.tensor, "dve": nc.vector, "act": nc.scalar, "pool": nc.gpsimd, "sp": nc.sync}
        self.cnt = {e: 0 for e in self.eng}
        self.epoch = {e: 0 for e in self.eng}
        self.sems = {}
        self.seen = {e: {} for e in self.eng}
        self.pending = {e: False for e in self.eng}
        self.dma_nq = {}
        self.dma_cnt = {}
        self.bufs = {}
        self.psum_names = {}
        self.n_wait = 0
        self.n_ins = 0

    def _sem(self, key):
        s = self.sems.get(key)
        if s is None:
            s = self.nc.alloc_semaphore("s_" + "_".join(str(k) for k in key))
            self.sems[key] = s
        return s

    def _next_event(self, e):
        if self.cnt[e] >= SEM_LIMIT and not self.pending[e]:
            self.epoch[e] += 1
            self.cnt[e] = 0
        return (("eng", e, self.epoch[e]), self.cnt[e] + 1)

    def _wait(self, e, ev):
        key, val = ev
        if key[0] == "eng" and key[1] == "pe" and e == "pe":
            return
        if self.seen[e].get(key, 0) >= val:
            return
        self.seen[e][key] = val
        self.eng[e].wait_ge(self._sem(key), val)
        self.n_wait += 1

    def _deps(self, e, reads, writes):
        evs = []
        for k in reads:
            st = self.bufs.get(k)
            if st is not None and st[0] is not None:
                evs.append(st[0])
        for k in writes:
            st = self.bufs.get(k)
            if st is not None:
                if st[0] is not None:
                    evs.append(st[0])
                evs.extend(st[1].values())
        for ev in evs:
            self._wait(e, ev)

    def _record(self, ev, reads, writes, who):
        for k in reads:
            st = self.bufs.setdefault(k, [None, {}])
            st[1][who] = ev
        for k in writes:
            self.bufs[k] = [ev, {}]

    def psum(self, *names, merge=None):
        for n in names:
            self.psum_names[n] = merge

    def _canon(self, reads, writes):
        r2, w2 = [], list(writes)
        for k in reads:
            base = k[0] if isinstance(k, tuple) else k
            if base in self.psum_names:
                w2.append(k)
            else:
                r2.append(k)
        out = []
        for k in w2:
            base = k[0] if isinstance(k, tuple) else k
            if base in self.psum_names and self.psum_names[base] is not None:
                k = self.psum_names[base]
            if k not in out:
                out.append(k)
        return r2, out

    def op(self, e, fn, reads=(), writes=(), inc=True):
        reads, writes = self._canon(reads, writes)
        self._deps(e, reads, writes)
        ev = self._next_event(e)
        ins = fn(self.eng[e])
        self.n_ins += 1
        if inc:
            ins.then_inc(self._sem(ev[0]), 1)
            self.cnt[e] += 1
            self.pending[e] = False
        else:
            self.pending[e] = True
        self._record(ev, reads, writes, ("eng", e))
        return ins

    def dma(self, out, in_, reads=(), writes=(), q="sp", **kw):
        n = self.dma_nq.get(q, 0)
        slot = (q, n % N_DMA_SLOTS)
        key = ("dma",) + slot
        cnt = self.dma_cnt.get(slot, 0)
        if cnt > 0:
            self._wait(q, (key, 16 * cnt))
        self._deps(q, reads, writes)
        ins = self.eng[q].dma_start(out=out, in_=in_, **kw)
        self.dma_nq[q] = n + 1
        self.dma_cnt[slot] = cnt + 1
        ev = (key, 16 * (cnt + 1))
        ins.then_inc(self._sem(key), 16)
        self.n_ins += 1
        self._record(ev, reads, writes, ("dma", slot, cnt + 1))
        return ins

    def barrier(self):
        evs = []
        for e2 in self.eng:
            assert not self.pending[e2], e2
            if self.cnt[e2] > 0:
                evs.append((("eng", e2, self.epoch[e2]), self.cnt[e2]))
        for slot, c in self.dma_cnt.items():
            evs.append((("dma",) + slot, 16 * c))
        for e in self.eng:
            for ev in evs:
                if not (ev[0][0] == "eng" and ev[0][1] == e):
                    self._wait(e, ev)

    def finish(self, e="sp"):
        for slot, c in self.dma_cnt.items():
            self._wait(e, (("dma",) + slot, 16 * c))


def run_pipeline(gens, nstages=2):
    gens = list(gens)
    active = []
    for newer in gens + [None] * (nstages - 1):
        if newer is not None:
            active.append(newer)
        done = [False] * len(active)
        finished = set()
        while not all(done):
            for i, g in enumerate(active):
                if done[i]:
                    continue
                try:
                    if next(g) == "STAGE":
                        done[i] = True
                except StopIteration:
                    done[i] = True
                    finished.add(i)
        active = [g for i, g in enumerate(active) if i not in finished]


class Ctx:
    def sb(self, name, shape, dtype):
        return self.stack.enter_context(self.nc.sbuf_tensor(name, shape, dtype))


def _bcast_rows(ap_1d, n, parts=128):
    return ap_1d.rearrange("(o n) -> o n", o=1).broadcast(0, parts) if hasattr(ap_1d, "broadcast") else None


def prologue(C, x_dram):
    E, nc = C.E, C.nc
    xt = C.sb("pro_x", [128, 2, D], F32)
    xb = C.sb("pro_xb", [128, 2, D], BF16)
    E.psum("pro_ps")
    with nc.psum_tensor("pro_ps", [128, 2, D], BF16) as ps:
        for tt in range(NT):
            s = tt % 2
            E.dma(xt[:, s, :], x_dram[tt * 128:(tt + 1) * 128, :], writes=[("pro_x", s)])
            E.op("act", lambda e: e.copy(out=xb[:, s, :], in_=xt[:, s, :]), reads=[("pro_x", s)], writes=[("pro_xb", s)])
            for k in range(8):
                E.op("pe", lambda e: e.transpose(ps[:, s, k * 128:(k + 1) * 128], xb[:, s, k * 128:(k + 1) * 128], C.ident[:]),
                     reads=[("pro_xb", s), "ident"], writes=[("pro_ps", s)], inc=(k == 7))
            E.op("dve", lambda e: e.tensor_copy(out=C.xT[:, :, tt * 128:(tt + 1) * 128],
                                                in_=ps[:, s, :].rearrange("p (k t) -> p k t", k=8)),
                 reads=[("pro_ps", s)], writes=[("xT", tt)])


def tail(C, L, KF, w_out_d, mixT_d, xres_in, xres_out, last):
    E, nc, W = C.E, C.nc, C.W
    KC = KF // 128
    wout = C.sb(f"tl{L}_wout", [128, KC, D], BF16)
    wg = C.sb(f"tl{L}_wg", [128, 8, D], BF16)
    wp = C.sb(f"tl{L}_wp", [128, 2, D], BF16)
    pT = C.sb(f"tl{L}_pT", [128, 2, T], BF16)
    lng = C.sb(f"tl{L}_lng", [128, D], F32)
    lnb = C.sb(f"tl{L}_lnb", [128, D], F32)
    for k in range(KC):
        E.dma(wout[:, k, :], w_out_d[k * 128:(k + 1) * 128, :], writes=[(f"tl{L}_wout", k)], q="pool")
    for k in range(8):
        E.dma(wg[:, k, :], W["ple_gate"][L, k * 128:(k + 1) * 128, :], writes=[(f"tl{L}_wg", k)], q="pool")
    for k in range(2):
        E.dma(wp[:, k, :], W["ple_proj"][L, k * 128:(k + 1) * 128, :], writes=[(f"tl{L}_wp", k)], q="pool")
        E.dma(pT[:, k, :], C.pT_d[L, k * 128:(k + 1) * 128, :], writes=[(f"tl{L}_pT", k)], q="pool")
    E.dma(lng[:], W["ln_g"][L, :].partition_broadcast(128), writes=[f"tl{L}_lng"])
    E.dma(lnb[:], W["ln_b"][L, :].partition_broadcast(128), writes=[f"tl{L}_lnb"])
    wkeys_out = [(f"tl{L}_wout", k) for k in range(KC)]
    wkeys_g = [(f"tl{L}_wg", k) for k in range(8)]
    wkeys_p = [(f"tl{L}_wp", k) for k in range(2)] + [(f"tl{L}_pT", k) for k in range(2)]

    NB = 4
    mT = C.sb(f"tl{L}_mT", [128, NB, KC, 128], BF16)
    xr = C.sb(f"tl{L}_xr", [128, NB, D], F32)
    h = C.sb(f"tl{L}_h", [128, NB, D], F32)
    yb = C.sb(f"tl{L}_yb", [128, NB, D], BF16)
    yT = C.sb(f"tl{L}_yT", [128, NB, 8, 128], BF16)
    sg = C.sb(f"tl{L}_sg", [128, NB, D], F32)
    xn = C.sb(f"tl{L}_xn", [128, NB, D], F32)
    xnb = C.sb(f"tl{L}_xnb", [128, NB, D], BF16)
    st = C.sb(f"tl{L}_st", [128, NB, 2, 6], F32)
    mv = C.sb(f"tl{L}_mv", [128, NB, 4], F32)
    E.psum("tl_pso", "tl_pt1", "tl_psg", "tl_psp", "tl_pt2")
    with (nc.psum_tensor(f"tl{L}_pso", [128, 2, 512], F32) as pso,
          nc.psum_tensor(f"tl{L}_psg", [128, 2, 512], F32) as psg,
          nc.psum_tensor(f"tl{L}_psp", [128, 2, 512], F32) as psp,
          nc.psum_tensor(f"tl{L}_pt1", [128, D], BF16) as pt1,
          nc.psum_tensor(f"tl{L}_pt2", [128, D], BF16) as pt2):
        def body(tt):
            s = tt % NB
            K = lambda n: (f"tl{L}_{n}", s)
            E.dma(mT[:, s], mixT_d[tt], reads=[("mixT", L, tt)], writes=[K("mT")])
            E.dma(xr[:, s, :], xres_in[tt * 128:(tt + 1) * 128, :], reads=[("xres", L, tt)], writes=[K("xr")])
            for n in range(2):
                yield
                for k in range(KC):
                    E.op("pe", lambda e: e.matmul(pso[:, n, :], lhsT=mT[:, s, k, :], rhs=wout[:, k, n * 512:(n + 1) * 512],
                                                  start=(k == 0), stop=(k == KC - 1)),
                         reads=[K("mT")] + wkeys_out, writes=["tl_pso"], inc=(n == 1 and k == KC - 1))
            yield
            E.op("dve", lambda e: e.scalar_tensor_tensor(out=h[:, s, :], in0=xr[:, s, :], scalar=ALPHA,
                                                         in1=pso[:].rearrange("p a b -> p (a b)"), op0=ALU.mult, op1=ALU.add),
                 reads=[K("xr"), "tl_pso"], writes=[K("h")])
            yield "STAGE"
            for a in range(2):
                E.op("dve", lambda e: e.bn_stats(out=st[:, s, a, :], in_=h[:, s, a * 512:(a + 1) * 512]),
                     reads=[K("h")], writes=[(f"tl{L}_st{a}", s)])
            E.op("dve", lambda e: e.bn_aggr(out=mv[:, s, 0:2], in_=st[:, s].rearrange("p a b -> p (a b)")),
                 reads=[(f"tl{L}_st0", s), (f"tl{L}_st1", s)], writes=[K("mv")])
            E.op("act", lambda e: e.activation(out=mv[:, s, 2:3], in_=mv[:, s, 1:2], func=AF.Sqrt, bias=C.eps_ln[:, 0:1], scale=1.0),
                 reads=[K("mv"), "consts"], writes=[K("mv2")])
            E.op("dve", lambda e: e.scalar_tensor_tensor(out=h[:, s, :], in0=h[:, s, :], scalar=mv[:, s, 0:1], in1=lng[:],
                                                         op0=ALU.subtract, op1=ALU.mult),
                 reads=[K("h"), K("mv"), f"tl{L}_lng"], writes=[K("h")])
            E.op("dve", lambda e: e.reciprocal(out=mv[:, s, 3:4], in_=mv[:, s, 2:3]), reads=[K("mv2")], writes=[K("mv3")])
            yield
            E.op("dve", lambda e: e.scalar_tensor_tensor(out=h[:, s, :], in0=h[:, s, :], scalar=mv[:, s, 3:4], in1=lnb[:],
                                                         op0=ALU.mult, op1=ALU.add),
                 reads=[K("h"), K("mv3"), f"tl{L}_lnb"], writes=[K("h")])
            E.op("act", lambda e: e.copy(out=yb[:, s, :], in_=h[:, s, :]), reads=[K("h")], writes=[K("yb")])
            yield
            for k in range(8):
                E.op("pe", lambda e: e.transpose(pt1[:, k * 128:(k + 1) * 128], yb[:, s, k * 128:(k + 1) * 128], C.ident[:]),
                     reads=[K("yb"), "ident"], writes=["tl_pt1"], inc=(k == 7))
            E.op("act", lambda e: e.copy(out=yT[:, s].rearrange("p k t -> p (k t)"), in_=pt1[:]), reads=["tl_pt1"], writes=[K("yT")])
            yield "STAGE"
            for n in range(2):
                yield
                for k in range(8):
                    E.op("pe", lambda e: e.matmul(psg[:, n, :], lhsT=yT[:, s, k, :], rhs=wg[:, k, n * 512:(n + 1) * 512],
                                                  start=(k == 0), stop=(k == 7)),
                         reads=[K("yT")] + wkeys_g, writes=["tl_psg"], inc=(n == 1 and k == 7))
            for n in range(2):
                for k in range(2):
                    E.op("pe", lambda e: e.matmul(psp[:, n, :], lhsT=pT[:, k, tt * 128:(tt + 1) * 128], rhs=wp[:, k, n * 512:(n + 1) * 512],
                                                  start=(k == 0), stop=(k == 1)),
                         reads=wkeys_p, writes=["tl_psp"], inc=(n == 1 and k == 1))
            yield "STAGE"
            E.op("act", lambda e: e.activation(out=sg[:, s, :], in_=psg[:].rearrange("p a b -> p (a b)"), func=AF.Sigmoid),
                 reads=["tl_psg"], writes=[K("sg")])
            E.op("dve", lambda e: e.tensor_tensor(out=sg[:, s, :], in0=sg[:, s, :], in1=psp[:].rearrange("p a b -> p (a b)"), op=ALU.mult),
                 reads=[K("sg"), "tl_psp"], writes=[K("sg")])
            E.op("pool", lambda e: e.tensor_tensor(out=xn[:, s, :], in0=sg[:, s, :], in1=h[:, s, :], op=ALU.add),
                 reads=[K("sg"), K("h")], writes=[K("xn")])
            yield
            if last:
                E.dma(xres_out[tt * 128:(tt + 1) * 128, :], xn[:, s, :], reads=[K("xn")], writes=[("xres", L + 1, tt)])
            else:
                E.dma(xres_out[tt * 128:(tt + 1) * 128, :], xn[:, s, :], reads=[K("xn")], writes=[("xres", L + 1, tt)])
                E.op("act", lambda e: e.copy(out=xnb[:, s, :], in_=xn[:, s, :]), reads=[K("xn")], writes=[K("xnb")])
                for k in range(8):
                    E.op("pe", lambda e: e.transpose(pt2[:, k * 128:(k + 1) * 128], xnb[:, s, k * 128:(k + 1) * 128], C.ident[:]),
                         reads=[K("xnb"), "ident"], writes=["tl_pt2"], inc=(k == 7))
                E.op("dve", lambda e: e.tensor_copy(out=C.xT[:, :, tt * 128:(tt + 1) * 128],
                                                    in_=pt2[:].rearrange("p (k t) -> p k t", k=8)),
                     reads=["tl_pt2"], writes=[("xT", tt)])

        run_pipeline([body(tt) for tt in range(NT)], nstages=4)


def setup_common(nc, E):
    C = Ctx()
    C.nc, C.E = nc, E
    C.free_later = []
    C.xT = nc.alloc_sbuf_tensor("xT", [128, 8, T], BF16)[:]
    C.ident = nc.alloc_sbuf_tensor("ident", [128, 128], BF16)
    identf = nc.alloc_sbuf_tensor("identf", [128, 128], F32)
    C.identf = identf
    C.eps_ln = nc.alloc_sbuf_tensor("eps_ln", [128, 4], F32)
    E.op("pool", lambda e: e.memset(identf[:], 0.0), writes=["identf"])
    E.op("pool", lambda e: e.affine_select(out=identf[:], in_=identf[:], pattern=[[-1, 128]], compare_op=ALU.not_equal,
                                           fill=1.0, base=0, channel_multiplier=1), reads=["identf"], writes=["identf"])
    E.op("pool", lambda e: e.tensor_copy(out=C.ident[:], in_=identf[:]), reads=["identf"], writes=["ident"])
    E.op("pool", lambda e: e.memset(C.eps_ln[:], LN_EPS), writes=["consts"])
    return C


WEIGHT_SPECS = {
    "ln_g": [4, 1024], "ln_b": [4, 1024], "ple_proj": [4, 256, 1024], "ple_gate": [4, 1024, 1024],
    "ev_w_out": [2, 1024, 1024], "ml_w_out": [2, 2048, 1024],
}


def build_tail_test(KF):
    nc = bass.Bass("TRN2", target_bir_lowering=False)
    E = Emit(nc)
    x_d = nc.dram_tensor("x", [T, D], F32, kind="ExternalInput").ap()
    pT_d = nc.dram_tensor("pT", [4, 256, T], F32, kind="ExternalInput").ap()
    mix_d = nc.dram_tensor("mixT", [NT, 128, KF // 128, 128], F32, kind="ExternalInput").ap()
    W = {k: nc.dram_tensor(k, s, F32, kind="ExternalInput").ap() for k, s in WEIGHT_SPECS.items()}
    out_d = nc.dram_tensor("out", [T, D], F32, kind="ExternalOutput").ap()
    xTo_d = nc.dram_tensor("xT_out", [128, 8, T], F32, kind="ExternalOutput").ap()
    mixb_d = nc.dram_tensor("mixb", [NT, 128, KF // 128, 128], BF16).ap()
    C = setup_common(nc, E)
    C.W, C.pT_d = W, pT_d
    tmp = nc.alloc_sbuf_tensor("tmpmix", [128, KF // 128, 128], BF16)
    for tt in range(NT):
        E.dma(tmp[:], mix_d[tt], writes=["tmpmix"], q="pool")
        E.dma(mixb_d[tt], tmp[:], reads=["tmpmix"], writes=[("mixT", 1, tt)])
    L = 1
    with ExitStack() as st:
        C.stack = st
        tail(C, L, KF, W["ml_w_out" if KF == 2048 else "ev_w_out"][0], mixb_d, x_d, out_d, last=False)
        E.barrier()
    tmpo = nc.alloc_sbuf_tensor("tmpo", [128, T], F32)
    for k in range(8):
        E.op("dve", lambda e: e.tensor_copy(out=tmpo[:], in_=C.xT[:, k, :]), reads=[("xT", tt) for tt in range(NT)], writes=["tmpo"])
        E.dma(xTo_d[:, k, :], tmpo[:], reads=["tmpo"], writes=[("xTo", k)])
    E.finish()
    print("instructions", E.n_ins, "waits", E.n_wait, "sems", len(E.sems))
    return nc


ML_KSCALE = 512.0 ** -0.5
import os
MODE = int(os.environ.get('P1MODE', '2'))


def odd_pass1(C, L, XM, XC, ZG):
    E, nc, W = C.E, C.nc, C.W
    j = L // 2
    P = f"o{L}a_"
    w_in = W["ml_w_in"][j]
    cw = C.sb(P + "cw", [128, 16, 4], F32)
    cb = C.sb(P + "cb", [128, 16], F32)
    E.dma(cw[:], C.HX["ml_cw"][j], writes=[P + "cw"])
    E.dma(cb[:], C.HX["ml_cb"][j], writes=[P + "cb"])
    wch = C.sb(P + "wch", [128, 2, 8, 128], BF16)
    xmf = C.sb(P + "xmf", [128, 2, T + 4], F32)
    acc = C.sb(P + "acc", [128, 2, T], F32)
    xmb = C.sb(P + "xmb", [128, 2, T], BF16)
    xcb = C.sb(P + "xcb", [128, 2, T], BF16)
    zgs = C.sb(P + "zgs", [128, 2, T], BF16)
    for s in range(2):
        E.op("pool", lambda e: e.memset(xmf[:, s, 0:4], 0.0), writes=[(P + "xmf", s)])
    xTkeys = [("xT", tt) for tt in range(NT)]
    E.psum(P + "ps")
    with nc.psum_tensor(P + "ps", [128, 4, 512], F32) as ps:
        nb = 0
        for m in range(32):
            s = m % 2
            is_xm = m < 16
            E.dma(wch[:, s], w_in[:, m * 128:(m + 1) * 128].rearrange("(k p) c -> p k c", p=128), writes=[(P + "wch", s)], q="pool")
            for blk in range(4):
                b = nb % 4
                nb += 1
                for k in range(8):
                    E.op("pe", lambda e: e.matmul(ps[:, b, :], lhsT=wch[:, s, k, :], rhs=C.xT[:, k, blk * 512:(blk + 1) * 512],
                                                  start=(k == 0), stop=(k == 7)),
                         reads=[(P + "wch", s)] + xTkeys[blk * 4:(blk + 1) * 4], writes=[(P + "ps", b)], inc=(k == 7))
                if is_xm:
                    E.op("act", lambda e: e.copy(out=xmf[:, s, 4 + blk * 512:4 + (blk + 1) * 512], in_=ps[:, b, :]),
                         reads=[(P + "ps", b)], writes=[(P + "xmf", s)])
                    E.op("dve", lambda e: e.tensor_copy(out=xmb[:, s, blk * 512:(blk + 1) * 512], in_=ps[:, b, :]),
                         reads=[(P + "ps", b)], writes=[(P + "xmb", s)])
                else:
                    E.op("act", lambda e: e.activation(out=zgs[:, s, blk * 512:(blk + 1) * 512], in_=ps[:, b, :], func=AF.Silu),
                         reads=[(P + "ps", b)], writes=[(P + "zgs", s)])
            if MODE < 1:
                continue
            if is_xm:
                E.op("dve", lambda e: e.tensor_scalar(out=acc[:, s, :], in0=xmf[:, s, 1:1 + T], scalar1=cw[:, m, 0:1], scalar2=cb[:, m:m + 1],
                                                      op0=ALU.mult, op1=ALU.add),
                     reads=[(P + "xmf", s), P + "cw", P + "cb"], writes=[(P + "acc", s)])
                for i in range(1, 4):
                    E.op("dve", lambda e: e.scalar_tensor_tensor(out=acc[:, s, :], in0=xmf[:, s, 1 + i:1 + i + T], scalar=cw[:, m, i:i + 1],
                                                                 in1=acc[:, s, :], op0=ALU.mult, op1=ALU.add),
                         reads=[(P + "xmf", s), (P + "acc", s), P + "cw"], writes=[(P + "acc", s)])
                E.op("act", lambda e: e.activation(out=xcb[:, s, :], in_=acc[:, s, :], func=AF.Silu),
                     reads=[(P + "acc", s)], writes=[(P + "xcb", s)])
                if MODE >= 2:
                    E.dma(XM[m], xmb[:, s, :], reads=[(P + "xmb", s)], writes=[("XM", L, m)])
                    E.dma(XC[m], xcb[:, s, :], reads=[(P + "xcb", s)], writes=[("XC", L, m)])
            elif MODE >= 2:
                E.dma(ZG[m - 16], zgs[:, s, :], reads=[(P + "zgs", s)], writes=[("ZG", L, m - 16)])


def odd_pass2(C, L, XM, XC, ZG, mixT_d):
    E, nc, W = C.E, C.nc, C.W
    j = L // 2
    P = f"o{L}b_"
    K = lambda n: P + n
    bd = C.sb(P + "bd", [128, 3, 16, 128], BF16)
    for wi in range(3):
        E.dma(bd[:, wi], C.bd_d[j, wi].rearrange("m c d -> c m d"), writes=[(P + "bd", wi)], q="pool")
    bdkeys = [(P + "bd", wi) for wi in range(3)]
    G1f = C.sb(P + "G1f", [128, 16, 8], F32)
    G2f = C.sb(P + "G2f", [128, 16, 8], F32)
    G1 = C.sb(P + "G1", [128, 16, 8], BF16)
    G2 = C.sb(P + "G2", [128, 16, 8], BF16)
    wg = C.sb(P + "wgate", [128, 3, 16, 8], F32)
    E.dma(wg[:], C.HX["ml_wgate"][j], writes=[(P + "wgate", wi, a) for wi in range(3) for a in range(2)])
    wgk = [(P + "wgate", wi, a) for wi in range(3) for a in range(2)]
    bdT = C.sb(P + "bdT", [128, 2, 3, 128], F32)
    E.psum(P + "psG", merge=P + "psG")
    with nc.psum_tensor(P + "psG", [128, 2, 2, 8], F32) as psG:
        for m in range(16):
            s = m % 2
            for wi in range(3):
                E.dma(bdT[:, s, wi, :], C.bdT_d[j, wi, m], writes=[(P + "bdT", s, wi)])
            E.op("pe", lambda e: e.matmul(psG[:, s, 0, :], lhsT=bdT[:, s, 0, :], rhs=wg[:, 0, m, :], start=True, stop=False),
                 reads=[(P + "bdT", s, 0)] + wgk, writes=[(P + "psG", s)], inc=False)
            E.op("pe", lambda e: e.matmul(psG[:, s, 0, :], lhsT=bdT[:, s, 1, :], rhs=wg[:, 1, m, :], start=False, stop=True),
                 reads=[(P + "bdT", s, 1)] + wgk, writes=[(P + "psG", s)], inc=False)
            E.op("pe", lambda e: e.matmul(psG[:, s, 1, :], lhsT=bdT[:, s, 2, :], rhs=wg[:, 2, m, :], start=True, stop=True),
                 reads=[(P + "bdT", s, 2)] + wgk, writes=[(P + "psG", s)])
            E.op("dve", lambda e: e.tensor_copy(out=G1[:, m, :], in_=psG[:, s, 0, :]), reads=[(P + "psG", s)], writes=[K("G1")])
            E.op("dve", lambda e: e.tensor_copy(out=G2[:, m, :], in_=psG[:, s, 1, :]), reads=[(P + "psG", s)], writes=[K("G2")])
    nrm_bc = C.sb(P + "nrm", [128, 2048], F32)
    skp = C.sb(P + "skp", [128, 16], F32)
    E.dma(nrm_bc[:], W["ml_norm"][j].partition_broadcast(128), writes=[K("nrm")])
    E.dma(skp[:], C.HX["ml_skp"][j], writes=[K("skp")])
    big = C.sb(P + "big", [4, 1], F32)
    bfg = C.sb(P + "bfg", [4, 1], F32)
    nbfg = C.sb(P + "nbfg", [4, 1], F32)
    E.dma(big[:], W["ml_b_ig"][j].rearrange("(h o) -> h o", o=1), writes=[K("big")])
    E.dma(bfg[:], W["ml_b_fg"][j].rearrange("(h o) -> h o", o=1), writes=[K("bfg")])
    E.op("dve", lambda e: e.tensor_scalar(out=nbfg[:], in0=bfg[:], scalar1=-1.0, scalar2=None, op0=ALU.mult), reads=[K("bfg")], writes=[K("nbfg")])
    mask01 = C.sb(P + "mask", [128, 4, 128], F32)
    E.op("pool", lambda e: e.memset(mask01[:], 1.0), writes=[K("mask")])
    E.op("pool", lambda e: e.affine_select(out=mask01[:], in_=mask01[:], pattern=[[0, 4], [1, 128]], compare_op=ALU.is_ge,
                                           fill=0.0, base=0, channel_multiplier=-1), reads=[K("mask")], writes=[K("mask")])
    dmask = C.sb(P + "dmask", [4, 4], F32)
    E.op("pool", lambda e: e.memset(dmask[:], 0.0), writes=[K("dmask")])
    E.op("pool", lambda e: e.affine_select(out=dmask[:], in_=dmask[:], pattern=[[-1, 4]], compare_op=ALU.not_equal,
                                           fill=1.0, base=0, channel_multiplier=1), reads=[K("dmask")], writes=[K("dmask")])
    Ysel = C.sb(P + "Ysel", [4, 4, 128], F32)
    E.op("pool", lambda e: e.memset(Ysel[:], 0.0), writes=[K("Ysel")])
    E.op("pool", lambda e: e.affine_select(out=Ysel[:], in_=Ysel[:], pattern=[[-1, 4], [0, 128]], compare_op=ALU.not_equal,
                                           fill=1.0, base=0, channel_multiplier=1), reads=[K("Ysel")], writes=[K("Ysel")])
    ones4 = C.sb(P + "ones4", [4, 128], F32)
    E.op("pool", lambda e: e.memset(ones4[:], 1.0), writes=[K("ones4")])
    onesb = C.sb(P + "onesb", [128, 1], BF16)
    E.op("pool", lambda e: e.memset(onesb[:], 1.0), writes=[K("onesb")])
    Cf = C.sb(P + "Cf", [128, 4, 4, 512], F32)
    Cb = C.sb(P + "Cb", [128, 4, 4, 512], BF16)
    nf = C.sb(P + "nf", [128, 16], F32)
    nbf = C.sb(P + "nbf", [128, 16], BF16)
    for h in range(4):
        E.op("pool", lambda e: e.memset(Cf[:, h], 0.0), writes=[(P + "Cf", h)])
        E.op("pool", lambda e: e.memset(Cb[:, h], 0.0), writes=[(P + "Cb", h)])
    E.op("pool", lambda e: e.memset(nf[:], 0.0), writes=[K("nf")])
    E.op("pool", lambda e: e.memset(nbf[:], 0.0), writes=[K("nbf")])
    mcar = C.sb(P + "mcar", [4, 2], F32)
    E.op("pool", lambda e: e.memset(mcar[:], 0.0), writes=[(P + "mcar", 0), (P + "mcar", 1)])
    NB = 2
    xm_t = C.sb(P + "xm_t", [128, NB, 16, 128], BF16)
    xc_t = C.sb(P + "xc_t", [128, NB, 16, 128], BF16)
    zg_t = C.sb(P + "zg_t", [128, NB, 16, 128], BF16)
    qT = C.sb(P + "qT", [128, 16, 128], BF16)
    kT = C.sb(P + "kT", [128, 16, 128], BF16)
    qsT = C.sb(P + "qsT", [128, 16, 128], BF16)
    vtok = C.sb(P + "vtok", [128, 2048], BF16)
    kws = C.sb(P + "kws", [128, 2048], BF16)
    R = C.sb(P + "R", [4, 16, 128], F32)
    Xb = C.sb(P + "Xb", [4, 2, 4, 128], F32)
    ddec = C.sb(P + "ddec", [4, 4], F32)
    col = C.sb(P + "col", [128, 16], F32)
    sm = C.sb(P + "sm", [128, 4, 8], F32)
    wexp = C.sb(P + "wexp", [128, 512], F32)
    wtmp = C.sb(P + "wtmp", [128, 512], F32)
    wts = C.sb(P + "wts", [128, 4, 128], BF16)
    hsc = C.sb(P + "hsc", [128, 2, 512], F32)
    hn = C.sb(P + "hn", [128, 2048], BF16)
    st6 = C.sb(P + "st6", [128, 2, 6], F32)
    t1 = C.sb(P + "t1", [128, 16, 128], F32)
    oT = C.sb(P + "oT", [128, NB, 16, 128], BF16)
    ROW = {n: i for i, n in enumerate(["ig", "e", "lf", "logf", "bcum", "m", "negmu", "g", "a1", "c1", "sint", "ws", "emm", "ones"])}
    rr = lambda n: R[:, ROW[n], :]
    E.op("pool", lambda e: e.memset(rr("ones"), 1.0), writes=[K("r_ones")])
    E.psum(P + "pg", P + "pe", P + "pz", P + "pqk", P + "pt")
    E.psum(P + "psm_g", P + "psm_c", P + "psm_d", P + "psm_n", merge=P + "psm")
    with (nc.psum_tensor(P + "pg", [128, 3, 512], F32) as pg,
          nc.psum_tensor(P + "psm", [128, 512], F32) as psm,
          nc.psum_tensor(P + "pe", [128, 512], F32) as pe_,
          nc.psum_tensor(P + "pz", [128, 512], F32) as pz,
          nc.psum_tensor(P + "pqk", [128, 512], F32) as pqk,
          nc.psum_tensor(P + "pt", [128, 1024], BF16) as pt):
        gi = [0]

        def gbank():
            b = gi[0] % 3
            gi[0] += 1
            return b

        for tt in range(C.ntiles):
            s = tt % NB
            tsl = slice(tt * 128, (tt + 1) * 128)
            E.dma(xm_t[:, s], XM[:, :, tsl].rearrange("m p t -> p m t"), reads=[("XM", L, m) for m in range(16)], writes=[(P + "xm_t", s)])
            E.dma(xc_t[:, s], XC[:, :, tsl].rearrange("m p t -> p m t"), reads=[("XC", L, m) for m in range(16)], writes=[(P + "xc_t", s)])
            E.dma(zg_t[:, s], ZG[:, :, tsl].rearrange("m p t -> p m t"), reads=[("ZG", L, m) for m in range(16)], writes=[(P + "zg_t", s)])
            kxm, kxc, kzg = (P + "xm_t", s), (P + "xc_t", s), (P + "zg_t", s)
            for gsel in range(2):
                for m in range(16):
                    E.op("pe", lambda e: e.matmul(psm[0:4, gsel * 128:(gsel + 1) * 128], lhsT=G1[:, m, gsel * 4:(gsel + 1) * 4], rhs=xc_t[:, s, m, :],
                                                  start=(m == 0), stop=False), reads=[K("G1"), kxc], writes=[K("psm_g")], inc=False)
                for m in range(16):
                    E.op("pe", lambda e: e.matmul(psm[0:4, gsel * 128:(gsel + 1) * 128], lhsT=G2[:, m, gsel * 4:(gsel + 1) * 4], rhs=xm_t[:, s, m, :],
                                                  start=False, stop=(m == 15)), reads=[K("G2"), kxm], writes=[K("psm_g")], inc=(m == 15 and gsel == 1))
            mprev = mcar[:, (tt % 2):(tt % 2) + 1]
            mnext = mcar[:, ((tt + 1) % 2):((tt + 1) % 2) + 1]
            kmp, kmn = (P + "mcar", tt % 2), (P + "mcar", (tt + 1) % 2)
            E.op("act", lambda e: e.activation(out=rr("ig"), in_=psm[0:4, 0:128], func=AF.Identity, bias=big[:, 0:1], scale=1.0),
                 reads=[K("psm_g"), K("big")], writes=[K("r_ig")])
            E.op("act", lambda e: e.activation(out=rr("e"), in_=psm[0:4, 128:256], func=AF.Exp, bias=nbfg[:, 0:1], scale=-1.0),
                 reads=[K("psm_g"), K("nbfg")], writes=[K("r_e")])
            E.op("act", lambda e: e.activation(out=rr("lf"), in_=rr("e"), func=AF.Ln, bias=1.0, scale=1.0), reads=[K("r_e")], writes=[K("r_lf")])
            E.op("dve", lambda e: e.tensor_scalar(out=rr("logf"), in0=rr("lf"), scalar1=-1.0, scalar2=None, op0=ALU.mult),
                 reads=[K("r_lf")], writes=[K("r_logf")])
            E.op("dve", lambda e: e.tensor_tensor_scan(out=rr("bcum"), data0=rr("ones"), data1=rr("logf"), initial=0.0, op0=ALU.mult, op1=ALU.add),
                 reads=[K("r_ones"), K("r_logf")], writes=[K("r_bcum")])
            E.op("dve", lambda e: e.tensor_tensor_scan(out=rr("m"), data0=rr("logf"), data1=rr("ig"), initial=mprev, op0=ALU.add, op1=ALU.max),
                 reads=[K("r_logf"), K("r_ig"), kmp], writes=[K("r_m")])
            E.op("dve", lambda e: e.tensor_copy(out=mnext, in_=R[:, ROW["m"], 127:128]), reads=[K("r_m")], writes=[kmn])
            E.op("dve", lambda e: e.tensor_tensor(out=rr("negmu"), in0=rr("bcum"), in1=rr("m"), op=ALU.subtract),
                 reads=[K("r_bcum"), K("r_m")], writes=[K("r_negmu")])
            E.op("dve", lambda e: e.tensor_tensor(out=rr("g"), in0=rr("ig"), in1=rr("bcum"), op=ALU.subtract),
                 reads=[K("r_ig"), K("r_bcum")], writes=[K("r_g")])
            E.op("act", lambda e: e.activation(out=rr("sint"), in_=rr("negmu"), func=AF.Exp, bias=mprev, scale=1.0),
                 reads=[K("r_negmu"), kmp], writes=[K("r_sint")])
            E.op("act", lambda e: e.activation(out=rr("ws"), in_=rr("g"), func=AF.Exp, bias=R[:, ROW["negmu"], 127:128], scale=1.0),
                 reads=[K("r_g"), K("r_negmu")], writes=[K("r_ws")])
            E.op("dve", lambda e: e.tensor_scalar(out=rr("ws"), in0=rr("ws"), scalar1=ML_KSCALE, scalar2=None, op0=ALU.mult),
                 reads=[K("r_ws")], writes=[K("r_ws")])
            E.op("act", lambda e: e.activation(out=rr("emm"), in_=rr("m"), func=AF.Exp, scale=-1.0), reads=[K("r_m")], writes=[K("r_emm")])
            E.op("act", lambda e: e.activation(out=R[:, ROW["a1"], 0:1], in_=R[:, ROW["negmu"], 127:128], func=AF.Exp, bias=mprev, scale=1.0),
                 reads=[K("r_negmu"), kmp], writes=[K("r_dec")])
            E.op("dve", lambda e: e.tensor_tensor(out=Xb[:, 0], in0=rr("negmu").unsqueeze(1).broadcast_to([4, 4, 128]),
                                                  in1=dmask[:].unsqueeze(2).broadcast_to([4, 4, 128]), op=ALU.mult),
                 reads=[K("r_negmu"), K("dmask")], writes=[K("Xb0")])
            E.op("dve", lambda e: e.tensor_tensor(out=Xb[:, 1], in0=rr("sint").unsqueeze(1).broadcast_to([4, 4, 128]),
                                                  in1=dmask[:].unsqueeze(2).broadcast_to([4, 4, 128]), op=ALU.mult),
                 reads=[K("r_sint"), K("dmask")], writes=[K("Xb1")])
            E.op("dve", lambda e: e.tensor_scalar(out=ddec[:], in0=dmask[:], scalar1=R[:, ROW["a1"], 0:1], scalar2=None, op0=ALU.mult),
                 reads=[K("r_dec"), K("dmask")], writes=[K("ddec")])
            for qi, nm in enumerate(["sint", "ws", "emm"]):
                E.op("pe", lambda e: e.matmul(psm[:, 256 + qi * 4:256 + (qi + 1) * 4], lhsT=rr(nm), rhs=C.identf[0:4, 0:4], start=True, stop=True),
                     reads=[K("r_" + nm), "identf"], writes=[K("psm_c")], inc=False)
            E.op("pe", lambda e: e.matmul(psm[:, 268:272], lhsT=ones4[:], rhs=ddec[:], start=True, stop=True),
                 reads=[K("ones4"), K("ddec")], writes=[K("psm_c")])
            E.op("dve", lambda e: e.tensor_copy(out=col[:], in_=psm[:, 256:272]), reads=[K("psm_c")], writes=[K("col")])
            for which, dst in ((0, qT), (1, kT)):
                for h in range(4):
                    b = gbank()
                    for mm in range(4):
                        m = 4 * h + mm
                        E.op("pe", lambda e: e.matmul(pg[:, b, mm * 128:(mm + 1) * 128], lhsT=bd[:, which, m, :], rhs=xc_t[:, s, m, :], start=True, stop=True),
                             reads=[kxc] + bdkeys, writes=[(P + "pg", b)], inc=(mm == 3))
                    if which == 0:
                        E.op("act", lambda e: e.copy(out=qT[:, 4 * h:4 * h + 4, :].rearrange("p a b -> p (a b)"), in_=pg[:, b, :]),
                             reads=[(P + "pg", b)], writes=[(P + "qT", h)])
                    else:
                        E.op("act", lambda e: e.activation(out=kT[:, 4 * h:4 * h + 4, :].rearrange("p a b -> p (a b)"), in_=pg[:, b, :],
                                                           func=AF.Copy, scale=ML_KSCALE),
                             reads=[(P + "pg", b)], writes=[(P + "kT", h)])
            for h in range(4):
                b = gbank()
                for mm in range(4):
                    m = 4 * h + mm
                    E.op("pe", lambda e: e.matmul(pg[:, b, mm * 128:(mm + 1) * 128], lhsT=xc_t[:, s, m, :], rhs=bd[:, 1, m, :], start=True, stop=True),
                         reads=[kxc] + bdkeys, writes=[(P + "pg", b)], inc=(mm == 3))
                E.op("act", lambda e: e.activation(out=kws[:, h * 512:(h + 1) * 512], in_=pg[:, b, :], func=AF.Copy, scale=col[:, 4 + h:5 + h]),
                     reads=[(P + "pg", b), K("col")], writes=[(P + "kws", h)])
                b = gbank()
                for mm in range(4):
                    m = 4 * h + mm
                    E.op("pe", lambda e: e.matmul(pg[:, b, mm * 128:(mm + 1) * 128], lhsT=xm_t[:, s, m, :], rhs=bd[:, 2, m, :], start=True, stop=True),
                         reads=[kxm] + bdkeys, writes=[(P + "pg", b)], inc=(mm == 3))
                E.op("dve", lambda e: e.tensor_copy(out=vtok[:, h * 512:(h + 1) * 512], in_=pg[:, b, :]),
                     reads=[(P + "pg", b)], writes=[(P + "vtok", h)])
            E.op("pe", lambda e: e.matmul(pe_[:], lhsT=ones4[:], rhs=Xb[:, 0].rearrange("q h t -> q (h t)"), start=True, stop=False),
                 reads=[K("ones4"), K("Xb0")], writes=[K("pe")], inc=False)
            E.op("pe", lambda e: e.matmul(pe_[:], lhsT=rr("g"), rhs=Ysel[:].rearrange("q h t -> q (h t)"), start=False, stop=True),
                 reads=[K("r_g"), K("Ysel")], writes=[K("pe")])
            E.op("act", lambda e: e.activation(out=wexp[:], in_=pe_[:], func=AF.Exp), reads=[K("pe")], writes=[K("wexp")])
            E.op("pe", lambda e: e.matmul(pz[:], lhsT=ones4[:], rhs=Xb[:, 1].rearrange("q h t -> q (h t)"), start=True, stop=True),
                 reads=[K("ones4"), K("Xb1")], writes=[K("pz")])
            for h in range(4):
                for mm in range(4):
                    m = 4 * h + mm
                    E.op("pe", lambda e: e.matmul(pqk[:, h * 128:(h + 1) * 128], lhsT=kT[:, m, :], rhs=qT[:, m, :], start=(mm == 0), stop=(mm == 3)),
                         reads=[(P + "kT", h), (P + "qT", h)], writes=[K("pqk")], inc=(h == 3 and mm == 3))
            E.op("dve", lambda e: e.tensor_tensor(out=wtmp[:], in0=wexp[:], in1=pqk[:], op=ALU.mult), reads=[K("wexp"), K("pqk")], writes=[K("wtmp")])
            E.op("pool", lambda e: e.tensor_tensor(out=wts[:].rearrange("p h t -> p (h t)"), in0=wtmp[:], in1=mask01[:].rearrange("p h t -> p (h t)"), op=ALU.mult),
                 reads=[K("wtmp"), K("mask")], writes=[K("wts")])
            for h in range(4):
                E.op("dve", lambda e: e.tensor_tensor(out=qsT[:, 4 * h:4 * h + 4, :], in0=qT[:, 4 * h:4 * h + 4, :],
                                                      in1=pz[:, h * 128:(h + 1) * 128].unsqueeze(1).broadcast_to([128, 4, 128]), op=ALU.mult),
                     reads=[(P + "qT", h), K("pz")], writes=[(P + "qsT", h)])
            for h in range(4):
                b = gbank()
                E.op("pe", lambda e: e.matmul(pg[:, b, :], lhsT=wts[:, h, :], rhs=vtok[:, h * 512:(h + 1) * 512], start=True, stop=False),
                     reads=[K("wts"), (P + "vtok", h)], writes=[(P + "pg", b)], inc=False)
                for mm in range(4):
                    E.op("pe", lambda e: e.matmul(pg[:, b, :], lhsT=qsT[:, 4 * h + mm, :], rhs=Cb[:, h, mm, :], start=False, stop=(mm == 3)),
                         reads=[(P + "qsT", h), (P + "Cb", h)], writes=[(P + "pg", b)], inc=(mm == 3))
                E.op("pe", lambda e: e.matmul(psm[:, 272 + h:273 + h], lhsT=wts[:, h, :], rhs=onesb[:], start=True, stop=False),
                     reads=[K("wts"), K("onesb")], writes=[(P + "psm_d", h)], inc=False)
                for mm in range(4):
                    E.op("pe", lambda e: e.matmul(psm[:, 272 + h:273 + h], lhsT=qsT[:, 4 * h + mm, :], rhs=nbf[:, 4 * h + mm:4 * h + mm + 1], start=False, stop=(mm == 3)),
                         reads=[(P + "qsT", h), K("nbf")], writes=[(P + "psm_d", h)], inc=(mm == 3))
                sh = h % 2
                E.op("act", lambda e: e.activation(out=sm[:, h, 6:7], in_=psm[:, 272 + h:273 + h], func=AF.Abs),
                     reads=[(P + "psm_d", h)], writes=[(P + "sm6", h)])
                E.op("dve", lambda e: e.tensor_tensor(out=sm[:, h, 0:1], in0=sm[:, h, 6:7], in1=col[:, 8 + h:9 + h], op=ALU.max),
                     reads=[(P + "sm6", h), K("col")], writes=[(P + "sm0", h)])
                E.op("dve", lambda e: e.reciprocal(out=sm[:, h, 1:2], in_=sm[:, h, 0:1]), reads=[(P + "sm0", h)], writes=[(P + "sm1", h)])
                E.op("act", lambda e: e.activation(out=hsc[:, sh, :], in_=pg[:, b, :], func=AF.Copy, scale=sm[:, h, 1:2]),
                     reads=[(P + "pg", b), (P + "sm1", h)], writes=[(P + "hsc", sh)])
                E.op("dve", lambda e: e.bn_stats(out=st6[:, sh, :], in_=hsc[:, sh, :]), reads=[(P + "hsc", sh)], writes=[(P + "st6", sh)])
                E.op("dve", lambda e: e.bn_aggr(out=sm[:, h, 2:4], in_=st6[:, sh, :]), reads=[(P + "st6", sh)], writes=[(P + "sm2", h)])
                E.op("act", lambda e: e.activation(out=sm[:, h, 4:5], in_=sm[:, h, 3:4], func=AF.Sqrt, bias=C.eps_ln[:, 0:1], scale=1.0),
                     reads=[(P + "sm2", h), "consts"], writes=[(P + "sm4", h)])
                E.op("dve", lambda e: e.reciprocal(out=sm[:, h, 5:6], in_=sm[:, h, 4:5]), reads=[(P + "sm4", h)], writes=[(P + "sm5", h)])
                E.op("dve", lambda e: e.tensor_scalar(out=hsc[:, sh, :], in0=hsc[:, sh, :], scalar1=sm[:, h, 2:3], scalar2=sm[:, h, 5:6],
                                                      op0=ALU.subtract, op1=ALU.mult),
                     reads=[(P + "hsc", sh), (P + "sm2", h), (P + "sm5", h)], writes=[(P + "hsc", sh)])
                E.op("pool", lambda e: e.tensor_tensor(out=hn[:, h * 512:(h + 1) * 512], in0=hsc[:, sh, :], in1=nrm_bc[:, h * 512:(h + 1) * 512], op=ALU.mult),
                     reads=[(P + "hsc", sh), K("nrm")], writes=[(P + "hn", h)])
            for half in range(2):
                for mm in range(8):
                    m = half * 8 + mm
                    E.op("pe", lambda e: e.transpose(pt[:, mm * 128:(mm + 1) * 128], hn[:, m * 128:(m + 1) * 128], C.ident[:]),
                         reads=[(P + "hn", m // 4), "ident"], writes=[K("pt")], inc=(mm == 7))
                for mm in range(8):
                    m = half * 8 + mm
                    E.op("dve", lambda e: e.scalar_tensor_tensor(out=t1[:, m, :], in0=xc_t[:, s, m, :], scalar=skp[:, m:m + 1],
                                                                 in1=pt[:, mm * 128:(mm + 1) * 128], op0=ALU.mult, op1=ALU.add),
                         reads=[kxc, K("skp"), K("pt")], writes=[(P + "t1", half)])
            E.op("pool", lambda e: e.tensor_tensor(out=oT[:, s].rearrange("p m t -> p (m t)"), in0=t1[:].rearrange("p m t -> p (m t)"),
                                                   in1=zg_t[:, s].rearrange("p m t -> p (m t)"), op=ALU.mult),
                 reads=[(P + "t1", 0), (P + "t1", 1), kzg], writes=[(P + "oT", s)])
            E.dma(mixT_d[tt], oT[:, s], reads=[(P + "oT", s)], writes=[("mixT", L, tt)])
            for h in range(4):
                for mm in range(4):
                    b = gbank()
                    E.op("pe", lambda e: e.matmul(pg[:, b, :], lhsT=kws[:, h * 512 + mm * 128:h * 512 + (mm + 1) * 128], rhs=vtok[:, h * 512:(h + 1) * 512],
                                                  start=True, stop=True),
                         reads=[(P + "kws", h), (P + "vtok", h)], writes=[(P + "pg", b)])
                    E.op("dve", lambda e: e.scalar_tensor_tensor(out=Cf[:, h, mm, :], in0=Cf[:, h, mm, :], scalar=col[:, 12 + h:13 + h],
                                                                 in1=pg[:, b, :], op0=ALU.mult, op1=ALU.add),
                         reads=[(P + "Cf", h), K("col"), (P + "pg", b)], writes=[(P + "Cf", h)])
                E.op("act", lambda e: e.copy(out=Cb[:, h].rearrange("p a b -> p (a b)"), in_=Cf[:, h].rearrange("p a b -> p (a b)")),
                     reads=[(P + "Cf", h)], writes=[(P + "Cb", h)])
            for h in range(4):
                for mm in range(4):
                    E.op("pe", lambda e: e.matmul(psm[:, 276 + 4 * h + mm:277 + 4 * h + mm], lhsT=kws[:, h * 512 + mm * 128:h * 512 + (mm + 1) * 128], rhs=onesb[:],
                                                  start=True, stop=True),
                         reads=[(P + "kws", h), K("onesb")], writes=[K("psm_n")], inc=(h == 3 and mm == 3))
            for h in range(4):
                E.op("dve", lambda e: e.scalar_tensor_tensor(out=nf[:, 4 * h:4 * h + 4], in0=nf[:, 4 * h:4 * h + 4], scalar=col[:, 12 + h:13 + h],
                                                             in1=psm[:, 276 + 4 * h:280 + 4 * h], op0=ALU.mult, op1=ALU.add),
                     reads=[K("nf"), K("col"), K("psm_n")], writes=[K("nf")])
            E.op("dve", lambda e: e.tensor_copy(out=nbf[:], in_=nf[:]), reads=[K("nf")], writes=[K("nbf")])


ALL_WEIGHTS = {
    "hg_lower_bounds": [4, 512], "ln_g": [4, 1024], "ln_b": [4, 1024], "ple_proj": [4, 256, 1024], "ple_gate": [4, 1024, 1024],
    "ev_first_w_in": [1024, 4224], "ev_first_mu": [1664], "ev_rest_w_in": [1, 1024, 4256], "ev_rest_mu": [1, 1696],
    "rw_v0": [1, 512], "rw_v_up": [1, 32, 512], "hg_gnorm": [2, 512], "rw_w0": [2, 512], "rw_w_up": [2, 64, 512],
    "rw_a0": [2, 512], "rw_a_up": [2, 64, 512], "rw_kk": [2, 512], "rw_ka": [2, 512], "rw_rk": [2, 512],
    "rw_gn_w": [2, 512], "rw_gn_b": [2, 512], "ev_w_out": [2, 1024, 1024], "ml_w_in": [2, 1024, 4096],
    "ml_conv_w": [2, 4, 2048], "ml_conv_b": [2, 2048], "ml_w_ig": [2, 6144, 4], "ml_b_ig": [2, 4], "ml_w_fg": [2, 6144, 4],
    "ml_b_fg": [2, 4], "ml_norm": [2, 2048], "ml_skip": [2, 2048], "ml_w_out": [2, 2048, 1024],
}
HOST_EXTRA = {"ml_bd": [2, 3, 16, 128, 128], "ml_bdT": [2, 3, 16, 128, 128], "ml_cw": [2, 128, 16, 4], "ml_cb": [2, 128, 16],
              "ml_skp": [2, 128, 16], "ml_wgate": [2, 128, 3, 16, 8], "ev_mu_pm": [2, 128, 14], "hg_lb_pm": [128, 4, 4]}


def build_program(layers=(0, 1, 2, 3), debug_mix=False, stages=None, ntiles=NT):
    nc = bass.Bass("TRN2", target_bir_lowering=False)
    E = Emit(nc)
    x_d = nc.dram_tensor("x", [T, D], F32, kind="ExternalInput").ap()
    pT_d = nc.dram_tensor("pT", [4, 256, T], F32, kind="ExternalInput").ap()
    W = {k: nc.dram_tensor(k, s, F32, kind="ExternalInput").ap() for k, s in ALL_WEIGHTS.items()}
    HX = {k: nc.dram_tensor(k, s, F32, kind="ExternalInput").ap() for k, s in HOST_EXTRA.items()}
    out_d = nc.dram_tensor("out", [T, D], F32, kind="ExternalOutput").ap()
    xres = [nc.dram_tensor(f"xres{i}", [T, D], F32).ap() for i in range(2)]
    XM = nc.dram_tensor("XM", [16, 128, T], BF16).ap()
    XC = nc.dram_tensor("XC", [16, 128, T], BF16).ap()
    ZG = nc.dram_tensor("ZG", [16, 128, T], BF16).ap()
    mixo = nc.dram_tensor("mixo", [NT, 128, 16, 128], BF16).ap()
    mixe = nc.dram_tensor("mixe", [NT, 128, 8, 128], BF16).ap()
    EVS = {k: nc.dram_tensor("evs_" + k, sh, dt).ap() for k, (sh, dt) in EV_SCR.items()}
    C = setup_common(nc, E)
    C.W, C.pT_d, C.bd_d, C.bdT_d = W, pT_d, HX["ml_bd"], HX["ml_bdT"]
    C.HX = HX
    C.ntiles = ntiles
    def phase(fn, *a):
        with ExitStack() as st:
            C.stack = st
            fn(*a)
            E.barrier()

    phase(prologue, C, x_d)
    cur = x_d
    for li, L in enumerate(layers):
        last = li == len(layers) - 1
        nxt = out_d if last else xres[li % 2]
        if L % 2 == 1:
            if stages is None or "p1" in stages:
                phase(odd_pass1, C, L, XM, XC, ZG)
            if stages is None or "p2" in stages:
                phase(odd_pass2, C, L, XM, XC, ZG, mixo)
            if stages is None or "tail" in stages:
                phase(tail, C, L, 2048, W["ml_w_out"][L // 2], mixo, cur, nxt, last)
        else:
            even_layer_phases(C, L, EVS, mixe, phase, stages)
            if stages is None or "tail" in stages:
                phase(tail, C, L, 1024, W["ev_w_out"][L // 2], mixe, cur, nxt, last)
        cur = nxt
    if debug_mix:
        KC = 16 if layers[-1] % 2 == 1 else 8
        src = mixo if KC == 16 else mixe
        dbg = nc.dram_tensor("dbg_mix", [NT, 128, KC, 128], F32, kind="ExternalOutput").ap()
        tb = nc.alloc_sbuf_tensor("dbg_b", [128, KC, 128], BF16)
        tf = nc.alloc_sbuf_tensor("dbg_f", [128, KC, 128], F32)
        for tt in range(NT):
            E.dma(tb[:], src[tt], reads=[("mixT", layers[-1], tt)], writes=["dbg_b"])
            E.op("dve", lambda e: e.tensor_copy(out=tf[:], in_=tb[:]), reads=["dbg_b"], writes=["dbg_f"])
            E.dma(dbg[tt], tf[:], reads=["dbg_f"], writes=[("dbg", tt)])
    E.finish()
    print("instructions", E.n_ins, "waits", E.n_wait, "sems", len(E.sems), flush=True)
    return nc


def host_prep(inputs):
    f = lambda a: np.ascontiguousarray(np.asarray(a, dtype=np.float32))
    shared = {k: f(inputs[k]) for k in ALL_WEIGHTS}
    bd = np.zeros((2, 3, 16, 128, 128), np.float32)
    for wi, nm in enumerate(["ml_wq", "ml_wk", "ml_wv"]):
        w = np.asarray(inputs[nm], np.float32).reshape(2, 16, 32, 4, 4)
        for b in range(32):
            bd[:, wi, :, 4 * b:4 * b + 4, 4 * b:4 * b + 4] = w[:, :, b]
    shared["ml_bd"] = bd
    shared["ml_bdT"] = np.ascontiguousarray(bd.transpose(0, 1, 2, 4, 3))
    fm = lambda v: np.ascontiguousarray(np.asarray(v, np.float32).reshape(2, 16, 128).transpose(0, 2, 1))
    shared["ml_cw"] = np.ascontiguousarray(np.asarray(inputs["ml_conv_w"], np.float32).reshape(2, 4, 16, 128).transpose(0, 3, 2, 1))
    shared["ml_cb"] = fm(inputs["ml_conv_b"])
    shared["ml_skp"] = fm(inputs["ml_skip"])
    wgt = np.zeros((2, 128, 3, 16, 8), np.float32)
    for a, nm in enumerate(["ml_w_ig", "ml_w_fg"]):
        w = np.asarray(inputs[nm], np.float32).reshape(2, 3, 16, 128, 4)
        wgt[:, :, :, :, 4 * a:4 * a + 4] = w.transpose(0, 3, 1, 2, 4)
    shared["ml_wgate"] = wgt
    mu = np.zeros((2, 14 * 128), np.float32)
    mu[0, :1664] = np.asarray(inputs["ev_first_mu"], np.float32)
    mu[1, :1696] = np.asarray(inputs["ev_rest_mu"], np.float32)[0]
    shared["ev_mu_pm"] = np.ascontiguousarray(mu.reshape(2, 14, 128).transpose(0, 2, 1))
    shared["hg_lb_pm"] = np.ascontiguousarray(np.asarray(inputs["hg_lower_bounds"], np.float32).reshape(4, 4, 128).transpose(2, 1, 0))
    x = np.asarray(inputs["x"], np.float32)
    p = np.asarray(inputs["p"], np.float32)
    per_core = []
    for b in range(x.shape[0]):
        per_core.append({"x": f(x[b]), "pT": f(np.transpose(p[:, b], (0, 2, 1)))})
    return shared, per_core


RW_KAPPA = -float(np.exp(-0.5))
RW_GN_EPS = 64e-5
EV_SCR = {"HQ": ([4, 128, T], F32), "HF": ([4, 128, T], F32), "HI": ([4, 128, T], BF16), "GT": ([8, 128, T], BF16),
          "RR": ([4, 128, T], BF16), "KK": ([4, 128, T], BF16), "VV": ([4, 128, T], BF16), "LR": ([1, 128, T], F32),
          "LV": ([1, 128, T], F32), "VF": ([T, 512], BF16)}


def even_pass1(C, L, S):
    E, nc, W = C.E, C.nc, C.W
    j = L // 2
    P = f"e{L}a_"
    w_in = W["ev_first_w_in"] if j == 0 else W["ev_rest_w_in"][j - 1]
    ncol = 4224 if j == 0 else 4256
    mu = C.sb(P + "mu", [128, 14], F32)
    E.dma(mu[:], C.HX["ev_mu_pm"][j], writes=[P + "mu"])
    wch = C.sb(P + "wch", [128, 2, 8, 128], BF16)
    zf = C.sb(P + "zf", [128, 2, T + 4], F32)
    o32 = C.sb(P + "o32", [128, 2, T], F32)
    o16 = C.sb(P + "o16", [128, 2, T], BF16)
    for s in range(2):
        E.op("pool", lambda e: e.memset(zf[:, s, 0:4], 0.0), writes=[(P + "zf", s)])
    xTkeys = [("xT", tt) for tt in range(NT)]
    kinds = ["HQ"] * 4 + ["HF"] * 4 + ["HI"] * 4 + ["GT"] * 8 + ["RR"] * 4 + ["KK"] * 4 + ["VV"] * 4 + ["LR", "LV"]
    base = {"HQ": 0, "HF": 4, "HI": 8, "GT": 12, "RR": 20, "KK": 24, "VV": 28, "LR": 32, "LV": 33}
    nchunks = (ncol + 127) // 128
    E.psum(P + "ps")
    with nc.psum_tensor(P + "ps", [128, 4, 512], F32) as ps:
        nb = 0
        for m in range(nchunks):
            s = m % 2
            cw = min(128, ncol - m * 128)
            kind = kinds[m]
            mi = m - base[kind]
            shift = m >= 20
            E.dma(wch[:, s, :, 0:cw], w_in[:, m * 128:m * 128 + cw].rearrange("(k p) c -> p k c", p=128), writes=[(P + "wch", s)], q="pool")
            for blk in range(4):
                b = nb % 4
                nb += 1
                for k in range(8):
                    E.op("pe", lambda e: e.matmul(ps[0:cw, b, :], lhsT=wch[:, s, k, 0:cw], rhs=C.xT[:, k, blk * 512:(blk + 1) * 512],
                                                  start=(k == 0), stop=(k == 7)),
                         reads=[(P + "wch", s)] + xTkeys[blk * 4:(blk + 1) * 4], writes=[(P + "ps", b)], inc=(k == 7))
                bs = slice(blk * 512, (blk + 1) * 512)
                if shift:
                    E.op("act", lambda e: e.copy(out=zf[0:cw, s, 4 + blk * 512:4 + (blk + 1) * 512], in_=ps[0:cw, b, :]),
                         reads=[(P + "ps", b)], writes=[(P + "zf", s)])
                elif kind in ("HQ", "HF"):
                    E.op("act", lambda e: e.copy(out=o32[0:cw, s, bs], in_=ps[0:cw, b, :]), reads=[(P + "ps", b)], writes=[(P + "o32", s)])
                elif kind == "HI":
                    E.op("act", lambda e: e.copy(out=o16[0:cw, s, bs], in_=ps[0:cw, b, :]), reads=[(P + "ps", b)], writes=[(P + "o16", s)])
                else:
                    E.op("act", lambda e: e.activation(out=o16[0:cw, s, bs], in_=ps[0:cw, b, :], func=AF.Silu),
                         reads=[(P + "ps", b)], writes=[(P + "o16", s)])
            if shift:
                mr = m - 20
                E.op("dve", lambda e: e.tensor_tensor(out=o32[0:cw, s, :], in0=zf[0:cw, s, 3:3 + T], in1=zf[0:cw, s, 4:4 + T], op=ALU.subtract),
                     reads=[(P + "zf", s)], writes=[(P + "o32", s)])
                dst = o16 if kind in ("RR", "KK", "VV") else o32
                E.op("dve", lambda e: e.scalar_tensor_tensor(out=dst[0:cw, s, :], in0=o32[0:cw, s, :], scalar=mu[0:cw, mr:mr + 1],
                                                             in1=zf[0:cw, s, 4:4 + T], op0=ALU.mult, op1=ALU.add),
                     reads=[(P + "o32", s), (P + "zf", s), P + "mu"], writes=[(P + ("o16" if dst is o16 else "o32"), s)])
                src, skey = (o16, "o16") if dst is o16 else (o32, "o32")
            else:
                src, skey = (o32, "o32") if kind in ("HQ", "HF") else (o16, "o16")
            E.dma(S[kind][mi, 0:cw, :], src[0:cw, s, :], reads=[(P + skey, s)], writes=[(kind, L, mi)])


def even_pass2(C, L, S, mixT_d):
    E, nc, W = C.E, C.nc, C.W
    j = L // 2
    P = f"e{L}b_"
    K = lambda n: P + n
    first = (j == 0)
    sb = C.sb
    def bc(name, src):
        t_ = sb(P + name, [128, 512], F32)
        E.dma(t_[:], src.partition_broadcast(128), writes=[K(name)])
        return t_
    gnorm = bc("gnorm", W["hg_gnorm"][j])
    kksc = bc("kksc", W["rw_kk"][j])
    kasc = bc("kasc", W["rw_ka"][j])
    rksc = bc("rksc", W["rw_rk"][j])
    gnw = bc("gnw", W["rw_gn_w"][j])
    gnb = bc("gnb", W["rw_gn_b"][j])
    waup = sb(P + "waup", [128, 512], F32)
    rows = sb(P + "rows", [1, 3, 512], F32)
    E.dma(waup[0:64, :], W["rw_w_up"][j], writes=[K("wup")])
    E.dma(waup[64:128, :], W["rw_a_up"][j], writes=[K("aup")])
    E.dma(rows[:, 0, :], W["rw_w0"][j:j + 1, :], writes=[K("rows")])
    E.dma(rows[:, 1, :], W["rw_a0"][j:j + 1, :], writes=[K("rows")])
    if not first:
        vup = sb(P + "vup", [32, 512], F32)
        E.dma(vup[:], W["rw_v_up"][j - 1], writes=[K("vup")])
        E.dma(rows[:, 2, :], W["rw_v0"][j - 1:j, :], writes=[K("rows")])
    ones1 = sb(P + "ones1", [1, 128], F32)
    E.op("pool", lambda e: e.memset(ones1[:], 1.0), writes=[K("ones1")])
    epsg = sb(P + "epsg", [128, 2], F32)
    E.op("pool", lambda e: e.memset(epsg[:, 0:1], RW_GN_EPS), writes=[K("epsg")])
    E.op("pool", lambda e: e.memset(epsg[:, 1:2], LN_EPS), writes=[K("epsg")])

    def sel_tile(name, shape, dtype, fillv, sels):
        t_ = sb(P + name, shape, dtype)
        E.op("pool", lambda e: e.memset(t_[:], fillv), writes=[K(name)])
        for (pat, cmp_, fv, base, cm) in sels:
            E.op("pool", lambda e: e.affine_select(out=t_[:], in_=t_[:], pattern=pat, compare_op=cmp_, fill=fv, base=base, channel_multiplier=cm),
                 reads=[K(name)], writes=[K(name)])
        return t_
    m_iu = sel_tile("m_iu", [128, 4, 128], BF16, 1.0, [([[0, 4], [1, 128]], ALU.is_ge, 0.0, 0, -1)])
    m_su = sel_tile("m_su", [128, 4, 128], BF16, 1.0, [([[0, 4], [1, 128]], ALU.is_ge, 0.0, -1, -1)])
    m_sl = sel_tile("m_sl", [128, 4, 128], BF16, 1.0, [([[0, 4], [-1, 128]], ALU.is_ge, 0.0, -1, 1)])
    id4 = sel_tile("id4", [128, 4, 128], BF16, 0.0, [([[0, 4], [-1, 128]], ALU.not_equal, 1.0, 0, 1)])
    Utri = sel_tile("Utri", [128, 128], F32, RW_KAPPA, [([[1, 128]], ALU.is_ge, 0.0, 0, -1)])
    Umid = sel_tile("Umid", [128, 128], F32, RW_KAPPA, [([[0, 128]], ALU.is_ge, 0.0, 63, -1)])
    Uall = sel_tile("Uall", [128, 128], F32, RW_KAPPA, [])
    Ucol = sb(P + "Ucol", [128, 2], F32)
    E.op("pool", lambda e: e.tensor_copy(out=Ucol[:, 0:1], in_=Uall[:, 0:1]), reads=[K("Uall")], writes=[K("Ucol")])
    E.op("pool", lambda e: e.tensor_copy(out=Ucol[:, 1:2], in_=Umid[:, 0:1]), reads=[K("Umid")], writes=[K("Ucol")])
    rst = sel_tile("rst", [128, 4, 128], F32, 1.0, [([[0, 4], [1, 128]], ALU.not_equal, 0.0, 0, 0), ([[0, 4], [1, 128]], ALU.not_equal, 0.0, -64, 0)])
    hlb = sb(P + "hlb", [128, 4, 4], F32)
    lbt = sb(P + "lbt", [128, 4, 4], F32)
    E.dma(hlb[:], C.HX["hg_lb_pm"], writes=[K("hlb")])
    E.op("act", lambda e: e.activation(out=hlb[:], in_=hlb[:], func=AF.Exp), reads=[K("hlb")], writes=[K("hlb")])
    E.op("dve", lambda e: e.reduce_sum(out=lbt[:, :, 2], in_=hlb[:], axis=AX.X), reads=[K("hlb")], writes=[K("lbt2")])
    E.op("dve", lambda e: e.reciprocal(out=lbt[:, :, 3], in_=lbt[:, :, 2]), reads=[K("lbt2")], writes=[K("lbt3")])
    if L == 0:
        E.op("dve", lambda e: e.memset(lbt[:, :, 0], 0.0), writes=[K("lbt0")])
    else:
        E.op("dve", lambda e: e.reduce_sum(out=lbt[:, :, 2], in_=hlb[:, :, 1:L + 1], axis=AX.X), reads=[K("hlb"), K("lbt3")], writes=[K("lbt2")])
        E.op("dve", lambda e: e.tensor_tensor(out=lbt[:, :, 0], in0=lbt[:, :, 2], in1=lbt[:, :, 3], op=ALU.mult),
             reads=[K("lbt2"), K("lbt3")], writes=[K("lbt0")])
    E.op("dve", lambda e: e.tensor_scalar(out=lbt[:, :, 1], in0=lbt[:, :, 0], scalar1=-1.0, scalar2=1.0, op0=ALU.mult, op1=ALU.add),
         reads=[K("lbt0")], writes=[K("lbt1")])
    Sh = sb(P + "Sh", [128, 4, 128], F32)
    Shb = sb(P + "Shb", [128, 4, 128], BF16)
    Sr = sb(P + "Sr", [64, 8, 64], F32)
    Srb = sb(P + "Srb", [64, 8, 64], BF16)
    E.op("pool", lambda e: e.memset(Sh[:], 0.0), writes=[K("Sh")])
    E.op("pool", lambda e: e.memset(Shb[:], 0.0), writes=[K("Shb")])
    E.op("pool", lambda e: e.memset(Sr[:], 0.0), writes=[K("Sr")])
    qead = sb(P + "qea", [128, 2, 4, 128], BF16)
    kzd = sb(P + "kz", [128, 2, 4, 2, 128], BF16)
    E.op("pool", lambda e: e.memset(kzd[:], 0.0), writes=[(P + "kz", 0), (P + "kz", 1)])
    eref = sb(P + "eref", [128, 8], F32)
    e01d = sb(P + "e01", [128, 2, 4], F32)
    hq_t = sb(P + "hq_t", [128, 4, 128], F32)
    hf_t = sb(P + "hf_t", [128, 4, 128], F32)
    f16 = {n: sb(P + n, [128, (8 if n == "gt_t" else 4), 128], BF16) for n in ["hi_t", "gt_t", "r_t", "k_t", "v_t"]}
    lr_t = sb(P + "lr_t", [128, 128], F32)
    lv_t = sb(P + "lv_t", [32, 128], F32)
    tokkv = sb(P + "tokkv", [128, 1024], BF16)
    F = {n: sb(P + "F_" + n, [128, 512], F32) for n in ["a", "b", "c", "d", "e", "f", "g", "h", "j"]}
    H = {n: sb(P + "H_" + n, [128, 512], F32) for n in ["b", "c", "d", "f"]}
    B1 = {n: sb(P + "B_" + n, [128, 512], BF16) for n in ["At", "Rt", "Bt", "Kt", "kd"]}
    FT1 = {n: sb(P + "T_" + n, [64, 8, 128], BF16) for n in ["BT", "KT"]}
    s81 = sb(P + "s81", [128, 8, 8], F32)
    th = sb(P + "th", [64, 128], F32)
    vf_t = sb(P + "vf_t", [128, 512], BF16)
    tokd = sb(P + "tokd", [128, 2, 2, 1024], BF16)
    B2 = {n: sb(P + "B_" + n, [128, 2, 512], BF16) for n in ["Vt", "Bh", "Kh", "qt"]}
    FT2 = {n: sb(P + "T_" + n, [64, 2, 8, 128], BF16) for n in ["AT", "RT"]}
    M2 = {n: sb(P + "M_" + n, [128, 2, 8, 128], BF16) for n in ["abT", "ak", "br", "kr"]}
    TQ0 = sb(P + "TQ0", [128, 2, 8, 2, 128], BF16)
    for pp in range(2):
        for g in range(2):
            E.op("pool", lambda e: e.tensor_copy(out=TQ0[:, pp, 4 * g:4 * g + 4, 1, :], in_=id4[:]), reads=[K("id4")], writes=[(P + "TQ0i", pp)])
    kdtokd = sb(P + "kdtok", [128, 2, 4, 128], BF16)
    elcd = sb(P + "elc", [64, 2, 8, 2], F32)
    ebld = sb(P + "ebl", [128, 2, 8], F32)
    bond = sb(P + "bon", [128, 2, 512], F32)
    TQ = [sb(P + f"TQ{i + 1}", [128, 8, 2, 128], BF16) for i in range(2)]
    TT = [sb(P + f"TT{i + 1}", [128, 8, 128], BF16) for i in range(2)]
    scm = sb(P + "scm", [128, 4, 128], BF16)
    S2 = {n: sb(P + "S2_" + n, [128, 512], F32) for n in ["a", "b", "c", "d"]}
    s82 = sb(P + "s82", [128, 8, 8], F32)
    xu = sb(P + "xu", [128, 2, 512], BF16)
    mixtok = sb(P + "mixtok", [128, 1024], BF16)
    mT = sb(P + "mT", [128, 8, 128], BF16)
    E.psum(P + "pa", P + "ptb", P + "pgm", P + "pseq", P + "phg")
    with (nc.psum_tensor(P + "pa", [128, 2, 512], F32) as pa,
          nc.psum_tensor(P + "ptb", [128, 1024], BF16) as ptb,
          nc.psum_tensor(P + "pgm", [128, 3, 512], F32) as pgm,
          nc.psum_tensor(P + "pseq", [128, 512], F32) as pseq,
          nc.psum_tensor(P + "phg", [128, 512], F32) as phg):
        ai = [0]

        def abank():
            b = ai[0] % 2
            ai[0] += 1
            return b

        def ew(eng, fn_name, out, reads, writes, **kw):
            E.op(eng, lambda e: getattr(e, fn_name)(out=out, **kw), reads=reads, writes=writes)

        v3 = lambda ap, a=8: ap.rearrange("p (a b) -> p a b", a=a)
        f4 = lambda ap: ap.rearrange("p (h t) -> p h t", h=4)
        hc = lambda ap, h: ap[:, h * 64:(h + 1) * 64]

        def body(tt):
            par = tt % 2
            KP = lambda n: (P + n, par)
            tsl = slice(tt * 128, (tt + 1) * 128)
            qea, kz, kdtok, elc, ebl, e01 = qead[:, par], kzd[:, par], kdtokd[:, par], elcd[:, par], ebld[:, par], e01d[:, par]
            bon = bond[:, par, :]
            Vt, Bh, Kh, qt = (B2[n][:, par, :] for n in ["Vt", "Bh", "Kh", "qt"])
            AT, RT = FT2["AT"][:, par], FT2["RT"][:, par]
            FTs = {"AT": AT, "RT": RT, "BT": FT1["BT"][:], "KT": FT1["KT"][:]}
            FTk = {"AT": KP("TAT"), "RT": KP("TRT"), "BT": K("TBT"), "KT": K("TKT")}
            ld = lambda dst, nm, key: E.dma(dst, S[nm][:, :, tsl].rearrange("m p t -> p m t"),
                                            reads=[(nm, L, m) for m in range(S[nm].shape[0])], writes=[K(key)])
            ld(hq_t[:], "HQ", "hq_t"); ld(hf_t[:], "HF", "hf_t")
            ld(f16["hi_t"][:], "HI", "hi_t"); ld(f16["gt_t"][:], "GT", "gt_t")
            ld(f16["r_t"][:], "RR", "r_t"); ld(f16["k_t"][:], "KK", "k_t"); ld(f16["v_t"][:], "VV", "v_t")
            E.dma(lr_t[:], S["LR"][0, :, tsl], reads=[("LR", L, 0)], writes=[K("lr_t")])
            if not first:
                E.dma(lv_t[:], S["LV"][0, 0:32, tsl], reads=[("LV", L, 0)], writes=[K("lv_t")])
                E.dma(vf_t[:], S["VF"][tsl, :], reads=[("VF", tt)], writes=[K("vf_t")])
            yield
            for (names, dst, dkey) in [(("hi_t", "r_t"), tokd[:, par, 0, :], KP("tok0")), (("gt_t",), tokd[:, par, 1, :], KP("tok1")),
                                       (("k_t", "v_t"), tokkv[:], K("tokkv"))]:
                ci = 0
                for nm in names:
                    for m in range(f16[nm].shape[1]):
                        E.op("pe", lambda e: e.transpose(ptb[:, ci * 128:(ci + 1) * 128], f16[nm][:, m, :], C.ident[:]),
                             reads=[K(nm), "ident"], writes=[K("ptb")], inc=(ci == 7))
                        ci += 1
                E.op("act", lambda e: e.copy(out=dst, in_=ptb[:]), reads=[K("ptb")], writes=[dkey])
                yield
            hi_k = r_k = KP("tok0"); gt_k = KP("tok1"); kv_k = K("tokkv")
            vh_tok = tokd[:, par, 0, 0:512]; r_tok = tokd[:, par, 0, 512:1024]; g_tok = tokd[:, par, 1, :]
            k_tok = tokkv[:, 0:512]; v_tok = tokkv[:, 512:1024]
            def hgrn1():
                sg, lf, bcm, tq = hf_t, H["b"], H["c"], H["d"]
                ew("act", "activation", hf_t[:], [K("hf_t")], [K("hf_t")], in_=hf_t[:], func=AF.Sigmoid)
                for h in range(4):
                    ew("dve", "tensor_scalar", sg[:][:, h, :], [K("hf_t"), K("lbt0"), K("lbt1")], [K("hf_t")], in0=sg[:][:, h, :],
                       scalar1=lbt[:, h, 1:2], scalar2=lbt[:, h, 0:1], op0=ALU.mult, op1=ALU.add)
                ew("act", "activation", lf[:], [K("hf_t")], [K("Hb")], in_=sg[:].rearrange("p h t -> p (h t)"), func=AF.Ln)
                yield
                ew("dve", "tensor_scalar", sg[:], [K("hf_t")], [K("hf_t")], in0=sg[:], scalar1=-1.0, scalar2=1.0, op0=ALU.mult, op1=ALU.add)
                E.op("dve", lambda e: e.tensor_tensor_scan(out=bcm[:], data0=rst[:].rearrange("p h t -> p (h t)"), data1=lf[:], initial=0.0,
                                                           op0=ALU.mult, op1=ALU.add), reads=[K("rst"), K("Hb")], writes=[K("Hc")])
                b8 = bcm[:].rearrange("p (c t) -> p c t", c=8)
                ew("dve", "tensor_tensor", v3(lf[:]), [K("Hc")], [K("Hb")], in0=b8, in1=b8[:, :, 31:32].broadcast_to([128, 8, 64]), op=ALU.subtract)
                ew("pool", "tensor_tensor", v3(tq[:]), [K("Hc")], [K("Hd")], in0=b8[:, :, 63:64].broadcast_to([128, 8, 64]), in1=b8, op=ALU.subtract)
                ew("act", "activation", ebl, [K("Hc")], [KP("ebl")], in_=b8[:, :, 63], func=AF.Exp)
                ew("act", "activation", eref[:], [K("Hc")], [K("eref")], in_=b8[:, :, 31], func=AF.Exp)
                yield
                eq, ek, eb = H["b"], H["f"], H["c"]
                ew("act", "activation", ek[:], [K("Hb")], [K("Hf")], in_=lf[:], func=AF.Exp, scale=-1.0)
                ew("act", "activation", eq[:], [K("Hb")], [K("Hb")], in_=lf[:], func=AF.Exp)
                ew("act", "activation", eb[:], [K("Hc")], [K("Hc")], in_=bcm[:], func=AF.Exp)
                ew("act", "activation", tq[:], [K("Hd")], [K("Hd")], in_=tq[:], func=AF.Exp)
                yield
                hqf = hq_t[:].rearrange("p h t -> p (h t)")
                ew("dve", "tensor_tensor", qt, [K("hq_t"), K("Hb")], [KP("Bqt")], in0=hqf, in1=eq[:], op=ALU.mult)
                ew("pool", "tensor_tensor", B1["kd"][:], [K("hf_t"), K("Hd")], [K("Bkd")], in0=sg[:].rearrange("p h t -> p (h t)"), in1=tq[:], op=ALU.mult)
                ebl2 = ebl.rearrange("p (h c) -> p h c", c=2)
                erf2 = eref[:].rearrange("p (h c) -> p h c", c=2)
                kd4 = f4(B1["kd"][:])
                ew("dve", "tensor_tensor", qea, [K("hq_t"), K("Hc")], [KP("qea")], in0=hq_t[:], in1=f4(eb[:]), op=ALU.mult)
                ew("pool", "tensor_tensor", qea[:, :, 64:128], [KP("qea"), KP("ebl")], [KP("qea")], in0=qea[:, :, 64:128],
                   in1=ebl2[:, :, 0:1].broadcast_to([128, 4, 64]), op=ALU.mult)
                for c in range(2):
                    cs = slice(c * 64, (c + 1) * 64)
                    ew("pool", "tensor_tensor", kz[:, :, c, cs], [K("hf_t"), K("Hf")], [KP("kz")], in0=sg[:][:, :, cs], in1=f4(ek[:])[:, :, cs], op=ALU.mult)
                ew("dve", "tensor_tensor", kz[:, :, 1, 0:64], [K("Bkd"), K("eref")], [KP("kz")], in0=kd4[:, :, 0:64],
                   in1=erf2[:, :, 1:2].broadcast_to([128, 4, 64]), op=ALU.mult)
                ew("pool", "tensor_tensor", kd4[:, :, 0:64], [K("Bkd"), KP("ebl")], [K("Bkd")], in0=kd4[:, :, 0:64],
                   in1=ebl2[:, :, 1:2].broadcast_to([128, 4, 64]), op=ALU.mult)
                ew("dve", "tensor_tensor", e01, [KP("ebl")], [KP("e01")], in0=ebl2[:, :, 0], in1=ebl2[:, :, 1], op=ALU.mult)
                yield
                for h in range(4):
                    E.op("pe", lambda e: e.transpose(ptb[:, h * 128:(h + 1) * 128], B1["kd"][:, h * 128:(h + 1) * 128], C.ident[:]),
                         reads=[K("Bkd"), "ident"], writes=[K("ptb")], inc=(h == 3))
                E.op("act", lambda e: e.copy(out=kdtok.rearrange("p h k -> p (h k)"), in_=ptb[:, 0:512]), reads=[K("ptb")], writes=[KP("kdtok")])
                yield

            def rwkv1():
                ew("act", "activation", th[:], [K("lr_t")], [K("th0")], in_=lr_t[0:64, :], func=AF.Tanh)
                sgw, av = F["a"], F["b"]
                b = abank()
                E.op("pe", lambda e: e.matmul(pa[:, b, :], lhsT=th[:], rhs=waup[0:64, :], start=True, stop=False), reads=[K("th0"), K("wup")], writes=[(P + "pa", b)], inc=False)
                E.op("pe", lambda e: e.matmul(pa[:, b, :], lhsT=ones1[:], rhs=rows[:, 0, :], start=False, stop=True), reads=[K("ones1"), K("rows")], writes=[(P + "pa", b)])
                ew("act", "activation", sgw[:], [(P + "pa", b)], [K("Fa")], in_=pa[:, b, :], func=AF.Sigmoid)
                b = abank()
                E.op("pe", lambda e: e.matmul(pa[:, b, :], lhsT=lr_t[64:128, :], rhs=waup[64:128, :], start=True, stop=False), reads=[K("lr_t"), K("aup")], writes=[(P + "pa", b)], inc=False)
                E.op("pe", lambda e: e.matmul(pa[:, b, :], lhsT=ones1[:], rhs=rows[:, 1, :], start=False, stop=True), reads=[K("ones1"), K("rows")], writes=[(P + "pa", b)])
                ew("act", "activation", av[:], [(P + "pa", b)], [K("Fb")], in_=pa[:, b, :], func=AF.Sigmoid)
                yield
                vv = F["c"]
                if first:
                    ew("pool", "tensor_copy", vv[:], [kv_k], [K("Fc")], in_=v_tok)
                    E.dma(S["VF"][tsl, :], v_tok, reads=[kv_k], writes=[("VF", tt)])
                else:
                    b = abank()
                    E.op("pe", lambda e: e.matmul(pa[:, b, :], lhsT=lv_t[:], rhs=vup[:], start=True, stop=False), reads=[K("lv_t"), K("vup")], writes=[(P + "pa", b)], inc=False)
                    E.op("pe", lambda e: e.matmul(pa[:, b, :], lhsT=ones1[:], rhs=rows[:, 2, :], start=False, stop=True), reads=[K("ones1"), K("rows")], writes=[(P + "pa", b)])
                    ew("act", "activation", F["d"][:], [(P + "pa", b)], [K("Fd")], in_=pa[:, b, :], func=AF.Sigmoid)
                    ew("dve", "tensor_tensor", vv[:], [K("vf_t"), kv_k], [K("Fc")], in0=vf_t[:], in1=v_tok, op=ALU.subtract)
                    ew("dve", "tensor_tensor", vv[:], [K("Fc"), K("Fd")], [K("Fc")], in0=vv[:], in1=F["d"][:], op=ALU.mult)
                    ew("dve", "tensor_tensor", vv[:], [K("Fc"), kv_k], [K("Fc")], in0=vv[:], in1=v_tok, op=ALU.add)
                ew("act", "copy", Vt, [K("Fc")], [KP("BVt")], in_=vv[:])
                yield
                d1, d4, cms = F["d"], F["e"], F["f"]
                b = abank()
                E.op("pe", lambda e: e.matmul(pa[:, b, :], lhsT=Umid[:], rhs=sgw[:], start=True, stop=True), reads=[K("Umid"), K("Fa")], writes=[(P + "pa", b)])
                ew("act", "copy", cms[:], [(P + "pa", b)], [K("Ff")], in_=pa[:, b, :])
                b = abank()
                E.op("pe", lambda e: e.matmul(pa[:, b, :], lhsT=Utri[:], rhs=sgw[:], start=True, stop=True), reads=[K("Utri"), K("Fa")], writes=[(P + "pa", b)])
                ew("dve", "tensor_tensor", d1[:], [(P + "pa", b), K("Ff")], [K("Fd")], in0=pa[:, b, :], in1=cms[:], op=ALU.subtract)
                yield
                b = abank()
                E.op("pe", lambda e: e.matmul(pa[:, b, :], lhsT=Uall[:], rhs=sgw[:], start=True, stop=True), reads=[K("Uall"), K("Fa")], writes=[(P + "pa", b)])
                ew("dve", "tensor_tensor", d4[:], [(P + "pa", b), K("Ff")], [K("Fe")], in0=pa[:, b, :], in1=cms[:], op=ALU.subtract)
                ew("pool", "tensor_tensor", d4[:], [K("Fe"), K("Fd")], [K("Fe")], in0=d4[:], in1=d1[:], op=ALU.subtract)
                d3 = F["f"]
                ew("dve", "scalar_tensor_tensor", d3[:], [K("Fa"), K("Fd")], [K("Ff")], in0=sgw[:], scalar=-RW_KAPPA, in1=d1[:], op0=ALU.mult, op1=ALU.add)
                b = abank()
                for h in range(8):
                    E.op("pe", lambda e: e.matmul(pa[0:64, b, 2 * h:2 * h + 2], lhsT=sgw[:, h * 64:(h + 1) * 64], rhs=Ucol[:], start=True, stop=True),
                         reads=[K("Fa"), K("Ucol")], writes=[(P + "pa", b)], inc=(h == 7))
                ew("act", "activation", elc.rearrange("p h c -> p (h c)"), [(P + "pa", b)], [KP("elc")], in_=pa[0:64, b, 0:16], func=AF.Exp)
                yield
                e1, e2, e3, e4 = F["g"], F["h"], F["f"], F["e"]
                ew("act", "activation", e1[:], [K("Fd")], [K("Fg")], in_=d1[:], func=AF.Exp)
                ew("act", "activation", e2[:], [K("Fd")], [K("Fh")], in_=d1[:], func=AF.Exp, scale=-1.0)
                ew("act", "activation", e3[:], [K("Ff")], [K("Ff")], in_=d3[:], func=AF.Exp)
                ew("act", "activation", e4[:], [K("Fe")], [K("Fe")], in_=d4[:], func=AF.Exp)
                yield
                kkn, kp, tmp = F["d"], F["a"], F["j"]
                ew("dve", "tensor_tensor", kkn[:], [kv_k, K("kksc")], [K("Fd")], in0=k_tok, in1=kksc[:], op=ALU.mult)
                ew("pool", "tensor_tensor", tmp[:], [K("Fd")], [K("Fj")], in0=kkn[:], in1=kkn[:], op=ALU.mult)
                ew("dve", "reduce_sum", s81[:, :, 3], [K("Fj")], [K("s8d")], in_=v3(tmp[:]), axis=AX.X)
                ew("dve", "tensor_scalar", s81[:, :, 4], [K("s8d")], [K("s8e")], in0=s81[:, :, 3], scalar1=1e-24, scalar2=None, op0=ALU.max)
                ew("act", "activation", s81[:, :, 4], [K("s8e")], [K("s8e")], in_=s81[:, :, 4], func=AF.Ln)
                ew("act", "activation", s81[:, :, 5], [K("s8e")], [K("s8f")], in_=s81[:, :, 4], func=AF.Exp, scale=-0.5)
                yield
                ew("dve", "tensor_tensor", v3(kkn[:]), [K("Fd"), K("s8f")], [K("Fd")], in0=v3(kkn[:]), in1=s81[:, :, 5:6].broadcast_to([128, 8, 64]), op=ALU.mult)
                ew("dve", "scalar_tensor_tensor", kp[:], [K("Fb"), K("kasc")], [K("Fa")], in0=av[:], scalar=-1.0, in1=kasc[:], op0=ALU.add, op1=ALU.mult)
                ew("dve", "scalar_tensor_tensor", kp[:], [K("Fa"), kv_k], [K("Fa")], in0=kp[:], scalar=1.0, in1=k_tok, op0=ALU.add, op1=ALU.mult)
                kka = F["b"]
                ew("dve", "tensor_tensor", kka[:], [K("Fd"), K("Fb")], [K("Fb")], in0=kkn[:], in1=av[:], op=ALU.mult)
                yield
                ew("dve", "tensor_tensor", tmp[:], [r_k, K("Fa")], [K("Fj")], in0=r_tok, in1=kp[:], op=ALU.mult)
                ew("pool", "tensor_tensor", tmp[:], [K("Fj"), K("rksc")], [K("Fj")], in0=tmp[:], in1=rksc[:], op=ALU.mult)
                ew("dve", "reduce_sum", s81[:, :, 6], [K("Fj")], [K("s8g")], in_=v3(tmp[:]), axis=AX.X)
                ew("dve", "tensor_tensor", v3(bon), [K("Fc"), K("s8g")], [KP("bon")], in0=v3(vv[:]), in1=s81[:, :, 6:7].broadcast_to([128, 8, 64]), op=ALU.mult)
                yield
                ew("dve", "scalar_tensor_tensor", B1["At"][:], [K("Fd"), K("Ff")], [K("BAt")], in0=kkn[:], scalar=-1.0, in1=e3[:], op0=ALU.mult, op1=ALU.mult)
                ew("pool", "tensor_tensor", B1["Rt"][:], [r_k, K("Fg")], [K("BRt")], in0=r_tok, in1=e1[:], op=ALU.mult)
                ew("dve", "tensor_tensor", B1["Bt"][:], [K("Fb"), K("Fh")], [K("BBt")], in0=kka[:], in1=e2[:], op=ALU.mult)
                ew("pool", "tensor_tensor", B1["Kt"][:], [K("Fa"), K("Fh")], [K("BKt")], in0=kp[:], in1=e2[:], op=ALU.mult)
                yield
                ew("dve", "tensor_tensor", Bh, [K("Fb"), K("Fe")], [KP("BBh")], in0=kka[:], in1=e4[:], op=ALU.mult)
                ew("pool", "tensor_tensor", Kh, [K("Fa"), K("Fe")], [KP("BKh")], in0=kp[:], in1=e4[:], op=ALU.mult)
                for nm, src in (("AT", "At"), ("BT", "Bt"), ("KT", "Kt"), ("RT", "Rt")):
                    for h in range(8):
                        E.op("pe", lambda e: e.transpose(ptb[0:64, h * 128:(h + 1) * 128], B1[src][:, h * 64:(h + 1) * 64], C.ident[:]),
                             reads=[K("B" + src), "ident"], writes=[K("ptb")], inc=(h == 7))
                    E.op("act", lambda e: e.copy(out=FTs[nm].rearrange("p h t -> p (h t)"), in_=ptb[0:64, :]), reads=[K("ptb")], writes=[FTk[nm]])
                    yield
                def gram(dst, lh, rh, mask):
                    for g in range(2):
                        for hh in range(4):
                            h = 4 * g + hh
                            E.op("pe", lambda e: e.matmul(pgm[:, g, hh * 128:(hh + 1) * 128], lhsT=FTs[lh][:, h, :], rhs=FTs[rh][:, h, :], start=True, stop=True),
                                 reads=[FTk[lh], FTk[rh]], writes=[(P + "pgm", g)], inc=(hh == 3))
                        dsta = TQ0[:, par, 4 * g:4 * g + 4, 0, :] if dst == "ab" else M2[dst][:, par, 4 * g:4 * g + 4, :]
                        ew("dve", "tensor_tensor", dsta, [(P + "pgm", g), K(mask[1])], [(P + "M" + dst, g, par)],
                           in0=pgm[:, g, :].rearrange("p (h t) -> p h t", h=4), in1=mask[0][:], op=ALU.mult)
                gram("ab", "BT", "AT", (m_su, "m_su"))
                yield
                gram("abT", "AT", "BT", (m_sl, "m_sl"))
                yield
                gram("ak", "KT", "AT", (m_su, "m_su"))
                yield
                gram("br", "BT", "RT", (m_iu, "m_iu"))
                yield
                gram("kr", "KT", "RT", (m_iu, "m_iu"))
                yield

            g1, g2 = rwkv1(), hgrn1()
            d1_, d2_ = False, False
            while not (d1_ and d2_):
                if not d1_:
                    try:
                        next(g1)
                    except StopIteration:
                        d1_ = True
                if not d2_:
                    try:
                        next(g2)
                    except StopIteration:
                        d2_ = True
                yield
            yield "STAGE"

            def hgrn2():
                b = abank()
                for h in range(4):
                    for c in range(2):
                        E.op("pe", lambda e: e.matmul(pa[:, b, h * 128 + c * 64:h * 128 + (c + 1) * 64], lhsT=kz[:, h, c, :], rhs=qt[:, h * 128 + c * 64:h * 128 + (c + 1) * 64],
                                                      start=True, stop=True), reads=[KP("kz"), KP("Bqt")], writes=[(P + "pa", b)], inc=(h == 3 and c == 1))
                ew("dve", "tensor_tensor", scm[:].rearrange("p h t -> p (h t)"), [(P + "pa", b), K("m_iu")], [K("scm")],
                   in0=pa[:, b, :], in1=m_iu[:].rearrange("p h t -> p (h t)"), op=ALU.mult)
                yield
                for h in range(4):
                    E.op("pe", lambda e: e.matmul(phg[:, h * 128:(h + 1) * 128], lhsT=scm[:, h, :], rhs=vh_tok[:, h * 128:(h + 1) * 128], start=True, stop=False),
                         reads=[K("scm"), hi_k], writes=[K("phg")], inc=False)
                    E.op("pe", lambda e: e.matmul(phg[:, h * 128:(h + 1) * 128], lhsT=qea[:, h, :], rhs=Shb[:, h, :], start=False, stop=True),
                         reads=[KP("qea"), K("Shb")], writes=[K("phg")], inc=(h == 3))
                yield
                b2 = abank()
                for h in range(4):
                    E.op("pe", lambda e: e.matmul(pa[:, b2, h * 128:(h + 1) * 128], lhsT=kdtok[:, h, :], rhs=vh_tok[:, h * 128:(h + 1) * 128], start=True, stop=True),
                         reads=[KP("kdtok"), hi_k], writes=[(P + "pa", b2)], inc=(h == 3))
                ew("dve", "tensor_tensor", Sh[:], [K("Sh"), KP("e01")], [K("Sh")], in0=Sh[:], in1=e01.unsqueeze(2).broadcast_to([128, 4, 128]), op=ALU.mult)
                ew("dve", "tensor_tensor", Sh[:].rearrange("p h v -> p (h v)"), [K("Sh"), (P + "pa", b2)], [K("Sh")], in0=Sh[:].rearrange("p h v -> p (h v)"),
                   in1=pa[:, b2, :], op=ALU.add)
                ew("act", "copy", Shb[:], [K("Sh")], [K("Shb")], in_=Sh[:])
                yield
                osq, hgn = S2["a"], S2["b"]
                ew("act", "activation", osq[:], [K("phg")], [K("S2a")], in_=phg[:], func=AF.Square)
                ew("dve", "reduce_sum", s82[:, 0:4, 0], [K("S2a")], [K("s2a")], in_=f4(osq[:]), axis=AX.X)
                ew("act", "activation", s82[:, 0:4, 1], [K("s2a"), K("epsg")], [K("s2b")], in_=s82[:, 0:4, 0], func=AF.Ln, bias=epsg[:, 1:2], scale=1.0 / 128.0)
                ew("act", "activation", s82[:, 0:4, 2], [K("s2b")], [K("s2c")], in_=s82[:, 0:4, 1], func=AF.Exp, scale=-0.5)
                yield
                ew("dve", "tensor_tensor", f4(hgn[:]), [K("phg"), K("s2c")], [K("S2b")], in0=f4(phg[:]), in1=s82[:, 0:4, 2:3].broadcast_to([128, 4, 128]), op=ALU.mult)
                ew("pool", "tensor_tensor", hgn[:], [K("S2b"), K("gnorm")], [K("S2b")], in0=hgn[:], in1=gnorm[:], op=ALU.mult)
                ew("pool", "tensor_tensor", mixtok[:, 0:512], [K("S2b"), gt_k], [K("mix0")], in0=hgn[:], in1=g_tok[:, 0:512], op=ALU.mult)
                yield

            def rwkv2():
                Mv = {n: M2[n][:, par] for n in M2}
                Mk = lambda n, g: (P + "M" + n, g, par)
                tq_src, tq_key = TQ0[:, par], lambda g: [Mk("ab", g), (P + "TQ0i", par)]
                tt_src, tt_key = Mv["abT"], lambda g: [Mk("abT", g)]
                for kstep in range(1, 8):
                    dq, dt = TQ[kstep % 2], TT[kstep % 2]
                    dqk = lambda g: (P + f"TQ{kstep % 2 + 1}", g)
                    dtk = lambda g: (P + f"TT{kstep % 2 + 1}", g)
                    wide = kstep <= 5
                    for g in range(2):
                        gs = slice(4 * g, 4 * g + 4)
                        for hh in range(4):
                            h = 4 * g + hh
                            if wide:
                                E.op("pe", lambda e: e.matmul(pgm[:, hh // 2, (hh % 2) * 256:(hh % 2 + 1) * 256], lhsT=tt_src[:, h, :],
                                                              rhs=tq_src[:, h].rearrange("p x t -> p (x t)"), start=True, stop=True),
                                     reads=tt_key(g) + tq_key(g), writes=[(P + "pgm", 0), (P + "pgm", 1)], inc=(hh == 3))
                            else:
                                E.op("pe", lambda e: e.matmul(pgm[:, 0, hh * 128:(hh + 1) * 128], lhsT=tt_src[:, h, :], rhs=tq_src[:, h, 1, :], start=True, stop=True),
                                     reads=tt_key(g) + tq_key(g), writes=[(P + "pgm", 0)], inc=(hh == 3))
                        if wide:
                            pv = pgm[:, 0:2, :].rearrange("p b (h x t) -> p (b h) x t", h=2, x=2)
                            ew("act", "copy", dq[:, gs, 0, :], [(P + "pgm", 0), (P + "pgm", 1)], [dqk(g)], in_=pv[:, :, 0, :])
                            ew("dve", "tensor_tensor", dq[:, gs, 1, :], [(P + "pgm", 0), (P + "pgm", 1)] + tq_key(g), [dqk(g)],
                               in0=pv[:, :, 1, :], in1=tq_src[:, gs, 1, :], op=ALU.add)
                        else:
                            ew("dve", "tensor_tensor", dq[:, gs, 1, :], [(P + "pgm", 0)] + tq_key(g), [dqk(g)],
                               in0=pgm[:, 0, :].rearrange("p (h t) -> p h t", h=4), in1=tq_src[:, gs, 1, :], op=ALU.add)
                        yield
                        if kstep <= 6:
                            for hh in range(4):
                                h = 4 * g + hh
                                E.op("pe", lambda e: e.matmul(pgm[:, 2, hh * 128:(hh + 1) * 128], lhsT=tq_src[:, h, 0, :], rhs=tt_src[:, h, :], start=True, stop=True),
                                     reads=tt_key(g) + tq_key(g), writes=[(P + "pgm", 2)], inc=(hh == 3))
                            ew("act", "copy", dt[:, gs, :], [(P + "pgm", 2)], [dtk(g)], in_=pgm[:, 2, :].rearrange("p (h t) -> p h t", h=4))
                            yield
                    tq_src, tt_src = dq, dt
                    tq_key = (lambda kq: (lambda g: [(P + f"TQ{kq % 2 + 1}", g)]))(kstep)
                    tt_key = (lambda kq: (lambda g: [(P + f"TT{kq % 2 + 1}", g)]))(kstep)
                Pfin = tq_src
                Pk = tq_key(0) + tq_key(1)
                ew("dve", "tensor_tensor", Srb[:], [K("Sr"), KP("elc")], [K("Srb")], in0=Sr[:], in1=elc[:, :, 1:2].broadcast_to([64, 8, 64]), op=ALU.mult)
                for h in range(8):
                    E.op("pe", lambda e: e.matmul(hc(pseq, h), lhsT=AT[:, h, :], rhs=Srb[:, h, :], start=True, stop=False),
                         reads=[KP("TAT"), K("Srb")], writes=[K("pseq")], inc=False)
                    E.op("pe", lambda e: e.matmul(hc(pseq, h), lhsT=Mv["ak"][:, h, :], rhs=hc(Vt, h), start=False, stop=True),
                         reads=[Mk("ak", h // 4), KP("BVt")], writes=[K("pseq")], inc=(h == 7))
                ew("act", "copy", xu[:, 0, :], [K("pseq")], [K("xu0")], in_=pseq[:])
                yield
                for h in range(8):
                    E.op("pe", lambda e: e.matmul(hc(pseq, h), lhsT=Pfin[:, h, 1, :], rhs=hc(xu[:, 0, :], h), start=True, stop=True),
                         reads=Pk + [K("xu0")], writes=[K("pseq")], inc=(h == 7))
                ew("act", "copy", xu[:, 1, :], [K("pseq")], [K("xu1")], in_=pseq[:])
                yield
                for h in range(8):
                    E.op("pe", lambda e: e.matmul(hc(pseq, h), lhsT=RT[:, h, :], rhs=Srb[:, h, :], start=True, stop=False),
                         reads=[KP("TRT"), K("Srb")], writes=[K("pseq")], inc=False)
                    E.op("pe", lambda e: e.matmul(hc(pseq, h), lhsT=Mv["br"][:, h, :], rhs=hc(xu[:, 1, :], h), start=False, stop=False),
                         reads=[Mk("br", h // 4), K("xu1")], writes=[K("pseq")], inc=False)
                    E.op("pe", lambda e: e.matmul(hc(pseq, h), lhsT=Mv["kr"][:, h, :], rhs=hc(Vt, h), start=False, stop=True),
                         reads=[Mk("kr", h // 4), KP("BVt")], writes=[K("pseq")], inc=(h == 7))
                yv = S2["c"]
                ew("act", "copy", yv[:], [K("pseq")], [K("S2c")], in_=pseq[:])
                yield
                for h in range(8):
                    E.op("pe", lambda e: e.matmul(pseq[0:64, h * 64:(h + 1) * 64], lhsT=hc(Bh, h), rhs=hc(xu[:, 1, :], h), start=True, stop=False),
                         reads=[KP("BBh"), K("xu1")], writes=[K("pseq")], inc=False)
                    E.op("pe", lambda e: e.matmul(pseq[0:64, h * 64:(h + 1) * 64], lhsT=hc(Kh, h), rhs=hc(Vt, h), start=False, stop=True),
                         reads=[KP("BKh"), KP("BVt")], writes=[K("pseq")], inc=(h == 7))
                ew("dve", "tensor_tensor", Sr[:], [K("Sr"), KP("elc")], [K("Sr")], in0=Sr[:], in1=elc[:, :, 0:1].broadcast_to([64, 8, 64]), op=ALU.mult)
                ew("dve", "tensor_tensor", Sr[:].rearrange("p h i -> p (h i)"), [K("Sr"), K("pseq")], [K("Sr")], in0=Sr[:].rearrange("p h i -> p (h i)"),
                   in1=pseq[0:64, :], op=ALU.add)
                yield
                yc = S2["d"]
                ew("dve", "reduce_sum", s82[:, :, 3], [K("S2c")], [K("s2d")], in_=v3(yv[:]), axis=AX.X)
                ew("dve", "scalar_tensor_tensor", v3(yc[:]), [K("s2d"), K("S2c")], [K("S2d")], in0=s82[:, :, 3:4].broadcast_to([128, 8, 64]), scalar=-1.0 / 64.0,
                   in1=v3(yv[:]), op0=ALU.mult, op1=ALU.add)
                ew("pool", "tensor_tensor", yv[:], [K("S2d")], [K("S2c")], in0=yc[:], in1=yc[:], op=ALU.mult)
                ew("dve", "reduce_sum", s82[:, :, 4], [K("S2c")], [K("s2e")], in_=v3(yv[:]), axis=AX.X)
                yield
                ew("act", "activation", s82[:, :, 5], [K("s2e"), K("epsg")], [K("s2f")], in_=s82[:, :, 4], func=AF.Ln, bias=epsg[:, 0:1], scale=1.0 / 64.0)
                ew("act", "activation", s82[:, :, 6], [K("s2f")], [K("s2g")], in_=s82[:, :, 5], func=AF.Exp, scale=-0.5)
                ew("dve", "tensor_tensor", v3(yc[:]), [K("S2d"), K("s2g")], [K("S2d")], in0=v3(yc[:]), in1=s82[:, :, 6:7].broadcast_to([128, 8, 64]), op=ALU.mult)
                ew("pool", "tensor_tensor", yc[:], [K("S2d"), K("gnw")], [K("S2d")], in0=yc[:], in1=gnw[:], op=ALU.mult)
                yield
                ew("pool", "tensor_tensor", yc[:], [K("S2d"), K("gnb")], [K("S2d")], in0=yc[:], in1=gnb[:], op=ALU.add)
                ew("dve", "tensor_tensor", yc[:], [K("S2d"), KP("bon")], [K("S2d")], in0=yc[:], in1=bon, op=ALU.add)
                ew("dve", "tensor_tensor", mixtok[:, 512:1024], [K("S2d"), gt_k], [K("mix1")], in0=yc[:], in1=g_tok[:, 512:1024], op=ALU.mult)
                yield

            ga, gb = hgrn2(), rwkv2()
            da = db = False
            while not (da and db):
                if not db:
                    try:
                        next(gb)
                    except StopIteration:
                        db = True
                if not da:
                    try:
                        next(ga)
                    except StopIteration:
                        da = True
                yield
            for m in range(8):
                E.op("pe", lambda e: e.transpose(ptb[:, m * 128:(m + 1) * 128], mixtok[:, m * 128:(m + 1) * 128], C.ident[:]),
                     reads=[K("mix0"), K("mix1"), "ident"], writes=[K("ptb")], inc=(m == 7))
            E.op("act", lambda e: e.copy(out=mT[:].rearrange("p m t -> p (m t)"), in_=ptb[:]), reads=[K("ptb")], writes=[K("mT")])
            E.dma(mixT_d[tt], mT[:], reads=[K("mT")], writes=[("mixT", L, tt)])

        run_pipeline([body(tt) for tt in range(C.ntiles)])


def even_layer_phases(C, L, S, mixe, phase, stages):
    if stages is None or "p1" in stages:
        phase(even_pass1, C, L, S)
    if stages is None or "p2" in stages:
        phase(even_pass2, C, L, S, mixe)


_NC_CACHE = {}


def kernel(**inputs):
    shared, per_core = host_prep(inputs)
    if "nc" not in _NC_CACHE:
        _NC_CACHE["nc"] = build_program()
    nc = _NC_CACHE["nc"]
    n = len(per_core)
    in_maps = [{**shared, **pc} for pc in per_core]
    res = run_bass_kernel_spmd(nc, in_maps, core_ids=list(range(n)))
    return np.stack([np.asarray(r["out"], dtype=np.float32) for r in res.results], axis=0)
```
